# Optimizing a Trainium2 kernel written in Bass

```python
import math
import jax, jax.numpy as jnp
from jax import lax
import numpy as np

D_MODEL = 1024
BATCH = 4
SEQ = 8192
DEPTH = 2

GRID_W = 64
HEAD_DIM = 64
NA_HEADS = 4
NA_WIN_ROWS = 8
NA_WIN_COLS = 16
SW_HEADS = 8
SW_KV_HEADS = 2
SW_WINDOW = 128
SW_BLOCK = 128
DIFF_HEADS = 4
DIFF_QK_DIM = HEAD_DIM // 2
DIFF_V_DIM = HEAD_DIM
DIFF_BLOCK = 128
D_FF = 4 * D_MODEL
N_BUCKETS = 32
MAX_DISTANCE = 128
LN_EPS = 1e-5
NEG = -1e30

A_W = NA_HEADS * HEAD_DIM
B_Q_W = SW_HEADS * HEAD_DIM
B_KV_W = SW_KV_HEADS * HEAD_DIM
C_QK_W = DIFF_HEADS * 2 * DIFF_QK_DIM
C_V_W = DIFF_HEADS * DIFF_V_DIM
IN_SPLITS = (A_W, A_W, A_W, B_Q_W, B_KV_W, B_KV_W, C_QK_W, C_QK_W, C_V_W)
IS_VALUE = (0, 0, 1, 0, 0, 1, 0, 0, 1)
IN_WIDTH = A_W * 3 + B_Q_W + 2 * B_KV_W + 2 * C_QK_W + C_V_W
MIX_WIDTH = A_W + B_Q_W + C_V_W

kernel_name = "hybrid_natten_swa_diffattn_encoder"


def layer_norm(x, g, b):
    xf = x.astype(jnp.float32)
    mu = jnp.mean(xf, axis=-1, keepdims=True)
    xc = xf - mu
    var = jnp.mean(xc * xc, axis=-1, keepdims=True)
    y = xc * lax.rsqrt(var + LN_EPS) * g.astype(jnp.float32) + b.astype(jnp.float32)
    return y.astype(x.dtype)


def t5_bucket(rel):
    nb = N_BUCKETS // 2
    max_exact = nb // 2
    n = jnp.abs(rel)
    nf = jnp.maximum(n, 1).astype(jnp.float32)
    large = max_exact + (jnp.log(nf / max_exact) / math.log(MAX_DISTANCE / max_exact)
                         * (nb - max_exact)).astype(jnp.int32)
    large = jnp.minimum(large, nb - 1)
    return jnp.where(rel > 0, nb, 0) + jnp.where(n < max_exact, n, large)


def neighbourhood_attention(q, k, v, rpb):
    b, s, h, d = q.shape
    rows = s // GRID_W
    wr = min(NA_WIN_ROWS, rows)
    qg = q.reshape(b, rows, GRID_W, h, d)
    kg = k.reshape(b, rows, GRID_W, h, d)
    vg = v.reshape(b, rows, GRID_W, h, d)
    r = jnp.arange(rows)
    row_start = jnp.clip(r - wr // 2, 0, rows - wr)
    row_idx = row_start[:, None] + jnp.arange(wr)[None, :]
    k_rows = kg[:, row_idx]
    v_rows = vg[:, row_idx]
    c = jnp.arange(GRID_W)
    col_start = jnp.clip(c - NA_WIN_COLS // 2, 0, GRID_W - NA_WIN_COLS)
    col_off = c[None, :] - col_start[:, None]
    col_mask = (col_off >= 0) & (col_off < NA_WIN_COLS)
    dr = row_idx - r[:, None]
    dc = jnp.clip(c[None, :] - c[:, None], -(NA_WIN_COLS - 1), NA_WIN_COLS - 1)
    bias = rpb[:, dr + NA_WIN_ROWS - 1][..., dc + NA_WIN_COLS - 1]
    bias = bias.transpose(0, 1, 3, 2, 4).astype(jnp.float32)
    scores = jnp.einsum('brqhd,brwkhd->bhrqwk', qg, k_rows).astype(jnp.float32) * (d ** -0.5)
    scores = jnp.where(col_mask[:, None, :], scores + bias[None], NEG)
    sh = scores.shape
    p = jax.nn.softmax(scores.reshape(sh[:-2] + (sh[-2] * sh[-1],)), axis=-1).reshape(sh)
    out = jnp.einsum('bhrqwk,brwkhd->brqhd', p.astype(v.dtype), v_rows)
    return out.reshape(b, s, h, d)


def sliding_window_gqa(q, k, v, sink, bias_table):
    b, s, hq, d = q.shape
    hkv = k.shape[2]
    g = hq // hkv
    nblk = s // SW_BLOCK
    pad = ((0, 0), (SW_BLOCK, SW_BLOCK), (0, 0), (0, 0))
    kp = jnp.pad(k, pad).reshape(b, nblk + 2, SW_BLOCK, hkv, d)
    vp = jnp.pad(v, pad).reshape(b, nblk + 2, SW_BLOCK, hkv, d)
    k_band = jnp.concatenate([kp[:, :-2], kp[:, 1:-1], kp[:, 2:]], axis=2)
    v_band = jnp.concatenate([vp[:, :-2], vp[:, 1:-1], vp[:, 2:]], axis=2)
    qb = q.reshape(b, nblk, SW_BLOCK, hkv, g, d)
    scores = jnp.einsum('bnqhgd,bnkhd->bnhgqk', qb, k_band).astype(jnp.float32) * (d ** -0.5)
    qi = jnp.arange(SW_BLOCK)
    kj = jnp.arange(3 * SW_BLOCK) - SW_BLOCK
    rel = kj[None, :] - qi[:, None]
    bias = bias_table[t5_bucket(rel)].astype(jnp.float32)
    bias = bias.transpose(2, 0, 1).reshape(hkv, g, SW_BLOCK, 3 * SW_BLOCK)
    kpos = jnp.arange(nblk)[:, None] * SW_BLOCK + kj[None, :]
    valid = ((jnp.abs(rel) <= SW_WINDOW)[None]
             & ((kpos >= 0) & (kpos < s))[:, None, :])
    scores = jnp.where(valid[None, :, None, None], scores + bias, NEG)
    sink_col = jnp.broadcast_to(sink.astype(jnp.float32).reshape(hkv, g, 1, 1),
                                scores.shape[:-1] + (1,))
    p = jax.nn.softmax(jnp.concatenate([scores, sink_col], axis=-1), axis=-1)[..., :-1]
    out = jnp.einsum('bnhgqk,bnkhd->bnqhgd', p.astype(v.dtype), v_band)
    return out.reshape(b, s, hq, d)


def differential_attention(q, k, v, lam_q, lam_k, subln_g, bias_table, lam_init):
    b, s, h, _, dqk = q.shape
    nblk = s // DIFF_BLOCK
    scale = dqk ** -0.5
    lqf = lam_q.astype(jnp.float32)
    lkf = lam_k.astype(jnp.float32)
    lam = (jnp.exp(jnp.sum(lqf[0] * lkf[0])) - jnp.exp(jnp.sum(lqf[1] * lkf[1]))
           + lam_init)
    kpos = jnp.arange(s)
    qb = jnp.moveaxis(q.reshape(b, nblk, DIFF_BLOCK, h, 2, dqk), 1, 0)

    def block(args):
        q_blk, i = args
        qpos = i * DIFF_BLOCK + jnp.arange(DIFF_BLOCK)
        rel = kpos[None, :] - qpos[:, None]
        bias = jnp.moveaxis(bias_table[t5_bucket(rel)], -1, 0).astype(jnp.float32)
        scores = jnp.einsum('bqhmd,bkhmd->bhmqk', q_blk, k).astype(jnp.float32) * scale
        p = jax.nn.softmax(scores + bias[None, :, None], axis=-1)
        attn = p[:, :, 0] - lam * p[:, :, 1]
        return jnp.einsum('bhqk,bkhd->bqhd', attn.astype(v.dtype), v)

    out = lax.map(block, (qb, jnp.arange(nblk)))
    out = jnp.moveaxis(out, 0, 1).reshape(b, s, h, v.shape[-1])
    of = out.astype(jnp.float32)
    of = of * lax.rsqrt(jnp.mean(of * of, axis=-1, keepdims=True) + LN_EPS)
    of = of * subln_g.astype(jnp.float32) * (1.0 - lam_init)
    return of.astype(v.dtype)


def setup_inputs(seed: int = 0) -> dict:
    key = jax.random.key(seed)
    ks = jax.random.split(key, 20)
    beta = (8 * DEPTH) ** -0.25
    nrm = jax.random.normal
    col_scale = jnp.concatenate([
        jnp.full((w,), beta if isv else 1.0, jnp.float32) for w, isv in zip(IN_SPLITS, IS_VALUE)])
    x = nrm(ks[0], (BATCH, SEQ, D_MODEL), jnp.float32)
    ln_in_g = 1.0 + 0.02 * nrm(ks[1], (D_MODEL,), jnp.float32)
    ln_in_b = 0.02 * nrm(ks[2], (D_MODEL,), jnp.float32)
    t5_table = 0.2 * nrm(ks[3], (N_BUCKETS, SW_HEADS + DIFF_HEADS), jnp.float32)
    w_in = nrm(ks[4], (DEPTH, D_MODEL, IN_WIDTH), jnp.float32) * (D_MODEL ** -0.5) * col_scale
    w_out = nrm(ks[5], (DEPTH, MIX_WIDTH, D_MODEL), jnp.float32) * (MIX_WIDTH ** -0.5) * beta
    na_rpb = 0.2 * nrm(ks[6], (DEPTH, NA_HEADS, 2 * NA_WIN_ROWS - 1, 2 * NA_WIN_COLS - 1), jnp.float32)
    sw_sink = 0.5 * nrm(ks[7], (DEPTH, SW_HEADS), jnp.float32)
    diff_lam_q = 0.1 * nrm(ks[8], (DEPTH, 2, DIFF_QK_DIM), jnp.float32)
    diff_lam_k = 0.1 * nrm(ks[9], (DEPTH, 2, DIFF_QK_DIM), jnp.float32)
    diff_subln_g = 1.0 + 0.02 * nrm(ks[10], (DEPTH, DIFF_V_DIM), jnp.float32)
    ln_mix_g = 1.0 + 0.02 * nrm(ks[11], (DEPTH, D_MODEL), jnp.float32)
    ln_mix_b = 0.02 * nrm(ks[12], (DEPTH, D_MODEL), jnp.float32)
    w_ff1 = nrm(ks[13], (DEPTH, D_MODEL, D_FF), jnp.float32) * (D_MODEL ** -0.5)
    w_ff2 = nrm(ks[14], (DEPTH, D_FF, D_MODEL), jnp.float32) * (D_FF ** -0.5) * beta
    ln_ff_g = 1.0 + 0.02 * nrm(ks[15], (DEPTH, D_MODEL), jnp.float32)
    ln_ff_b = 0.02 * nrm(ks[16], (DEPTH, D_MODEL), jnp.float32)
    return {"x": x, "ln_in_g": ln_in_g, "ln_in_b": ln_in_b, "t5_table": t5_table,
            "w_in": w_in, "w_out": w_out, "na_rpb": na_rpb, "sw_sink": sw_sink,
            "diff_lam_q": diff_lam_q, "diff_lam_k": diff_lam_k, "diff_subln_g": diff_subln_g,
            "ln_mix_g": ln_mix_g, "ln_mix_b": ln_mix_b, "w_ff1": w_ff1, "w_ff2": w_ff2,
            "ln_ff_g": ln_ff_g, "ln_ff_b": ln_ff_b}


def reference(x, ln_in_g, ln_in_b, t5_table, w_in, w_out, na_rpb, sw_sink,
              diff_lam_q, diff_lam_k, diff_subln_g, ln_mix_g, ln_mix_b,
              w_ff1, w_ff2, ln_ff_g, ln_ff_b):
    alpha = (2 * DEPTH) ** 0.25
    b, s, _ = x.shape
    split_points = np.cumsum(IN_SPLITS)[:-1].tolist()
    sw_table = t5_table[:, :SW_HEADS]
    diff_table = t5_table[:, SW_HEADS:]
    x = layer_norm(x, ln_in_g, ln_in_b)
    for l in range(DEPTH):
        lam_init = 0.8 - 0.6 * math.exp(-0.3 * l)
        proj = x @ w_in[l]
        qa, ka, va, qb, kb, vb, qc, kc, vc = jnp.split(proj, split_points, axis=-1)
        oa = neighbourhood_attention(
            qa.reshape(b, s, NA_HEADS, HEAD_DIM), ka.reshape(b, s, NA_HEADS, HEAD_DIM),
            va.reshape(b, s, NA_HEADS, HEAD_DIM), na_rpb[l])
        ob = sliding_window_gqa(
            qb.reshape(b, s, SW_HEADS, HEAD_DIM), kb.reshape(b, s, SW_KV_HEADS, HEAD_DIM),
            vb.reshape(b, s, SW_KV_HEADS, HEAD_DIM), sw_sink[l], sw_table)
        oc = differential_attention(
            qc.reshape(b, s, DIFF_HEADS, 2, DIFF_QK_DIM), kc.reshape(b, s, DIFF_HEADS, 2, DIFF_QK_DIM),
            vc.reshape(b, s, DIFF_HEADS, DIFF_V_DIM), diff_lam_q[l], diff_lam_k[l],
            diff_subln_g[l], diff_table, lam_init)
        mix = jnp.concatenate([oa.reshape(b, s, A_W), ob.reshape(b, s, B_Q_W),
                               oc.reshape(b, s, C_V_W)], axis=-1) @ w_out[l]
        x = layer_norm(alpha * x + mix, ln_mix_g[l], ln_mix_b[l])
        hdn = jax.nn.relu(x @ w_ff1[l])
        x = layer_norm(alpha * x + (hdn * hdn) @ w_ff2[l], ln_ff_g[l], ln_ff_b[l])
    return x
```

```python
import math
import numpy as np
from contextlib import ExitStack
import concourse.bass as bass
import concourse.mybir as mybir
from concourse.bass_utils import run_bass_kernel_spmd

F32 = mybir.dt.float32
BF16 = mybir.dt.bfloat16
U8 = mybir.dt.uint8
ALU = mybir.AluOpType
AF = mybir.ActivationFunctionType
AX = mybir.AxisListType

ENGS = ("pe", "act", "dve", "pool", "sp")


class Tok:
    __slots__ = ("name", "w", "r")

    def __init__(self, name=""):
        self.name = name
        self.w = None
        self.r = []


class Ev:
    __slots__ = ("eng", "is_dma", "sem", "val", "used", "op")

    def __init__(self, eng, is_dma, sem=None, val=None):
        self.eng = eng
        self.is_dma = is_dma
        self.sem = sem
        self.val = val
        self.used = False
        self.op = None


class DSem:
    def __init__(self, prog, name):
        self.name = name
        self.count = 0
        self.h = None
        prog.dsems.append(self)


class Op:
    __slots__ = ("eng", "fn", "deps", "ev", "is_dma", "epoch")

    def __init__(self, eng, fn, ev, is_dma):
        self.eng = eng
        self.fn = fn
        self.deps = []
        self.ev = ev
        self.is_dma = is_dma
        self.epoch = 0


class Prog:
    def __init__(self, nc):
        self.nc = nc
        self.ops = {e: [] for e in ENGS}
        self.dsems = []
        self.esem = {}
        self.n_wait = 0
        self.epoch = 0

    def _deps(self, op, reads, writes):
        ev = op.ev
        deps = op.deps
        for t in reads:
            if t.w is not None and t.w is not ev:
                deps.append(t.w)
        for t in writes:
            w = t.w
            if w is not None and w is not ev and (w.is_dma or ev.is_dma or w.eng != ev.eng):
                deps.append(w)
            for r in t.r:
                if r is not ev and (r.is_dma or ev.is_dma or r.eng != ev.eng):
                    deps.append(r)
        for t in reads:
            if not ev.is_dma:
                t.r = [r for r in t.r if r.is_dma or r.eng != ev.eng]
            t.r.append(ev)
        for t in writes:
            t.w = ev
            t.r = []

    def op(self, eng, fn, reads=(), writes=()):
        ev = Ev(eng, False)
        o = Op(eng, fn, ev, False)
        o.epoch = self.epoch
        ev.op = o
        self._deps(o, reads, writes)
        self.ops[eng].append(o)
        return ev

    def dma(self, q, pairs, sem, reads=(), writes=()):
        ev = Ev(q, True, sem, None)
        first_deps = None
        for pr in pairs:
            out_ap, in_ap = pr[0], pr[1]
            sem.count += 16

            def fn(e, out_ap=out_ap, in_ap=in_ap):
                return e.dma_start(out=out_ap, in_=in_ap)
            o = Op(q, fn, ev, True)
            if first_deps is None:
                self._deps(o, reads, writes)
                first_deps = o.deps
            else:
                o.deps = list(first_deps)
            self.ops[q].append(o)
        ev.val = sem.count
        return ev

    def wait(self, eng, evs):
        ev = Ev(eng, False)
        o = Op(eng, None, ev, False)
        o.deps = [e for e in evs if e is not None]
        self.ops[eng].append(o)
        return ev

    def barrier(self):
        evs = []
        for e in ENGS:
            for o in reversed(self.ops[e]):
                if o.fn is not None and not o.is_dma:
                    evs.append(o.ev)
                    break
        for s in self.dsems:
            if s.count:
                evs.append(Ev(None, True, s, s.count))
        for e in ENGS:
            self.wait(e, evs)
        self.epoch += 1

    def finalize(self, stack):
        nc = self.nc
        for e in ENGS:
            for ep in range(self.epoch + 1):
                if any((not o.is_dma) and o.fn is not None and o.epoch == ep for o in self.ops[e]):
                    self.esem[(e, ep)] = stack.enter_context(nc.semaphore("es_%s%d" % (e, ep)))
        for s in self.dsems:
            s.h = stack.enter_context(nc.semaphore("ds_" + s.name))
        for e in ENGS:
            for o in self.ops[e]:
                for d in o.deps:
                    d.used = True
        for e in ENGS:
            c = {}
            for o in self.ops[e]:
                if not o.is_dma and o.fn is not None and o.ev.used:
                    c[o.epoch] = c.get(o.epoch, 0) + 1
                    o.ev.sem = (e, o.epoch)
                    o.ev.val = c[o.epoch]
                    assert o.ev.val < 30000
        block = stack.enter_context(nc.Block())
        prog = self

        def run(engkey):
            def body(eng):
                waited = {}
                for o in prog.ops[engkey]:
                    need = {}
                    for d in o.deps:
                        if d.is_dma:
                            key = ("d", id(d.sem))
                            h = d.sem.h
                        else:
                            key = ("e", d.sem)
                            h = prog.esem[d.sem]
                        if d.val > waited.get(key, 0) and d.val > need.get(key, (0, None))[0]:
                            need[key] = (d.val, h)
                    for key, (v, h) in need.items():
                        eng.wait_ge(h, v)
                        waited[key] = v
                        prog.n_wait += 1
                    if o.fn is None:
                        continue
                    ins = o.fn(eng)
                    if o.is_dma:
                        ins.then_inc(o.ev.sem.h, 16)
                    elif o.ev.used:
                        ins.then_inc(prog.esem[(engkey, o.epoch)], 1)
            return body

        block.tensor(run("pe"))
        block.scalar(run("act"))
        block.vector(run("dve"))
        block.gpsimd(run("pool"))
        block.sync(run("sp"))


D = 1024
DFF = 4096
NEGM = -10000.0
LN_EPS = 1e-5
VW = 650
NQK = 14


def _a_deltas(i, NT):
    if i == 0:
        return [0, 1, 2, 3]
    if i == NT - 1:
        return [-3, -2, -1, 0]
    return [-2, -1, 0, 1, 2]


class Arena:
    def __init__(self, nc, st, nbytes):
        self.t = st.enter_context(nc.sbuf_tensor("arena", [128, nbytes], U8))
        self.off = 0
        self.cap = nbytes

    def alloc(self, cols, dtype):
        sz = cols * (4 if dtype == F32 else 2)
        sz_al = (sz + 63) // 64 * 64
        assert self.off + sz_al <= self.cap, ("arena overflow", self.off, sz_al, self.cap)
        v = self.t[:, self.off:self.off + sz].bitcast(dtype)
        self.off += sz_al
        return v


def build_program(S, L):
    NT = S // 128
    NG = S // 512
    assert S % 512 == 0 and NT >= 8
    alpha = float((2 * L) ** 0.25)
    nc = bass.Bass("TRN2", target_bir_lowering=False)
    dr = lambda n, s, d, kind=None: (nc.dram_tensor(n, s, d, kind=kind) if kind else nc.dram_tensor(n, s, d)).ap()
    x_d = dr("x", [S, D], F32, "ExternalInput")
    w_in_d = dr("w_in", [L, D, 2304], F32, "ExternalInput")
    w_out_d = dr("w_out", [L, D, D], F32, "ExternalInput")
    w_ff1_d = dr("w_ff1", [L, D, DFF], F32, "ExternalInput")
    w_ff2_d = dr("w_ff2", [L, DFF, D], F32, "ExternalInput")
    lnp_d = dr("lnp", [1 + 2 * L, 2, D], F32, "ExternalInput")
    ident_d = dr("ident", [128, 128], F32, "ExternalInput")
    biasA_d = dr("biasA", [L, 5, 128, 5 * 512], F32, "ExternalInput")
    biasB_d = dr("biasB", [128, 2 * 3 * 512], F32, "ExternalInput")
    bandC_d = dr("bandC", [128, 4 * 1152], F32, "ExternalInput")
    cvecC_d = dr("cvecC", [128, 8], F32, "ExternalInput")
    sinkrow_d = dr("sinkrow", [L, 1024], F32, "ExternalInput")
    lamq_d = dr("lamq", [L, 64], F32, "ExternalInput")
    lamk_d = dr("lamk", [L, 64], F32, "ExternalInput")
    subg_d = dr("subg", [L, 64], F32, "ExternalInput")
    out_d = dr("out", [S, D], F32, "ExternalOutput")
    xres_d = dr("xres", [S, D], F32)
    qkT_d = dr("qkT", [NQK, 128, S], BF16)
    vtok_d = dr("vtok", [S, VW], BF16)
    attT_d = dr("attT", [D, S], BF16)
    x1T_d = dr("x1T", [D, S], BF16)

    st = ExitStack()
    P = Prog(nc)
    A = Arena(nc, st, 206 * 1024)
    pball = st.enter_context(nc.psum_tensor("pball", [128, 8 * 512], F32))
    pb = [pball[:, i * 512:(i + 1) * 512] for i in range(8)]
    ptok = [Tok("pb%d" % i) for i in range(8)]
    nsem = [0]

    free_sems = []
    used_sems = []

    def dsem(name):
        if free_sems:
            d = free_sems.pop()
        else:
            nsem[0] += 1
            d = DSem(P, "s%d" % nsem[0])
        used_sems.append(d)
        return d

    free_psems = []
    used_psems = []

    def psem(name):
        if free_psems:
            d = free_psems.pop()
        else:
            nsem[0] += 1
            d = DSem(P, "p%d" % nsem[0])
        used_psems.append(d)
        return d

    def phase_end():
        P.barrier()
        free_sems.extend(used_sems)
        del used_sems[:]
        free_psems.extend(used_psems)
        del used_psems[:]

    def MM(out, lhsT, rhs, start, stop, r, w, tp=None, skip=False):
        kw = {}
        if tp is not None:
            kw["tile_position"] = tp
        if skip:
            kw["skip_group_check"] = True
        P.op("pe", lambda e: e.matmul(out, lhsT, rhs, start=start, stop=stop, **kw), r, w)

    def ACTF(out, in_, func, r, w, bias=None, scale=None):
        kw = {}
        if bias is not None:
            kw["bias"] = bias
        if scale is not None:
            kw["scale"] = scale
        return P.op("act", lambda e: e.activation(out, in_, func, **kw), r, w)

    def TT(eng, out, in0, in1, op, r, w):
        return P.op(eng, lambda e: e.tensor_tensor(out, in0, in1, op), r, w)

    def STT(eng, out, in0, scalar, in1, op0, op1, r, w):
        return P.op(eng, lambda e: e.scalar_tensor_tensor(out, in0, scalar, in1, op0=op0, op1=op1), r, w)

    def TS(eng, out, in0, s1, s2, op0, op1, r, w):
        if s2 is None:
            return P.op(eng, lambda e: e.tensor_scalar(out, in0, s1, None, op0=op0), r, w)
        return P.op(eng, lambda e: e.tensor_scalar(out, in0, s1, s2, op0=op0, op1=op1), r, w)

    def CP(eng, out, in_, r, w):
        if eng == "act":
            return ACTF(out, in_, AF.Copy, r, w)
        return P.op(eng, lambda e: e.tensor_copy(out, in_), r, w)

    def MEMSET(eng, ap, val, w):
        return P.op(eng, lambda e: e.memset(ap, val), (), w)

    def SPLIT_MM(out_ps, t_out, p0, p1, src, t_src, hi, lo, t_hl):
        CP("dve", hi[p0:p1, :], src[p0:p1, :], [t_src], [t_hl])
        STT("dve", lo[p0:p1, :], hi[p0:p1, :], -1.0, src[p0:p1, :], ALU.mult, ALU.add, [t_src, t_hl], [t_hl])
        MM(out_ps, ones_b[p0:p1, 0:64], hi[p0:p1, :], True, False, [t_ones, t_hl], [t_out])
        MM(out_ps, ones_b[p0:p1, 0:64], lo[p0:p1, :], False, True, [t_ones, t_hl], [t_out])

    idf = A.alloc(128, F32)
    idb = A.alloc(128, BF16)
    ones_f = A.alloc(64, F32)
    ones_b = A.alloc(64, BF16)
    epsc = A.alloc(1, F32)
    t_idf, t_idb, t_ones, t_eps = Tok(), Tok(), Tok(), Tok()
    s_misc = dsem("misc")
    P.dma("sp", [(idf, ident_d)], s_misc, writes=[t_idf])
    CP("dve", idb, idf, [t_idf], [t_idb])
    MEMSET("pool", ones_f, 1.0, [t_ones])
    MEMSET("pool", ones_b, 1.0, [t_ones])
    MEMSET("pool", epsc, LN_EPS, [t_eps])
    lnsm = []
    for i in range(2):
        lnsm.append(dict(stats=A.alloc(12, F32).rearrange("p (c f) -> p c f", c=2), mv=A.alloc(2, F32),
                         sd=A.alloc(1, F32), rstd=A.alloc(1, F32), nb=A.alloc(1, F32),
                         t_stats=Tok(), t_mv=Tok(), t_sd=Tok(), t_rstd=Tok(), t_nb=Tok()))
    mark = A.off
    lncnt = [0]

    def LN(z, t_z, g, b, t_gb, ybf=None, t_ybf=None):
        sm = lnsm[lncnt[0] % 2]
        lncnt[0] += 1
        zv = z.rearrange("p (c f) -> p c f", c=2)
        for c in range(2):
            P.op("dve", lambda e, c=c: e.bn_stats(sm["stats"][:, c, :], zv[:, c, :]), [t_z], [sm["t_stats"]])
        P.op("dve", lambda e: e.bn_aggr(sm["mv"], sm["stats"]), [sm["t_stats"]], [sm["t_mv"]])
        ACTF(sm["sd"], sm["mv"][:, 1:2], AF.Sqrt, [sm["t_mv"], t_eps], [sm["t_sd"]], bias=epsc, scale=1.0)
        P.op("dve", lambda e: e.reciprocal(sm["rstd"], sm["sd"]), [sm["t_sd"]], [sm["t_rstd"]])
        STT("dve", sm["nb"], sm["mv"][:, 0:1], -1.0, sm["rstd"], ALU.mult, ALU.mult,
            [sm["t_mv"], sm["t_rstd"]], [sm["t_nb"]])
        ACTF(z, z, AF.Identity, [t_z, sm["t_nb"], sm["t_rstd"]], [t_z], bias=sm["nb"], scale=sm["rstd"])
        TT("dve", z, z, g, ALU.mult, [t_z, t_gb], [t_z])
        TT("pool", z, z, b, ALU.add, [t_z, t_gb], [t_z])
        if ybf is not None:
            CP("act", ybf, z, [t_z], [t_ybf])

    def load_gb(idx, sem):
        gb = A.alloc(2 * D, F32).rearrange("p (c f) -> p c f", c=2)
        t = Tok()
        P.dma("sp", [(gb[:, 0, :], lnp_d[idx, 0:1, :].to_broadcast([128, D])),
                     (gb[:, 1, :], lnp_d[idx, 1:2, :].to_broadcast([128, D]))], sem, writes=[t])
        return gb, t

    def transposes(src_bf, t_src, dstT, t_dst, col0, bank):
        pT = pb[bank].bitcast(BF16).rearrange("p (k f) -> p k f", k=8)
        for k in range(8):
            P.op("pe", lambda e, k=k: e.transpose(pT[:, k, :], src_bf[:, k * 128:(k + 1) * 128], idb),
                 [t_src, t_idb], [ptok[bank]])
        CP("dve", dstT[:, :, col0:col0 + 128], pT, [ptok[bank]], [t_dst])

    def phase_P1(l):
        A.off = mark
        s_w = psem("w")
        w_sb = A.alloc(8 * 2432, BF16).rearrange("p (k n) -> p k n", k=8)
        t_w = Tok()
        src = w_in_d[l].rearrange("(k p) n -> p k n", p=128)
        cols = [(0, 0, 256), (256, 256, 256), (512, 768, 512), (1024, 1280, 64), (1088, 1280, 64),
                (1152, 1344, 64), (1216, 1344, 64), (1280, 1536, 256), (1536, 1792, 256),
                (1792, 512, 256), (2048, 1408, 128), (2176, 2048, 256)]
        P.dma("pool", [(w_sb[:, :, d0:d0 + w], src[:, :, s0:s0 + w]) for d0, s0, w in cols], s_w, writes=[t_w])
        if l == 0:
            gb, t_gb = load_gb(0, dsem("gb"))
        nx = 3
        xin = [A.alloc(D, F32) for _ in range(nx)]
        t_xin = [Tok() for _ in range(nx)]
        s_xin = [dsem("xin") for _ in range(nx)]
        s_yb = [dsem("yst") for _ in range(nx)]
        ybf = [A.alloc(D, BF16) for _ in range(2)]
        t_ybf = [Tok() for _ in range(2)]
        xT = [A.alloc(8 * 512, BF16).rearrange("p (k n) -> p k n", k=8) for _ in range(2)]
        t_xT = [Tok() for _ in range(2)]
        stg = [A.alloc(NQK * 512, BF16).rearrange("p (t n) -> p t n", t=NQK) for _ in range(2)]
        t_stg = [Tok() for _ in range(2)]
        s_stg = [dsem("stg") for _ in range(2)]
        vstg = [A.alloc(4 * VW, BF16).rearrange("p (t h c) -> p t h c", t=4, h=10) for _ in range(2)]
        t_vstg = [Tok() for _ in range(2)]
        s_vstg = [dsem("vstg") for _ in range(2)]
        for i in range(2):
            MEMSET("pool", vstg[i][:, :, :, 64:65], 1.0, [t_vstg[i]])
        src_d = x_d if l == 0 else xres_d
        qk_view = qkT_d.rearrange("t p s -> p t s")
        v_view = vtok_d.rearrange("(n p) c -> p n c", p=128)

        def load_x(ti):
            sl = ti % nx
            P.dma("sp", [(xin[sl], src_d[ti * 128:(ti + 1) * 128, :])], s_xin[sl], writes=[t_xin[sl]])

        load_x(0)
        load_x(1)
        pcnt = 0
        for g in range(NG):
            gs = g % 2
            for tt in range(4):
                ti = g * 4 + tt
                if ti + 2 < NT:
                    load_x(ti + 2)
                sl = ti % nx
                ys = ti % 2
                if l == 0:
                    LN(xin[sl], t_xin[sl], gb[:, 0, :], gb[:, 1, :], t_gb, ybf[ys], t_ybf[ys])
                    P.dma("sp", [(xres_d[ti * 128:(ti + 1) * 128, :], xin[sl])], s_yb[sl], reads=[t_xin[sl]])
                else:
                    CP("act", ybf[ys], xin[sl], [t_xin[sl]], [t_ybf[ys]])
                transposes(ybf[ys], t_ybf[ys], xT[gs], t_xT[gs], tt * 128, 7)
            for t in range(NQK):
                bk = pcnt % 3
                pcnt += 1
                for k in range(8):
                    MM(pb[bk], w_sb[:, k, t * 128:(t + 1) * 128], xT[gs][:, k, :], k == 0, k == 7,
                       [t_w, t_xT[gs]], [ptok[bk]])
                CP("act" if t % 2 == 0 else "dve", stg[gs][:, t, :], pb[bk], [ptok[bk]], [t_stg[gs]])
            P.dma("sp", [(qk_view[:, :, g * 512:(g + 1) * 512], stg[gs])], s_stg[gs], reads=[t_stg[gs]])
            for tt in range(4):
                ba, bb = (3, 4) if tt % 2 == 0 else (5, 6)
                for k in range(8):
                    MM(pb[ba], xT[gs][:, k, tt * 128:(tt + 1) * 128], w_sb[:, k, 1792:2304], k == 0, k == 7,
                       [t_w, t_xT[gs]], [ptok[ba]])
                for k in range(8):
                    MM(pb[bb][:, 0:128], xT[gs][:, k, tt * 128:(tt + 1) * 128], w_sb[:, k, 2304:2432], k == 0, k == 7,
                       [t_w, t_xT[gs]], [ptok[bb]])
                CP("dve", vstg[gs][:, tt, 0:8, 0:64], pb[ba].rearrange("p (h c) -> p h c", h=8),
                   [ptok[ba]], [t_vstg[gs]])
                CP("act", vstg[gs][:, tt, 8:10, 0:64], pb[bb][:, 0:128].rearrange("p (h c) -> p h c", h=2),
                   [ptok[bb]], [t_vstg[gs]])
            P.dma("sp", [(v_view[:, g * 4:(g + 1) * 4, :], vstg[gs].rearrange("p t h c -> p t (h c)"))],
                  s_vstg[gs], reads=[t_vstg[gs]])
        phase_end()

    def load_big(dst, src, nsplit, sem, tok):
        n = dst.shape[1]
        step = (n + nsplit - 1) // nsplit
        pairs = []
        for a in range(0, n, step):
            b = min(n, a + step)
            pairs.append((dst[:, a:b], src[:, a:b]))
        P.dma("sp", pairs, sem, writes=[tok])

    def phase_AB(l, mixer):
        A.off = mark
        qk_view = qkT_d.rearrange("t p s -> p t s")
        v_view = vtok_d.rearrange("(n p) c -> p n c", p=128)
        if mixer == "A":
            q0, nqt, k0, ngrp, deltas, vcol0, nvh, arow0 = 0, 2, 2, 1, [-2, -1, 0, 1, 2], 0, 4, 0
        else:
            q0, nqt, k0, ngrp, deltas, vcol0, nvh, arow0 = 4, 4, 8, 2, [-1, 0, 1], 260, 2, 256
        nd = len(deltas)
        QT = A.alloc(nqt * S, BF16).rearrange("p (t s) -> p t s", t=nqt)
        KT = A.alloc(2 * S, BF16).rearrange("p (t s) -> p t s", t=2)
        V = A.alloc(NT * nvh * 65, BF16).rearrange("p (n c) -> p n c", n=NT)
        t_Q, t_K, t_V = Tok(), Tok(), Tok()
        P.dma("sp", [(QT[:, t, :], qk_view[:, q0 + t, :]) for t in range(nqt)], dsem("ldq"), writes=[t_Q])
        P.dma("sp", [(KT[:, t, :], qk_view[:, k0 + t, :]) for t in range(2)], dsem("ldk"), writes=[t_K])
        load_big(V, v_view[:, :, vcol0:vcol0 + nvh * 65], max(1, NT // 8), dsem("ldv"), t_V)
        if mixer == "A":
            bias_g = A.alloc(5 * 512, F32).rearrange("p (d n) -> p d n", d=5)
            bias_s = A.alloc(5 * 512, F32).rearrange("p (d n) -> p d n", d=5)
            t_bg, t_bs = Tok(), Tok()
            s_bg, s_bs = dsem("bg"), dsem("bs")
            P.dma("sp", [(bias_g, biasA_d[l, 0].rearrange("p (d n) -> p d n", d=5))], s_bg, writes=[t_bg])
            special = {0: 1, 1: 2, NT - 2: 3, NT - 1: 4}
        else:
            bias_b = A.alloc(6 * 512, F32).rearrange("p (g d n) -> p g d n", g=2, d=3)
            t_bb = Tok()
            s_bb = dsem("bb")
            P.dma("sp", [(bias_b, biasB_d.rearrange("p (g d n) -> p g d n", g=2, d=3))], s_bb, writes=[t_bb])
            esrow = A.alloc(1024, F32)
            t_es = Tok()
            P.dma("sp", [(esrow[64:65, :], sinkrow_d[l:l + 1, :])], dsem("es"), writes=[t_es])
            ACTF(esrow[64:65, :], esrow[64:65, :], AF.Exp, [t_es], [t_es])
        tmpb = [A.alloc(512, F32) for _ in range(2)]
        t_tmpb = [Tok() for _ in range(2)]
        PT = [A.alloc(512, BF16) for _ in range(3)]
        t_PT = [Tok() for _ in range(3)]
        osb = [A.alloc(512, F32) for _ in range(2)]
        t_osb = [Tok() for _ in range(2)]
        rr = [A.alloc(512, F32) for _ in range(2)]
        t_rr = [Tok() for _ in range(2)]
        hi = [A.alloc(512, BF16) for _ in range(2)]
        lo = [A.alloc(512, BF16) for _ in range(2)]
        t_hl = [Tok() for _ in range(2)]
        stage = [A.alloc(ngrp * 4 * 512, BF16).rearrange("p (g j n) -> p g j n", g=ngrp, j=4) for _ in range(2)]
        t_stage = [Tok() for _ in range(2)]
        s_stage = [dsem("stage") for _ in range(2)]
        psi = 0
        cnt = 0
        import os as _os
        dbg = int(_os.environ.get("KDBG_A", "9"))
        for i in range(NT):
            if dbg < 2:
                break
            g4 = i // 4
            ss = g4 % 2
            if mixer == "A" and i in special:
                P.dma("sp", [(bias_s, biasA_d[l, special[i]].rearrange("p (d n) -> p d n", d=5))], s_bs, writes=[t_bs])
            for grp in range(ngrp):
                ob = 4 + (cnt % 2)
                os_ = cnt % 2
                cnt += 1
                dl = _a_deltas(i, NT) if mixer == "A" else deltas
                cands = [d for d in dl if 0 <= i + d < NT]
                for ci, d in enumerate(cands):
                    c = i + d
                    pp = psi % 2
                    tb = psi % 2
                    pt = psi % 3
                    psi += 1
                    for j in range(4):
                        h = grp * 4 + j
                        pr = (h % 2) * 64
                        qt = h // 2
                        kt = (j // 2) if mixer == "A" else grp
                        a_, b_ = j // 2, j % 2
                        MM(pb[2 * pp + b_][:, a_ * 128:(a_ + 1) * 128], KT[pr:pr + 64, kt, c * 128:(c + 1) * 128],
                           QT[pr:pr + 64, qt, i * 128:(i + 1) * 128], True, True, [t_K, t_Q], [ptok[2 * pp + b_]])
                    di = dl.index(d)
                    if mixer == "A":
                        if i in special:
                            btile, t_b = bias_s[:, di, :], t_bs
                        else:
                            btile, t_b = bias_g[:, di, :], t_bg
                    else:
                        btile, t_b = bias_b[:, grp, di, :], t_bb
                    ps2 = pball[:, 2 * pp * 512:(2 * pp + 2) * 512].rearrange("p (b x) -> p b x", b=2)[:, :, 0:256]
                    STT("dve", tmpb[tb].rearrange("p (b x) -> p b x", b=2), ps2, 0.125,
                        btile.rearrange("p (b x) -> p b x", b=2),
                        ALU.mult, ALU.add, [ptok[2 * pp], ptok[2 * pp + 1], t_b], [t_tmpb[tb]])
                    ACTF(PT[pt], tmpb[tb], AF.Exp, [t_tmpb[tb]], [t_PT[pt]])
                    for j in range(4):
                        if dbg < 3:
                            break
                        vc = (j if mixer == "A" else grp) * 65
                        pcol = ((j % 2) * 2 + j // 2) * 128
                        MM(pb[ob][0:65, j * 128:(j + 1) * 128], V[:, c, vc:vc + 65], PT[pt][:, pcol:pcol + 128],
                           ci == 0 and j == 0, ci == len(cands) - 1, [t_V, t_PT[pt]], [ptok[ob]], skip=True)
                if dbg < 4:
                    continue
                CP("dve" if cnt % 2 else "act", osb[os_][0:65, :], pb[ob][0:65, :], [ptok[ob]], [t_osb[os_]])
                if mixer == "B":
                    TT("dve", osb[os_][64:65, :], osb[os_][64:65, :], esrow[64:65, grp * 512:(grp + 1) * 512], ALU.add,
                       [t_osb[os_], t_es], [t_osb[os_]])
                P.op("dve", lambda e, a=rr[os_], b=osb[os_]: e.reciprocal(a[64:65, :], b[64:65, :]),
                     [t_osb[os_]], [t_rr[os_]])
                SPLIT_MM(pb[6][0:64, :], ptok[6], 64, 65, rr[os_], t_rr[os_], hi[os_], lo[os_], t_hl[os_])
                TT("dve", stage[ss][0:64, grp, :, (i % 4) * 128:(i % 4 + 1) * 128],
                   osb[os_][0:64, :].rearrange("p (j n) -> p j n", j=4),
                   pb[6][0:64, :].rearrange("p (j n) -> p j n", j=4), ALU.mult,
                   [t_osb[os_], ptok[6]], [t_stage[ss]])
            if i % 4 == 3 and dbg >= 5:
                pairs = []
                for grp in range(ngrp):
                    r0 = arow0 + grp * 256
                    pairs.append((attT_d[r0:r0 + 256, g4 * 512:(g4 + 1) * 512].rearrange("(j p) s -> p j s", p=64),
                                  stage[ss][0:64, grp, :, :]))
                P.dma("sp", pairs, s_stage[ss], reads=[t_stage[ss]])
        phase_end()

    def phase_C(l):
        A.off = mark
        lam_init = 0.8 - 0.6 * math.exp(-0.3 * l)
        scale = float(32 ** -0.5)
        qk_view = qkT_d.rearrange("t p s -> p t s")
        v_view = vtok_d.rearrange("(n p) c -> p n c", p=128)
        QT = A.alloc(2 * S, BF16).rearrange("p (t s) -> p t s", t=2)
        KT = A.alloc(2 * S, BF16).rearrange("p (t s) -> p t s", t=2)
        V = A.alloc(NT * 260, BF16).rearrange("p (n c) -> p n c", n=NT)
        t_Q, t_K, t_V = Tok(), Tok(), Tok()
        P.dma("sp", [(KT[:, t, :], qk_view[:, 12 + t, :]) for t in range(2)], dsem("ldk"), writes=[t_K])
        P.dma("sp", [(QT[:, t, :], qk_view[:, 10 + t, :]) for t in range(2)], dsem("ldq"), writes=[t_Q])
        load_big(V, v_view[:, :, 390:650], max(1, NT // 8), dsem("ldv"), t_V)
        band = A.alloc(4 * 1152, F32).rearrange("p (h n) -> p h n", h=4)
        cvec = A.alloc(8, F32)
        t_band, t_cvec = Tok(), Tok()
        P.dma("sp", [(band, bandC_d.rearrange("p (h n) -> p h n", h=4))], dsem("band"), writes=[t_band])
        P.dma("sp", [(cvec, cvecC_d)], dsem("cvec"), writes=[t_cvec])
        lq = A.alloc(64, F32)
        lk = A.alloc(64, F32)
        e2 = A.alloc(2, F32)
        nlam = A.alloc(1, F32)
        gcol = A.alloc(1, F32)
        t_lq, t_lk, t_e2, t_nlam, t_g = Tok(), Tok(), Tok(), Tok(), Tok()
        P.dma("sp", [(lq, lamq_d[l:l + 1, :].to_broadcast([128, 64]))], dsem("lq"), writes=[t_lq])
        P.dma("sp", [(lk, lamk_d[l:l + 1, :].to_broadcast([128, 64]))], dsem("lk"), writes=[t_lk])
        P.dma("sp", [(gcol[0:64, :], subg_d[l].rearrange("(p o) -> p o", o=1))], dsem("g"), writes=[t_g])
        TT("dve", lq, lq, lk, ALU.mult, [t_lq, t_lk], [t_lq])
        P.op("dve", lambda e: e.reduce_sum(e2, lq.rearrange("p (m d) -> p m d", m=2), AX.X), [t_lq], [t_e2])
        ACTF(e2, e2, AF.Exp, [t_e2], [t_e2])
        TT("dve", nlam, e2[:, 1:2], e2[:, 0:1], ALU.subtract, [t_e2], [t_nlam])
        TS("dve", nlam, nlam, -lam_init, None, ALU.add, None, [t_nlam], [t_nlam])
        TS("dve", gcol[0:64, :], gcol[0:64, :], 1.0 - lam_init, None, ALU.mult, None, [t_g], [t_g])

        tmpb = [A.alloc(512, F32) for _ in range(2)]
        t_tmpb = [Tok() for _ in range(2)]
        PT = [A.alloc(512, BF16) for _ in range(4)]
        t_PT = [Tok() for _ in range(4)]
        osb = [A.alloc(512, F32) for _ in range(2)]
        t_osb = [Tok() for _ in range(2)]
        rr = [A.alloc(512, F32) for _ in range(2)]
        t_rr = [Tok() for _ in range(2)]
        av = [A.alloc(512, F32) for _ in range(2)]
        t_av = [Tok() for _ in range(2)]
        ov = A.alloc(512, F32)
        sq = A.alloc(512, F32)
        sdv = A.alloc(512, F32)
        t_ov, t_sq, t_sdv = Tok(), Tok(), Tok()
        hi = [A.alloc(512, BF16) for _ in range(2)]
        lo = [A.alloc(512, BF16) for _ in range(2)]
        t_hl = [Tok() for _ in range(2)]
        stage = [A.alloc(4 * 512, BF16).rearrange("p (j n) -> p j n", j=4) for _ in range(2)]
        t_stage = [Tok() for _ in range(2)]
        s_stage = [dsem("stc") for _ in range(2)]
        psi = 0
        hcnt = 0
        zi = 0
        for G in range(NG):
            ss = G % 2
            for h in range(4):
                tile = h // 2
                obs = [3 + 2 * (hcnt % 2), 4 + 2 * (hcnt % 2)]
                hcnt += 1
                for m in range(2):
                    pr = ((h % 2) * 2 + m) * 32
                    ob = obs[m]
                    tp = (96, 0) if pr == 96 else None
                    for c in range(NT):
                        bk = psi % 3
                        pt = psi % 4
                        psi += 1
                        MM(pb[bk], KT[pr:pr + 32, tile, c * 128:(c + 1) * 128], QT[pr:pr + 32, tile, G * 512:(G + 1) * 512],
                           True, True, [t_K, t_Q], [ptok[bk]], tp=tp)
                        o = c - 4 * G
                        if -1 <= o <= 4:
                            tb = zi % 2
                            zi += 1
                            STT("dve", tmpb[tb], pb[bk], scale, band[:, h, 512 - 128 * o:1024 - 128 * o], ALU.mult, ALU.add,
                                [ptok[bk], t_band], [t_tmpb[tb]])
                            ACTF(PT[pt], tmpb[tb], AF.Exp, [t_tmpb[tb]], [t_PT[pt]])
                        else:
                            side = 0 if o < 0 else 1
                            ACTF(PT[pt], pb[bk], AF.Exp, [ptok[bk], t_cvec], [t_PT[pt]],
                                 bias=cvec[:, side * 4 + h:side * 4 + h + 1], scale=scale)
                        MM(pb[ob][0:65, :], V[:, c, h * 65:(h + 1) * 65], PT[pt], c == 0, c == NT - 1,
                           [t_V, t_PT[pt]], [ptok[ob]])
                for m in range(2):
                    ob = obs[m]
                    CP("dve", osb[m][0:65, :], pb[ob][0:65, :], [ptok[ob]], [t_osb[m]])
                    P.op("dve", lambda e, a=rr[m], b=osb[m]: e.reciprocal(a[64:65, :], b[64:65, :]), [t_osb[m]], [t_rr[m]])
                    SPLIT_MM(pb[7][0:64, :], ptok[7], 64, 65, rr[m], t_rr[m], hi[m], lo[m], t_hl[m])
                    TT("dve", av[m][0:64, :], osb[m][0:64, :], pb[7][0:64, :], ALU.mult, [t_osb[m], ptok[7]], [t_av[m]])
                STT("dve", ov[0:64, :], av[1][0:64, :], nlam[0:64, :], av[0][0:64, :], ALU.mult, ALU.add,
                    [t_av[0], t_av[1], t_nlam], [t_ov])
                TT("pool", sq[0:64, :], ov[0:64, :], ov[0:64, :], ALU.mult, [t_ov], [t_sq])
                SPLIT_MM(pb[7][0:64, :], ptok[7], 0, 64, sq, t_sq, hi[0], lo[0], t_hl[0])
                ACTF(sdv[0:64, :], pb[7][0:64, :], AF.Sqrt, [ptok[7], t_eps], [t_sdv], bias=epsc[0:64, :], scale=1.0 / 64.0)
                P.op("dve", lambda e: e.reciprocal(sdv[0:64, :], sdv[0:64, :]), [t_sdv], [t_sdv])
                STT("dve", stage[ss][0:64, h, :], ov[0:64, :], gcol[0:64, :], sdv[0:64, :], ALU.mult, ALU.mult,
                    [t_ov, t_g, t_sdv], [t_stage[ss]])
            P.dma("sp", [(attT_d[768:1024, G * 512:(G + 1) * 512].rearrange("(j p) s -> p j s", p=64),
                          stage[ss][0:64, :, :])], s_stage[ss], reads=[t_stage[ss]])
        phase_end()

    def phase_P3a(l):
        A.off = mark
        s_w = psem("w3")
        wo = A.alloc(8 * D, BF16).rearrange("p (k n) -> p k n", k=8)
        t_wo = Tok()
        P.dma("pool", [(wo, w_out_d[l].rearrange("(k p) n -> p k n", p=128))], s_w, writes=[t_wo])
        gb1, t_gb1 = load_gb(1 + 2 * l, dsem("gb"))
        aT = [A.alloc(8 * 512, BF16).rearrange("p (k n) -> p k n", k=8) for _ in range(2)]
        t_aT = [Tok() for _ in range(2)]
        s_aT = [dsem("aT") for _ in range(2)]
        nx = 3
        xr = [A.alloc(D, F32) for _ in range(nx)]
        t_xr = [Tok() for _ in range(nx)]
        s_xr = [dsem("xr") for _ in range(nx)]
        s_xs = [dsem("xs") for _ in range(nx)]
        x1bf = [A.alloc(D, BF16) for _ in range(2)]
        t_x1bf = [Tok() for _ in range(2)]
        x1T = [A.alloc(8 * 512, BF16).rearrange("p (k n) -> p k n", k=8) for _ in range(2)]
        t_x1T = [Tok() for _ in range(2)]
        s_x1T = [dsem("x1T") for _ in range(2)]
        aview = attT_d.rearrange("(k p) s -> p k s", p=128)
        xview = x1T_d.rearrange("(k p) s -> p k s", p=128)

        def loads(g):
            P.dma("sp", [(aT[g % 2], aview[:, :, g * 512:(g + 1) * 512])], s_aT[g % 2], writes=[t_aT[g % 2]])

        def load_xr(ti):
            sl = ti % nx
            P.dma("sp", [(xr[sl], xres_d[ti * 128:(ti + 1) * 128, :])], s_xr[sl], writes=[t_xr[sl]])

        loads(0)
        load_xr(0)
        load_xr(1)
        for g in range(NG):
            gs = g % 2
            if g + 1 < NG:
                loads(g + 1)
            for tt in range(4):
                ti = g * 4 + tt
                if ti + 2 < NT:
                    load_xr(ti + 2)
                xs = ti % nx
                bs = ti % 2
                for half in range(2):
                    bk = (tt % 2) * 2 + half
                    for k in range(8):
                        MM(pb[bk], aT[gs][:, k, tt * 128:(tt + 1) * 128], wo[:, k, half * 512:(half + 1) * 512],
                           k == 0, k == 7, [t_aT[gs], t_wo], [ptok[bk]])
                    STT("dve", xr[xs][:, half * 512:(half + 1) * 512], xr[xs][:, half * 512:(half + 1) * 512], alpha,
                        pb[bk], ALU.mult, ALU.add, [t_xr[xs], ptok[bk]], [t_xr[xs]])
                LN(xr[xs], t_xr[xs], gb1[:, 0, :], gb1[:, 1, :], t_gb1, x1bf[bs], t_x1bf[bs])
                P.dma("sp", [(xres_d[ti * 128:(ti + 1) * 128, :], xr[xs])], s_xs[xs], reads=[t_xr[xs]])
                transposes(x1bf[bs], t_x1bf[bs], x1T[gs], t_x1T[gs], tt * 128, 7)
            P.dma("sp", [(xview[:, :, g * 512:(g + 1) * 512], x1T[gs])], s_x1T[gs], reads=[t_x1T[gs]])
        phase_end()

    def phase_P3b(l):
        A.off = mark
        last = (l == L - 1)
        s_w = psem("w4")
        w1 = A.alloc(8 * DFF, BF16).rearrange("p (k n) -> p k n", k=8)
        w2 = A.alloc(32 * D, BF16).rearrange("p (k n) -> p k n", k=32)
        t_w1, t_w2 = Tok(), Tok()
        w1src = w_ff1_d[l].rearrange("(k p) n -> p k n", p=128)
        P.dma("pool", [(w1[:, :, a * 1024:(a + 1) * 1024], w1src[:, :, a * 1024:(a + 1) * 1024]) for a in range(4)],
              s_w, writes=[t_w1])
        w2src = w_ff2_d[l].rearrange("(k p) n -> p k n", p=128)
        P.dma("pool", [(w2[:, a * 8:(a + 1) * 8, :], w2src[:, a * 8:(a + 1) * 8, :]) for a in range(4)],
              psem("w2"), writes=[t_w2])
        gb2, t_gb2 = load_gb(2 + 2 * l, dsem("gb"))
        TG = 256
        NGR = S // TG
        x1T = [A.alloc(8 * TG, BF16).rearrange("p (k n) -> p k n", k=8) for _ in range(2)]
        t_x1T = [Tok() for _ in range(2)]
        s_x1T = [dsem("x1Tl") for _ in range(2)]
        nx = 4
        xr = [A.alloc(D, F32) for _ in range(nx)]
        t_xr = [Tok() for _ in range(nx)]
        s_xr = [dsem("xr") for _ in range(nx)]
        s_xs = [dsem("xs") for _ in range(nx)]
        rl = [A.alloc(512, F32) for _ in range(2)]
        t_rl = [Tok() for _ in range(2)]
        hT = [A.alloc(8 * TG, BF16).rearrange("p (f n) -> p f n", f=8) for _ in range(2)]
        t_hT = [Tok() for _ in range(2)]
        xview = x1T_d.rearrange("(k p) s -> p k s", p=128)
        dst_d = out_d if last else xres_d

        def loads(gi):
            sl = gi % 2
            P.dma("sp", [(x1T[sl], xview[:, :, gi * TG:(gi + 1) * TG])], s_x1T[sl], writes=[t_x1T[sl]])
            for tt in range(2):
                ti = gi * 2 + tt
                P.dma("sp", [(xr[ti % nx], xres_d[ti * 128:(ti + 1) * 128, :])], s_xr[ti % nx], writes=[t_xr[ti % nx]])

        loads(0)
        rcnt = 0
        hcnt = 0
        for gi in range(NGR):
            sl = gi % 2
            if gi + 1 < NGR:
                loads(gi + 1)
            for fb in range(4):
                hs = hcnt % 2
                hcnt += 1
                for fp in range(4):
                    bk = fp
                    for f2 in range(2):
                        f = fb * 8 + fp * 2 + f2
                        for k in range(8):
                            MM(pb[bk][:, f2 * 256:(f2 + 1) * 256], w1[:, k, f * 128:(f + 1) * 128], x1T[sl][:, k, :],
                               k == 0, k == 7, [t_w1, t_x1T[sl]], [ptok[bk]])
                    rs = rcnt % 2
                    rcnt += 1
                    ACTF(rl[rs], pb[bk], AF.Relu, [ptok[bk]], [t_rl[rs]])
                    TT("pool" if fp % 2 else "dve", hT[hs][:, fp * 2:fp * 2 + 2, :],
                       rl[rs].rearrange("p (f n) -> p f n", f=2), rl[rs].rearrange("p (f n) -> p f n", f=2),
                       ALU.mult, [t_rl[rs]], [t_hT[hs]])
                for tt in range(2):
                    for half in range(2):
                        bk = 4 + tt * 2 + half
                        for ft in range(8):
                            f = fb * 8 + ft
                            MM(pb[bk], hT[hs][:, ft, tt * 128:(tt + 1) * 128], w2[:, f, half * 512:(half + 1) * 512],
                               f == 0, f == 31, [t_hT[hs], t_w2], [ptok[bk]])
            for tt in range(2):
                ti = gi * 2 + tt
                xs = ti % nx
                for half in range(2):
                    bk = 4 + tt * 2 + half
                    STT("dve", xr[xs][:, half * 512:(half + 1) * 512], xr[xs][:, half * 512:(half + 1) * 512], alpha,
                        pb[bk], ALU.mult, ALU.add, [t_xr[xs], ptok[bk]], [t_xr[xs]])
                LN(xr[xs], t_xr[xs], gb2[:, 0, :], gb2[:, 1, :], t_gb2)
                P.dma("sp", [(dst_d[ti * 128:(ti + 1) * 128, :], xr[xs])], s_xs[xs], reads=[t_xr[xs]])
        phase_end()

    import os as _os
    _ph = _os.environ.get("KDBG_PHASES")
    phases = [("P1", phase_P1), ("A", lambda l: phase_AB(l, "A")), ("B", lambda l: phase_AB(l, "B")),
              ("C", phase_C), ("P3a", phase_P3a), ("P3b", phase_P3b)]
    for l in range(L):
        for nm, fn in phases:
            if _ph is None or nm in _ph.split(","):
                fn(l)
    P.finalize(st)
    st.close()
    return nc


def _t5_bucket_np(rel):
    nb = 16
    max_exact = 8
    rel = np.asarray(rel, dtype=np.int32)
    n = np.abs(rel)
    nf = np.maximum(n, 1).astype(np.float32)
    large = max_exact + (np.log(nf / np.float32(max_exact)) / np.float32(math.log(128 / max_exact))
                         * np.float32(nb - max_exact)).astype(np.int32)
    large = np.minimum(large, nb - 1)
    return np.where(rel > 0, nb, 0) + np.where(n < max_exact, n, large)


def _host_tables(S, L, t5_table, na_rpb, sw_sink):
    NT = S // 128
    rows = S // 64
    kk = np.arange(128)
    qq = np.arange(128)
    biasA = np.full((L, 5, 128, 5, 4, 128), NEGM, np.float32)
    sets = [NT // 2, 0, 1, NT - 2, NT - 1]
    for si, i in enumerate(sets):
        qtok = i * 128 + qq
        qr, qc = qtok // 64, qtok % 64
        rs = np.clip(qr - 4, 0, rows - 8)
        cs = np.clip(qc - 8, 0, 64 - 16)
        for di, d in enumerate(_a_deltas(i, NT)):
            c = i + d
            if c < 0 or c >= NT:
                continue
            ktok = c * 128 + kk
            kr, kc = ktok // 64, ktok % 64
            valid = ((kr[:, None] >= rs[None, :]) & (kr[:, None] < rs[None, :] + 8)
                     & (kc[:, None] >= cs[None, :]) & (kc[:, None] < cs[None, :] + 16))
            dri = np.clip(kr[:, None] - qr[None, :] + 7, 0, 14)
            dci = np.clip(kc[:, None] - qc[None, :], -15, 15) + 15
            for l in range(L):
                for h in range(4):
                    vals = na_rpb[l, h][dri, dci]
                    biasA[l, si, :, di, (h % 2) * 2 + h // 2, :] = np.where(valid, vals, np.float32(NEGM))
    biasA = biasA.reshape(L, 5, 128, 5 * 512)
    biasB = np.full((128, 2, 3, 4, 128), NEGM, np.float32)
    for di, d in enumerate([-1, 0, 1]):
        rel = d * 128 + kk[:, None] - qq[None, :]
        bkt = _t5_bucket_np(rel)
        valid = np.abs(rel) <= 128
        for h in range(8):
            vals = t5_table[:, h][bkt]
            j = h % 4
            biasB[:, h // 4, di, (j % 2) * 2 + j // 2, :] = np.where(valid, vals, np.float32(NEGM))
    biasB = biasB.reshape(128, 2 * 3 * 512)
    m = np.arange(1152) - 512
    rel = kk[:, None] - m[None, :]
    bkt = _t5_bucket_np(rel)
    bandC = np.stack([t5_table[:, 8 + h][bkt] for h in range(4)], axis=1).astype(np.float32)
    bandC = np.ascontiguousarray(bandC).reshape(128, 4 * 1152)
    bl = int(_t5_bucket_np(np.array([-1000]))[0])
    br = int(_t5_bucket_np(np.array([1000]))[0])
    cvec = np.concatenate([t5_table[bl, 8:12], t5_table[br, 8:12]]).astype(np.float32)
    cvecC = np.ascontiguousarray(np.broadcast_to(cvec[None, :], (128, 8)))
    sinkrow = np.ascontiguousarray(np.repeat(sw_sink.astype(np.float32), 128, axis=1))
    return biasA, biasB, bandC, cvecC, sinkrow


_NC_CACHE = {}


def _make_inputs(x_seq, L, ln_in_g, ln_in_b, t5_table, w_in, w_out, na_rpb, sw_sink, diff_lam_q, diff_lam_k,
                 diff_subln_g, ln_mix_g, ln_mix_b, w_ff1, w_ff2, ln_ff_g, ln_ff_b, tables):
    biasA, biasB, bandC, cvecC, sinkrow = tables
    rowsl = [np.stack([ln_in_g, ln_in_b])]
    for l in range(L):
        rowsl.append(np.stack([ln_mix_g[l], ln_mix_b[l]]))
        rowsl.append(np.stack([ln_ff_g[l], ln_ff_b[l]]))
    lnp = np.ascontiguousarray(np.stack(rowsl).astype(np.float32))
    return {
        "x": np.ascontiguousarray(x_seq, dtype=np.float32),
        "w_in": np.ascontiguousarray(w_in[:L], dtype=np.float32),
        "w_out": np.ascontiguousarray(w_out[:L], dtype=np.float32),
        "w_ff1": np.ascontiguousarray(w_ff1[:L], dtype=np.float32),
        "w_ff2": np.ascontiguousarray(w_ff2[:L], dtype=np.float32),
        "lnp": lnp,
        "ident": np.eye(128, dtype=np.float32),
        "biasA": biasA, "biasB": biasB, "bandC": bandC, "cvecC": cvecC, "sinkrow": sinkrow,
        "lamq": np.ascontiguousarray(diff_lam_q[:L].reshape(L, 64), dtype=np.float32),
        "lamk": np.ascontiguousarray(diff_lam_k[:L].reshape(L, 64), dtype=np.float32),
        "subg": np.ascontiguousarray(diff_subln_g[:L], dtype=np.float32),
    }


def kernel(x, ln_in_g, ln_in_b, t5_table, w_in, w_out, na_rpb, sw_sink, diff_lam_q, diff_lam_k,
           diff_subln_g, ln_mix_g, ln_mix_b, w_ff1, w_ff2, ln_ff_g, ln_ff_b):
    args = [np.asarray(a) for a in (ln_in_g, ln_in_b, t5_table, w_in, w_out, na_rpb, sw_sink, diff_lam_q,
                                    diff_lam_k, diff_subln_g, ln_mix_g, ln_mix_b, w_ff1, w_ff2, ln_ff_g, ln_ff_b)]
    x = np.asarray(x)
    B, S, _ = x.shape
    L = args[3].shape[0]
    tables = _host_tables(S, L, args[2], args[5], args[6])
    key = (S, L)
    if key not in _NC_CACHE:
        _NC_CACHE[key] = build_program(S, L)
    nc = _NC_CACHE[key]
    in_maps = [_make_inputs(x[b], L, *args, tables) for b in range(B)]
    res = run_bass_kernel_spmd(nc, in_maps, core_ids=list(range(B)))
    return np.stack([np.asarray(res.results[b]["out"], dtype=np.float32) for b in range(B)], axis=0)
```

```python
import math
import numpy as np
from contextlib import ExitStack
import concourse.bass as bass
import concourse.mybir as mybir
from concourse.bass_utils import run_bass_kernel_spmd

F32 = mybir.dt.float32
BF16 = mybir.dt.bfloat16
U8 = mybir.dt.uint8
ALU = mybir.AluOpType
AF = mybir.ActivationFunctionType
AX = mybir.AxisListType

ENGS = ("pe", "act", "dve", "pool", "sp")


class Tok:
    __slots__ = ("name", "w", "r")

    def __init__(self, name=""):
        self.name = name
        self.w = None
        self.r = []


class Ev:
    __slots__ = ("eng", "is_dma", "sem", "val", "used", "op")

    def __init__(self, eng, is_dma, sem=None, val=None):
        self.eng = eng
        self.is_dma = is_dma
        self.sem = sem
        self.val = val
        self.used = False
        self.op = None


class DSem:
    def __init__(self, prog, name):
        self.name = name
        self.count = 0
        self.h = None
        prog.dsems.append(self)


class Op:
    __slots__ = ("eng", "fn", "deps", "ev", "is_dma", "epoch")

    def __init__(self, eng, fn, ev, is_dma):
        self.eng = eng
        self.fn = fn
        self.deps = []
        self.ev = ev
        self.is_dma = is_dma
        self.epoch = 0


class Prog:
    def __init__(self, nc):
        self.nc = nc
        self.ops = {e: [] for e in ENGS}
        self.dsems = []
        self.esem = {}
        self.n_wait = 0
        self.epoch = 0

    def _deps(self, op, reads, writes):
        ev = op.ev
        deps = op.deps
        for t in reads:
            if t.w is not None and t.w is not ev:
                deps.append(t.w)
        for t in writes:
            w = t.w
            if w is not None and w is not ev and (w.is_dma or ev.is_dma or w.eng != ev.eng):
                deps.append(w)
            for r in t.r:
                if r is not ev and (r.is_dma or ev.is_dma or r.eng != ev.eng):
                    deps.append(r)
        for t in reads:
            if not ev.is_dma:
                t.r = [r for r in t.r if r.is_dma or r.eng != ev.eng]
            t.r.append(ev)
        for t in writes:
            t.w = ev
            t.r = []

    def op(self, eng, fn, reads=(), writes=()):
        ev = Ev(eng, False)
        o = Op(eng, fn, ev, False)
        o.epoch = self.epoch
        ev.op = o
        self._deps(o, reads, writes)
        self.ops[eng].append(o)
        return ev

    def dma(self, q, pairs, sem, reads=(), writes=()):
        ev = Ev(q, True, sem, None)
        first_deps = None
        for pr in pairs:
            out_ap, in_ap = pr[0], pr[1]
            sem.count += 16

            def fn(e, out_ap=out_ap, in_ap=in_ap):
                return e.dma_start(out=out_ap, in_=in_ap)
            o = Op(q, fn, ev, True)
            if first_deps is None:
                self._deps(o, reads, writes)
                first_deps = o.deps
            else:
                o.deps = list(first_deps)
            self.ops[q].append(o)
        ev.val = sem.count
        return ev

    def wait(self, eng, evs):
        ev = Ev(eng, False)
        o = Op(eng, None, ev, False)
        o.deps = [e for e in evs if e is not None]
        self.ops[eng].append(o)
        return ev

    def barrier(self):
        evs = []
        for e in ENGS:
            for o in reversed(self.ops[e]):
                if o.fn is not None and not o.is_dma:
                    evs.append(o.ev)
                    break
        for s in self.dsems:
            if s.count:
                evs.append(Ev(None, True, s, s.count))
        for e in ENGS:
            self.wait(e, evs)
        self.epoch += 1

    def finalize(self, stack):
        nc = self.nc
        for e in ENGS:
            for ep in range(self.epoch + 1):
                if any((not o.is_dma) and o.fn is not None and o.epoch == ep for o in self.ops[e]):
                    self.esem[(e, ep)] = stack.enter_context(nc.semaphore("es_%s%d" % (e, ep)))
        for s in self.dsems:
            s.h = stack.enter_context(nc.semaphore("ds_" + s.name))
        for e in ENGS:
            for o in self.ops[e]:
                for d in o.deps:
                    d.used = True
        for e in ENGS:
            c = {}
            for o in self.ops[e]:
                if not o.is_dma and o.fn is not None and o.ev.used:
                    c[o.epoch] = c.get(o.epoch, 0) + 1
                    o.ev.sem = (e, o.epoch)
                    o.ev.val = c[o.epoch]
                    assert o.ev.val < 30000
        block = stack.enter_context(nc.Block())
        prog = self

        def run(engkey):
            def body(eng):
                waited = {}
                for o in prog.ops[engkey]:
                    need = {}
                    for d in o.deps:
                        if d.is_dma:
                            key = ("d", id(d.sem))
                            h = d.sem.h
                        else:
                            key = ("e", d.sem)
                            h = prog.esem[d.sem]
                        if d.val > waited.get(key, 0) and d.val > need.get(key, (0, None))[0]:
                            need[key] = (d.val, h)
                    for key, (v, h) in need.items():
                        eng.wait_ge(h, v)
                        waited[key] = v
                        prog.n_wait += 1
                    if o.fn is None:
                        continue
                    ins = o.fn(eng)
                    if o.is_dma:
                        ins.then_inc(o.ev.sem.h, 16)
                    elif o.ev.used:
                        ins.then_inc(prog.esem[(engkey, o.epoch)], 1)
            return body

        block.tensor(run("pe"))
        block.scalar(run("act"))
        block.vector(run("dve"))
        block.gpsimd(run("pool"))
        block.sync(run("sp"))


D = 1024
DFF = 4096
NEGM = -10000.0
LN_EPS = 1e-5
VW = 650
NQK = 14


def _a_deltas(i, NT):
    if i == 0:
        return [0, 1, 2, 3]
    if i == NT - 1:
        return [-3, -2, -1, 0]
    return [-2, -1, 0, 1, 2]


class Arena:
    def __init__(self, nc, st, nbytes):
        self.t = st.enter_context(nc.sbuf_tensor("arena", [128, nbytes], U8))
        self.off = 0
        self.cap = nbytes

    def alloc(self, cols, dtype):
        sz = cols * (4 if dtype == F32 else 2)
        sz_al = (sz + 63) // 64 * 64
        assert self.off + sz_al <= self.cap, ("arena overflow", self.off, sz_al, self.cap)
        v = self.t[:, self.off:self.off + sz].bitcast(dtype)
        self.off += sz_al
        return v


def build_program(S, L):
    NT = S // 128
    NG = S // 512
    assert S % 512 == 0 and NT >= 8
    alpha = float((2 * L) ** 0.25)
    nc = bass.Bass("TRN2", target_bir_lowering=False)
    dr = lambda n, s, d, kind=None: (nc.dram_tensor(n, s, d, kind=kind) if kind else nc.dram_tensor(n, s, d)).ap()
    x_d = dr("x", [S, D], F32, "ExternalInput")
    w_in_d = dr("w_in", [L, D, 2304], F32, "ExternalInput")
    w_out_d = dr("w_out", [L, D, D], F32, "ExternalInput")
    w_ff1_d = dr("w_ff1", [L, D, DFF], F32, "ExternalInput")
    w_ff2_d = dr("w_ff2", [L, DFF, D], F32, "ExternalInput")
    lnp_d = dr("lnp", [1 + 2 * L, 2, D], F32, "ExternalInput")
    ident_d = dr("ident", [128, 128], F32, "ExternalInput")
    biasA_d = dr("biasA", [L, 5, 128, 5 * 512], F32, "ExternalInput")
    biasB_d = dr("biasB", [128, 2 * 3 * 512], F32, "ExternalInput")
    bandC_d = dr("bandC", [128, 4 * 1152], F32, "ExternalInput")
    cvecC_d = dr("cvecC", [128, 8], F32, "ExternalInput")
    sinkrow_d = dr("sinkrow", [L, 1024], F32, "ExternalInput")
    lamq_d = dr("lamq", [L, 64], F32, "ExternalInput")
    lamk_d = dr("lamk", [L, 64], F32, "ExternalInput")
    subg_d = dr("subg", [L, 64], F32, "ExternalInput")
    out_d = dr("out", [S, D], F32, "ExternalOutput")
    xres_d = dr("xres", [S, D], F32)
    qkT_d = dr("qkT", [NQK, 128, S], BF16)
    vtok_d = dr("vtok", [S, VW], BF16)
    attT_d = dr("attT", [D, S], BF16)
    x1T_d = dr("x1T", [D, S], BF16)

    st = ExitStack()
    P = Prog(nc)
    A = Arena(nc, st, 206 * 1024)
    pball = st.enter_context(nc.psum_tensor("pball", [128, 8 * 512], F32))
    pb = [pball[:, i * 512:(i + 1) * 512] for i in range(8)]
    ptok = [Tok("pb%d" % i) for i in range(8)]
    nsem = [0]

    free_sems = []
    used_sems = []

    def dsem(name):
        if free_sems:
            d = free_sems.pop()
        else:
            nsem[0] += 1
            d = DSem(P, "s%d" % nsem[0])
        used_sems.append(d)
        return d

    free_psems = []
    used_psems = []

    def psem(name):
        if free_psems:
            d = free_psems.pop()
        else:
            nsem[0] += 1
            d = DSem(P, "p%d" % nsem[0])
        used_psems.append(d)
        return d

    def phase_end():
        P.barrier()
        free_sems.extend(used_sems)
        del used_sems[:]
        free_psems.extend(used_psems)
        del used_psems[:]

    def MM(out, lhsT, rhs, start, stop, r, w, tp=None, skip=False):
        kw = {}
        if tp is not None:
            kw["tile_position"] = tp
        if skip:
            kw["skip_group_check"] = True
        P.op("pe", lambda e: e.matmul(out, lhsT, rhs, start=start, stop=stop, **kw), r, w)

    def ACTF(out, in_, func, r, w, bias=None, scale=None):
        kw = {}
        if bias is not None:
            kw["bias"] = bias
        if scale is not None:
            kw["scale"] = scale
        return P.op("act", lambda e: e.activation(out, in_, func, **kw), r, w)

    def TT(eng, out, in0, in1, op, r, w):
        return P.op(eng, lambda e: e.tensor_tensor(out, in0, in1, op), r, w)

    def STT(eng, out, in0, scalar, in1, op0, op1, r, w):
        return P.op(eng, lambda e: e.scalar_tensor_tensor(out, in0, scalar, in1, op0=op0, op1=op1), r, w)

    def TS(eng, out, in0, s1, s2, op0, op1, r, w):
        if s2 is None:
            return P.op(eng, lambda e: e.tensor_scalar(out, in0, s1, None, op0=op0), r, w)
        return P.op(eng, lambda e: e.tensor_scalar(out, in0, s1, s2, op0=op0, op1=op1), r, w)

    def CP(eng, out, in_, r, w):
        if eng == "act":
            return ACTF(out, in_, AF.Copy, r, w)
        return P.op(eng, lambda e: e.tensor_copy(out, in_), r, w)

    def MEMSET(eng, ap, val, w):
        return P.op(eng, lambda e: e.memset(ap, val), (), w)

    def SPLIT_MM(out_ps, t_out, p0, p1, src, t_src, hi, lo, t_hl):
        CP("dve", hi[p0:p1, :], src[p0:p1, :], [t_src], [t_hl])
        STT("dve", lo[p0:p1, :], hi[p0:p1, :], -1.0, src[p0:p1, :], ALU.mult, ALU.add, [t_src, t_hl], [t_hl])
        MM(out_ps, ones_b[p0:p1, 0:64], hi[p0:p1, :], True, False, [t_ones, t_hl], [t_out])
        MM(out_ps, ones_b[p0:p1, 0:64], lo[p0:p1, :], False, True, [t_ones, t_hl], [t_out])

    idf = A.alloc(128, F32)
    idb = A.alloc(128, BF16)
    ones_f = A.alloc(64, F32)
    ones_b = A.alloc(64, BF16)
    epsc = A.alloc(1, F32)
    t_idf, t_idb, t_ones, t_eps = Tok(), Tok(), Tok(), Tok()
    s_misc = dsem("misc")
    P.dma("sp", [(idf, ident_d)], s_misc, writes=[t_idf])
    CP("dve", idb, idf, [t_idf], [t_idb])
    MEMSET("pool", ones_f, 1.0, [t_ones])
    MEMSET("pool", ones_b, 1.0, [t_ones])
    MEMSET("pool", epsc, LN_EPS, [t_eps])
    lnsm = []
    for i in range(2):
        lnsm.append(dict(stats=A.alloc(12, F32).rearrange("p (c f) -> p c f", c=2), mv=A.alloc(2, F32),
                         sd=A.alloc(1, F32), rstd=A.alloc(1, F32), nb=A.alloc(1, F32),
                         t_stats=Tok(), t_mv=Tok(), t_sd=Tok(), t_rstd=Tok(), t_nb=Tok()))
    mark = A.off
    lncnt = [0]

    def LN(z, t_z, g, b, t_gb, ybf=None, t_ybf=None):
        sm = lnsm[lncnt[0] % 2]
        lncnt[0] += 1
        zv = z.rearrange("p (c f) -> p c f", c=2)
        for c in range(2):
            P.op("dve", lambda e, c=c: e.bn_stats(sm["stats"][:, c, :], zv[:, c, :]), [t_z], [sm["t_stats"]])
        P.op("dve", lambda e: e.bn_aggr(sm["mv"], sm["stats"]), [sm["t_stats"]], [sm["t_mv"]])
        ACTF(sm["sd"], sm["mv"][:, 1:2], AF.Sqrt, [sm["t_mv"], t_eps], [sm["t_sd"]], bias=epsc, scale=1.0)
        P.op("dve", lambda e: e.reciprocal(sm["rstd"], sm["sd"]), [sm["t_sd"]], [sm["t_rstd"]])
        STT("dve", sm["nb"], sm["mv"][:, 0:1], -1.0, sm["rstd"], ALU.mult, ALU.mult,
            [sm["t_mv"], sm["t_rstd"]], [sm["t_nb"]])
        ACTF(z, z, AF.Identity, [t_z, sm["t_nb"], sm["t_rstd"]], [t_z], bias=sm["nb"], scale=sm["rstd"])
        TT("dve", z, z, g, ALU.mult, [t_z, t_gb], [t_z])
        TT("pool", z, z, b, ALU.add, [t_z, t_gb], [t_z])
        if ybf is not None:
            CP("act", ybf, z, [t_z], [t_ybf])

    def load_gb(idx, sem):
        gb = A.alloc(2 * D, F32).rearrange("p (c f) -> p c f", c=2)
        t = Tok()
        P.dma("sp", [(gb[:, 0, :], lnp_d[idx, 0:1, :].to_broadcast([128, D])),
                     (gb[:, 1, :], lnp_d[idx, 1:2, :].to_broadcast([128, D]))], sem, writes=[t])
        return gb, t

    def transposes(src_bf, t_src, dstT, t_dst, col0, bank):
        pT = pb[bank].bitcast(BF16).rearrange("p (k f) -> p k f", k=8)
        for k in range(8):
            P.op("pe", lambda e, k=k: e.transpose(pT[:, k, :], src_bf[:, k * 128:(k + 1) * 128], idb),
                 [t_src, t_idb], [ptok[bank]])
        CP("dve", dstT[:, :, col0:col0 + 128], pT, [ptok[bank]], [t_dst])

    def phase_P1(l):
        A.off = mark
        s_w = psem("w")
        w_sb = A.alloc(8 * 2432, BF16).rearrange("p (k n) -> p k n", k=8)
        t_w = Tok()
        src = w_in_d[l].rearrange("(k p) n -> p k n", p=128)
        cols = [(0, 0, 256), (256, 256, 256), (512, 768, 512), (1024, 1280, 64), (1088, 1280, 64),
                (1152, 1344, 64), (1216, 1344, 64), (1280, 1536, 256), (1536, 1792, 256),
                (1792, 512, 256), (2048, 1408, 128), (2176, 2048, 256)]
        P.dma("pool", [(w_sb[:, :, d0:d0 + w], src[:, :, s0:s0 + w]) for d0, s0, w in cols], s_w, writes=[t_w])
        if l == 0:
            gb, t_gb = load_gb(0, dsem("gb"))
        nx = 3
        xin = [A.alloc(D, F32) for _ in range(nx)]
        t_xin = [Tok() for _ in range(nx)]
        s_xin = [dsem("xin") for _ in range(nx)]
        s_yb = [dsem("yst") for _ in range(nx)]
        ybf = [A.alloc(D, BF16) for _ in range(2)]
        t_ybf = [Tok() for _ in range(2)]
        xT = [A.alloc(8 * 512, BF16).rearrange("p (k n) -> p k n", k=8) for _ in range(2)]
        t_xT = [Tok() for _ in range(2)]
        stg = [A.alloc(NQK * 512, BF16).rearrange("p (t n) -> p t n", t=NQK) for _ in range(2)]
        t_stg = [Tok() for _ in range(2)]
        s_stg = [dsem("stg") for _ in range(2)]
        vstg = [A.alloc(4 * VW, BF16).rearrange("p (t h c) -> p t h c", t=4, h=10) for _ in range(2)]
        t_vstg = [Tok() for _ in range(2)]
        s_vstg = [dsem("vstg") for _ in range(2)]
        for i in range(2):
            MEMSET("pool", vstg[i][:, :, :, 64:65], 1.0, [t_vstg[i]])
        src_d = x_d if l == 0 else xres_d
        qk_view = qkT_d.rearrange("t p s -> p t s")
        v_view = vtok_d.rearrange("(n p) c -> p n c", p=128)

        def load_x(ti):
            sl = ti % nx
            P.dma("sp", [(xin[sl], src_d[ti * 128:(ti + 1) * 128, :])], s_xin[sl], writes=[t_xin[sl]])

        load_x(0)
        load_x(1)
        pcnt = 0
        for g in range(NG):
            gs = g % 2
            for tt in range(4):
                ti = g * 4 + tt
                if ti + 2 < NT:
                    load_x(ti + 2)
                sl = ti % nx
                ys = ti % 2
                if l == 0:
                    LN(xin[sl], t_xin[sl], gb[:, 0, :], gb[:, 1, :], t_gb, ybf[ys], t_ybf[ys])
                    P.dma("sp", [(xres_d[ti * 128:(ti + 1) * 128, :], xin[sl])], s_yb[sl], reads=[t_xin[sl]])
                else:
                    CP("act", ybf[ys], xin[sl], [t_xin[sl]], [t_ybf[ys]])
                transposes(ybf[ys], t_ybf[ys], xT[gs], t_xT[gs], tt * 128, 7)
            for t in range(NQK):
                bk = pcnt % 3
                pcnt += 1
                for k in range(8):
                    MM(pb[bk], w_sb[:, k, t * 128:(t + 1) * 128], xT[gs][:, k, :], k == 0, k == 7,
                       [t_w, t_xT[gs]], [ptok[bk]])
                CP("act" if t % 2 == 0 else "dve", stg[gs][:, t, :], pb[bk], [ptok[bk]], [t_stg[gs]])
            P.dma("sp", [(qk_view[:, :, g * 512:(g + 1) * 512], stg[gs])], s_stg[gs], reads=[t_stg[gs]])
            for tt in range(4):
                ba, bb = (3, 4) if tt % 2 == 0 else (5, 6)
                for k in range(8):
                    MM(pb[ba], xT[gs][:, k, tt * 128:(tt + 1) * 128], w_sb[:, k, 1792:2304], k == 0, k == 7,
                       [t_w, t_xT[gs]], [ptok[ba]])
                for k in range(8):
                    MM(pb[bb][:, 0:128], xT[gs][:, k, tt * 128:(tt + 1) * 128], w_sb[:, k, 2304:2432], k == 0, k == 7,
                       [t_w, t_xT[gs]], [ptok[bb]])
                CP("dve", vstg[gs][:, tt, 0:8, 0:64], pb[ba].rearrange("p (h c) -> p h c", h=8),
                   [ptok[ba]], [t_vstg[gs]])
                CP("act", vstg[gs][:, tt, 8:10, 0:64], pb[bb][:, 0:128].rearrange("p (h c) -> p h c", h=2),
                   [ptok[bb]], [t_vstg[gs]])
            P.dma("sp", [(v_view[:, g * 4:(g + 1) * 4, :], vstg[gs].rearrange("p t h c -> p t (h c)"))],
                  s_vstg[gs], reads=[t_vstg[gs]])
        phase_end()

    def load_big(dst, src, nsplit, sem, tok):
        n = dst.shape[1]
        step = (n + nsplit - 1) // nsplit
        pairs = []
        for a in range(0, n, step):
            b = min(n, a + step)
            pairs.append((dst[:, a:b], src[:, a:b]))
        P.dma("sp", pairs, sem, writes=[tok])

    def phase_AB(l, mixer):
        A.off = mark
        qk_view = qkT_d.rearrange("t p s -> p t s")
        v_view = vtok_d.rearrange("(n p) c -> p n c", p=128)
        if mixer == "A":
            q0, nqt, k0, ngrp, deltas, vcol0, nvh, arow0 = 0, 2, 2, 1, [-2, -1, 0, 1, 2], 0, 4, 0
        else:
            q0, nqt, k0, ngrp, deltas, vcol0, nvh, arow0 = 4, 4, 8, 2, [-1, 0, 1], 260, 2, 256
        nd = len(deltas)
        QT = A.alloc(nqt * S, BF16).rearrange("p (t s) -> p t s", t=nqt)
        KT = A.alloc(2 * S, BF16).rearrange("p (t s) -> p t s", t=2)
        V = A.alloc(NT * nvh * 65, BF16).rearrange("p (n c) -> p n c", n=NT)
        t_Q, t_K, t_V = Tok(), Tok(), Tok()
        P.dma("sp", [(QT[:, t, :], qk_view[:, q0 + t, :]) for t in range(nqt)], dsem("ldq"), writes=[t_Q])
        P.dma("sp", [(KT[:, t, :], qk_view[:, k0 + t, :]) for t in range(2)], dsem("ldk"), writes=[t_K])
        load_big(V, v_view[:, :, vcol0:vcol0 + nvh * 65], max(1, NT // 8), dsem("ldv"), t_V)
        if mixer == "A":
            bias_g = A.alloc(5 * 512, F32).rearrange("p (d n) -> p d n", d=5)
            bias_s2 = [A.alloc(5 * 512, F32).rearrange("p (d n) -> p d n", d=5) for _ in range(2)]
            t_bg = Tok()
            t_bs2 = [Tok(), Tok()]
            s_bg = dsem("bg")
            s_bs2 = [dsem("bs"), dsem("bs")]
            P.dma("sp", [(bias_g, biasA_d[l, 0].rearrange("p (d n) -> p d n", d=5))], s_bg, writes=[t_bg])
            special = {0: 1, 1: 2, NT - 2: 3, NT - 1: 4}
        else:
            bias_b = A.alloc(6 * 512, F32).rearrange("p (g d n) -> p g d n", g=2, d=3)
            t_bb = Tok()
            s_bb = dsem("bb")
            P.dma("sp", [(bias_b, biasB_d.rearrange("p (g d n) -> p g d n", g=2, d=3))], s_bb, writes=[t_bb])
            esrow = A.alloc(1024, F32)
            t_es = Tok()
            P.dma("sp", [(esrow[64:65, :], sinkrow_d[l:l + 1, :])], dsem("es"), writes=[t_es])
            ACTF(esrow[64:65, :], esrow[64:65, :], AF.Exp, [t_es], [t_es])
        tmpb = [A.alloc(512, F32) for _ in range(2)]
        t_tmpb = [Tok() for _ in range(2)]
        PT = [A.alloc(512, BF16) for _ in range(3)]
        t_PT = [Tok() for _ in range(3)]
        osb = [A.alloc(512, F32) for _ in range(2)]
        t_osb = [Tok() for _ in range(2)]
        rr = [A.alloc(512, F32) for _ in range(2)]
        t_rr = [Tok() for _ in range(2)]
        hi = [A.alloc(512, BF16) for _ in range(2)]
        lo = [A.alloc(512, BF16) for _ in range(2)]
        t_hl = [Tok() for _ in range(2)]
        stage = [A.alloc(ngrp * 4 * 512, BF16).rearrange("p (g j n) -> p g j n", g=ngrp, j=4) for _ in range(2)]
        t_stage = [Tok() for _ in range(2)]
        s_stage = [dsem("stage") for _ in range(2)]
        steps = []
        for i in range(NT):
            for grp in range(ngrp):
                dl = _a_deltas(i, NT) if mixer == "A" else deltas
                cands = [d for d in dl if 0 <= i + d < NT]
                for ci, d in enumerate(cands):
                    steps.append(dict(i=i, grp=grp, d=d, di=dl.index(d), c=i + d, first=(ci == 0),
                                      last=(ci == len(cands) - 1), unit=i * ngrp + grp))
        deferred = []

        def emit_qk(idx):
            sp_ = steps[idx]
            i, grp, c = sp_["i"], sp_["grp"], sp_["c"]
            pp = idx % 2
            if mixer == "A" and i in special and sp_["first"] and grp == 0:
                sl_ = special[i] % 2
                P.dma("sp", [(bias_s2[sl_], biasA_d[l, special[i]].rearrange("p (d n) -> p d n", d=5))], s_bs2[sl_],
                      writes=[t_bs2[sl_]])
            for j in range(4):
                h = grp * 4 + j
                pr = (h % 2) * 64
                qt = h // 2
                kt = (j // 2) if mixer == "A" else grp
                a_, b_ = j // 2, j % 2
                MM(pb[2 * pp + b_][:, a_ * 128:(a_ + 1) * 128], KT[pr:pr + 64, kt, c * 128:(c + 1) * 128],
                   QT[pr:pr + 64, qt, i * 128:(i + 1) * 128], True, True, [t_K, t_Q], [ptok[2 * pp + b_]])

        def emit_rest(idx):
            sp_ = steps[idx]
            i, grp, c, di = sp_["i"], sp_["grp"], sp_["c"], sp_["di"]
            pp = idx % 2
            tb = idx % 2
            pt = idx % 3
            ob = 4 + (sp_["unit"] % 2)
            if mixer == "A":
                if i in special:
                    btile, t_b = bias_s2[special[i] % 2][:, di, :], t_bs2[special[i] % 2]
                else:
                    btile, t_b = bias_g[:, di, :], t_bg
            else:
                btile, t_b = bias_b[:, grp, di, :], t_bb
            ps2 = pball[:, 2 * pp * 512:(2 * pp + 2) * 512].rearrange("p (b x) -> p b x", b=2)[:, :, 0:256]
            STT("dve", tmpb[tb].rearrange("p (b x) -> p b x", b=2), ps2, 0.125,
                btile.rearrange("p (b x) -> p b x", b=2),
                ALU.mult, ALU.add, [ptok[2 * pp], ptok[2 * pp + 1], t_b], [t_tmpb[tb]])
            ACTF(PT[pt], tmpb[tb], AF.Exp, [t_tmpb[tb]], [t_PT[pt]])
            for j in range(4):
                vc = (j if mixer == "A" else grp) * 65
                pcol = ((j % 2) * 2 + j // 2) * 128
                MM(pb[ob][0:65, j * 128:(j + 1) * 128], V[:, c, vc:vc + 65], PT[pt][:, pcol:pcol + 128],
                   sp_["first"] and j == 0, sp_["last"], [t_V, t_PT[pt]], [ptok[ob]], skip=True)
            if sp_["last"]:
                post_a(idx, sp_)

        def post_a(idx, sp_):
            i, grp = sp_["i"], sp_["grp"]
            ob = 4 + (sp_["unit"] % 2)
            os_ = sp_["unit"] % 2
            g4 = i // 4
            ss = g4 % 2
            CP("dve", osb[os_][0:65, :], pb[ob][0:65, :], [ptok[ob]], [t_osb[os_]])
            if mixer == "B":
                TT("dve", osb[os_][64:65, :], osb[os_][64:65, :], esrow[64:65, grp * 512:(grp + 1) * 512], ALU.add,
                   [t_osb[os_], t_es], [t_osb[os_]])
            P.op("dve", lambda e, a=rr[os_], b=osb[os_]: e.reciprocal(a[64:65, :], b[64:65, :]),
                 [t_osb[os_]], [t_rr[os_]])
            CP("dve", hi[os_][64:65, :], rr[os_][64:65, :], [t_rr[os_]], [t_hl[os_]])
            STT("dve", lo[os_][64:65, :], hi[os_][64:65, :], -1.0, rr[os_][64:65, :], ALU.mult, ALU.add,
                [t_rr[os_], t_hl[os_]], [t_hl[os_]])

            def part2():
                MM(pb[6][0:64, :], ones_b[64:65, 0:64], hi[os_][64:65, :], True, False, [t_ones, t_hl[os_]], [ptok[6]])
                MM(pb[6][0:64, :], ones_b[64:65, 0:64], lo[os_][64:65, :], False, True, [t_ones, t_hl[os_]], [ptok[6]])
                TT("dve", stage[ss][0:64, grp, :, (i % 4) * 128:(i % 4 + 1) * 128],
                   osb[os_][0:64, :].rearrange("p (j n) -> p j n", j=4),
                   pb[6][0:64, :].rearrange("p (j n) -> p j n", j=4), ALU.mult,
                   [t_osb[os_], ptok[6]], [t_stage[ss]])
                if i % 4 == 3 and grp == ngrp - 1:
                    pairs = []
                    for g_ in range(ngrp):
                        r0 = arow0 + g_ * 256
                        pairs.append((attT_d[r0:r0 + 256, g4 * 512:(g4 + 1) * 512].rearrange("(j p) s -> p j s", p=64),
                                      stage[ss][0:64, g_, :, :]))
                    P.dma("sp", pairs, s_stage[ss], reads=[t_stage[ss]])
            deferred.append((idx + 2, part2))

        LA = 1
        nst = len(steps)
        for idx in range(nst + LA + 3):
            if idx < nst:
                emit_qk(idx)
            j_ = idx - LA
            if 0 <= j_ < nst:
                emit_rest(j_)
            while deferred and deferred[0][0] <= j_:
                deferred.pop(0)[1]()
        assert not deferred
        phase_end()

    def phase_C(l):
        A.off = mark
        lam_init = 0.8 - 0.6 * math.exp(-0.3 * l)
        scale = float(32 ** -0.5)
        qk_view = qkT_d.rearrange("t p s -> p t s")
        v_view = vtok_d.rearrange("(n p) c -> p n c", p=128)
        QT = A.alloc(2 * S, BF16).rearrange("p (t s) -> p t s", t=2)
        KT = A.alloc(2 * S, BF16).rearrange("p (t s) -> p t s", t=2)
        V = A.alloc(NT * 260, BF16).rearrange("p (n c) -> p n c", n=NT)
        t_Q, t_K, t_V = Tok(), Tok(), Tok()
        P.dma("sp", [(KT[:, t, :], qk_view[:, 12 + t, :]) for t in range(2)], dsem("ldk"), writes=[t_K])
        P.dma("sp", [(QT[:, t, :], qk_view[:, 10 + t, :]) for t in range(2)], dsem("ldq"), writes=[t_Q])
        load_big(V, v_view[:, :, 390:650], max(1, NT // 8), dsem("ldv"), t_V)
        band = A.alloc(4 * 1152, F32).rearrange("p (h n) -> p h n", h=4)
        cvec = A.alloc(8, F32)
        t_band, t_cvec = Tok(), Tok()
        P.dma("sp", [(band, bandC_d.rearrange("p (h n) -> p h n", h=4))], dsem("band"), writes=[t_band])
        P.dma("sp", [(cvec, cvecC_d)], dsem("cvec"), writes=[t_cvec])
        lq = A.alloc(64, F32)
        lk = A.alloc(64, F32)
        e2 = A.alloc(2, F32)
        nlam = A.alloc(1, F32)
        gcol = A.alloc(1, F32)
        t_lq, t_lk, t_e2, t_nlam, t_g = Tok(), Tok(), Tok(), Tok(), Tok()
        P.dma("sp", [(lq, lamq_d[l:l + 1, :].to_broadcast([128, 64]))], dsem("lq"), writes=[t_lq])
        P.dma("sp", [(lk, lamk_d[l:l + 1, :].to_broadcast([128, 64]))], dsem("lk"), writes=[t_lk])
        P.dma("sp", [(gcol[0:64, :], subg_d[l].rearrange("(p o) -> p o", o=1))], dsem("g"), writes=[t_g])
        TT("dve", lq, lq, lk, ALU.mult, [t_lq, t_lk], [t_lq])
        P.op("dve", lambda e: e.reduce_sum(e2, lq.rearrange("p (m d) -> p m d", m=2), AX.X), [t_lq], [t_e2])
        ACTF(e2, e2, AF.Exp, [t_e2], [t_e2])
        TT("dve", nlam, e2[:, 1:2], e2[:, 0:1], ALU.subtract, [t_e2], [t_nlam])
        TS("dve", nlam, nlam, -lam_init, None, ALU.add, None, [t_nlam], [t_nlam])
        TS("dve", gcol[0:64, :], gcol[0:64, :], 1.0 - lam_init, None, ALU.mult, None, [t_g], [t_g])

        tmpb = [A.alloc(512, F32) for _ in range(2)]
        t_tmpb = [Tok() for _ in range(2)]
        PT = [A.alloc(512, BF16) for _ in range(4)]
        t_PT = [Tok() for _ in range(4)]
        osb = [A.alloc(512, F32) for _ in range(2)]
        t_osb = [Tok() for _ in range(2)]
        rr = [A.alloc(512, F32) for _ in range(2)]
        t_rr = [Tok() for _ in range(2)]
        av = [A.alloc(512, F32) for _ in range(2)]
        t_av = [Tok() for _ in range(2)]
        ov = A.alloc(512, F32)
        sq = A.alloc(512, F32)
        sdv = A.alloc(512, F32)
        t_ov, t_sq, t_sdv = Tok(), Tok(), Tok()
        hi = [A.alloc(512, BF16) for _ in range(2)]
        lo = [A.alloc(512, BF16) for _ in range(2)]
        t_hl = [Tok() for _ in range(2)]
        stage = [A.alloc(4 * 512, BF16).rearrange("p (j n) -> p j n", j=4) for _ in range(2)]
        t_stage = [Tok() for _ in range(2)]
        s_stage = [dsem("stc") for _ in range(2)]
        its = [(G, h, m, c) for G in range(NG) for h in range(4) for m in range(2) for c in range(NT)]
        nit = len(its)
        deferred = []
        zi = [0]

        def emit_qk(idx):
            G, h, m, c = its[idx]
            tile = h // 2
            pr = ((h % 2) * 2 + m) * 32
            tp = (96, 0) if pr == 96 else None
            bk = idx % 3
            MM(pb[bk], KT[pr:pr + 32, tile, c * 128:(c + 1) * 128], QT[pr:pr + 32, tile, G * 512:(G + 1) * 512],
               True, True, [t_K, t_Q], [ptok[bk]], tp=tp)

        def emit_rest(idx):
            G, h, m, c = its[idx]
            bk = idx % 3
            pt = idx % 4
            hc = G * 4 + h
            ob = 3 + 2 * (hc % 2) + m
            o = c - 4 * G
            if -1 <= o <= 4:
                tb = zi[0] % 2
                zi[0] += 1
                STT("dve", tmpb[tb], pb[bk], scale, band[:, h, 512 - 128 * o:1024 - 128 * o], ALU.mult, ALU.add,
                    [ptok[bk], t_band], [t_tmpb[tb]])
                ACTF(PT[pt], tmpb[tb], AF.Exp, [t_tmpb[tb]], [t_PT[pt]])
            else:
                side = 0 if o < 0 else 1
                ACTF(PT[pt], pb[bk], AF.Exp, [ptok[bk], t_cvec], [t_PT[pt]],
                     bias=cvec[:, side * 4 + h:side * 4 + h + 1], scale=scale)
            MM(pb[ob][0:65, :], V[:, c, h * 65:(h + 1) * 65], PT[pt], c == 0, c == NT - 1,
               [t_V, t_PT[pt]], [ptok[ob]])
            if c == NT - 1 and m == 1:
                post_c(idx, G, h)

        def post_c(idx, G, h):
            hc = G * 4 + h
            ss = G % 2
            for m in range(2):
                ob = 3 + 2 * (hc % 2) + m
                CP("dve", osb[m][0:65, :], pb[ob][0:65, :], [ptok[ob]], [t_osb[m]])
                P.op("dve", lambda e, a=rr[m], b=osb[m]: e.reciprocal(a[64:65, :], b[64:65, :]), [t_osb[m]], [t_rr[m]])
                CP("dve", hi[m][64:65, :], rr[m][64:65, :], [t_rr[m]], [t_hl[m]])
                STT("dve", lo[m][64:65, :], hi[m][64:65, :], -1.0, rr[m][64:65, :], ALU.mult, ALU.add,
                    [t_rr[m], t_hl[m]], [t_hl[m]])

            def part2():
                for m in range(2):
                    MM(pb[7][0:64, :], ones_b[64:65, 0:64], hi[m][64:65, :], True, False, [t_ones, t_hl[m]], [ptok[7]])
                    MM(pb[7][0:64, :], ones_b[64:65, 0:64], lo[m][64:65, :], False, True, [t_ones, t_hl[m]], [ptok[7]])
                    TT("dve", av[m][0:64, :], osb[m][0:64, :], pb[7][0:64, :], ALU.mult, [t_osb[m], ptok[7]], [t_av[m]])
                STT("dve", ov[0:64, :], av[1][0:64, :], nlam[0:64, :], av[0][0:64, :], ALU.mult, ALU.add,
                    [t_av[0], t_av[1], t_nlam], [t_ov])
                TT("pool", sq[0:64, :], ov[0:64, :], ov[0:64, :], ALU.mult, [t_ov], [t_sq])
                CP("dve", hi[0][0:64, :], sq[0:64, :], [t_sq], [t_hl[0]])
                STT("dve", lo[0][0:64, :], hi[0][0:64, :], -1.0, sq[0:64, :], ALU.mult, ALU.add, [t_sq, t_hl[0]], [t_hl[0]])

            def part3():
                MM(pb[7][0:64, :], ones_b[0:64, 0:64], hi[0][0:64, :], True, False, [t_ones, t_hl[0]], [ptok[7]])
                MM(pb[7][0:64, :], ones_b[0:64, 0:64], lo[0][0:64, :], False, True, [t_ones, t_hl[0]], [ptok[7]])
                ACTF(sdv[0:64, :], pb[7][0:64, :], AF.Sqrt, [ptok[7], t_eps], [t_sdv], bias=epsc[0:64, :], scale=1.0 / 64.0)
                P.op("dve", lambda e: e.reciprocal(sdv[0:64, :], sdv[0:64, :]), [t_sdv], [t_sdv])
                STT("dve", stage[ss][0:64, h, :], ov[0:64, :], gcol[0:64, :], sdv[0:64, :], ALU.mult, ALU.mult,
                    [t_ov, t_g, t_sdv], [t_stage[ss]])
                if h == 3:
                    P.dma("sp", [(attT_d[768:1024, G * 512:(G + 1) * 512].rearrange("(j p) s -> p j s", p=64),
                                  stage[ss][0:64, :, :])], s_stage[ss], reads=[t_stage[ss]])
            deferred.append((idx + 6, part2))
            deferred.append((idx + 14, part3))

        LA = 2
        for idx in range(nit + LA + 16):
            if idx < nit:
                emit_qk(idx)
            j_ = idx - LA
            if 0 <= j_ < nit:
                emit_rest(j_)
            while deferred and deferred[0][0] <= j_:
                deferred.pop(0)[1]()
        assert not deferred
        phase_end()

    def phase_P3a(l):
        A.off = mark
        s_w = psem("w3")
        wo = A.alloc(8 * D, BF16).rearrange("p (k n) -> p k n", k=8)
        t_wo = Tok()
        P.dma("pool", [(wo, w_out_d[l].rearrange("(k p) n -> p k n", p=128))], s_w, writes=[t_wo])
        gb1, t_gb1 = load_gb(1 + 2 * l, dsem("gb"))
        aT = [A.alloc(8 * 512, BF16).rearrange("p (k n) -> p k n", k=8) for _ in range(2)]
        t_aT = [Tok() for _ in range(2)]
        s_aT = [dsem("aT") for _ in range(2)]
        nx = 3
        xr = [A.alloc(D, F32) for _ in range(nx)]
        t_xr = [Tok() for _ in range(nx)]
        s_xr = [dsem("xr") for _ in range(nx)]
        s_xs = [dsem("xs") for _ in range(nx)]
        x1bf = [A.alloc(D, BF16) for _ in range(2)]
        t_x1bf = [Tok() for _ in range(2)]
        x1T = [A.alloc(8 * 512, BF16).rearrange("p (k n) -> p k n", k=8) for _ in range(2)]
        t_x1T = [Tok() for _ in range(2)]
        s_x1T = [dsem("x1T") for _ in range(2)]
        aview = attT_d.rearrange("(k p) s -> p k s", p=128)
        xview = x1T_d.rearrange("(k p) s -> p k s", p=128)

        def loads(g):
            P.dma("sp", [(aT[g % 2], aview[:, :, g * 512:(g + 1) * 512])], s_aT[g % 2], writes=[t_aT[g % 2]])

        def load_xr(ti):
            sl = ti % nx
            P.dma("sp", [(xr[sl], xres_d[ti * 128:(ti + 1) * 128, :])], s_xr[sl], writes=[t_xr[sl]])

        loads(0)
        load_xr(0)
        load_xr(1)
        for g in range(NG):
            gs = g % 2
            if g + 1 < NG:
                loads(g + 1)
            for tt in range(4):
                ti = g * 4 + tt
                if ti + 2 < NT:
                    load_xr(ti + 2)
                xs = ti % nx
                bs = ti % 2
                for half in range(2):
                    bk = (tt % 2) * 2 + half
                    for k in range(8):
                        MM(pb[bk], aT[gs][:, k, tt * 128:(tt + 1) * 128], wo[:, k, half * 512:(half + 1) * 512],
                           k == 0, k == 7, [t_aT[gs], t_wo], [ptok[bk]])
                    STT("dve", xr[xs][:, half * 512:(half + 1) * 512], xr[xs][:, half * 512:(half + 1) * 512], alpha,
                        pb[bk], ALU.mult, ALU.add, [t_xr[xs], ptok[bk]], [t_xr[xs]])
                LN(xr[xs], t_xr[xs], gb1[:, 0, :], gb1[:, 1, :], t_gb1, x1bf[bs], t_x1bf[bs])
                P.dma("sp", [(xres_d[ti * 128:(ti + 1) * 128, :], xr[xs])], s_xs[xs], reads=[t_xr[xs]])
                transposes(x1bf[bs], t_x1bf[bs], x1T[gs], t_x1T[gs], tt * 128, 7)
            P.dma("sp", [(xview[:, :, g * 512:(g + 1) * 512], x1T[gs])], s_x1T[gs], reads=[t_x1T[gs]])
        phase_end()

    def phase_P3b(l):
        A.off = mark
        last = (l == L - 1)
        s_w = psem("w4")
        w1 = A.alloc(8 * DFF, BF16).rearrange("p (k n) -> p k n", k=8)
        w2 = A.alloc(32 * D, BF16).rearrange("p (k n) -> p k n", k=32)
        t_w1, t_w2 = Tok(), Tok()
        w1src = w_ff1_d[l].rearrange("(k p) n -> p k n", p=128)
        P.dma("pool", [(w1[:, :, a * 1024:(a + 1) * 1024], w1src[:, :, a * 1024:(a + 1) * 1024]) for a in range(4)],
              s_w, writes=[t_w1])
        w2src = w_ff2_d[l].rearrange("(k p) n -> p k n", p=128)
        P.dma("pool", [(w2[:, a * 8:(a + 1) * 8, :], w2src[:, a * 8:(a + 1) * 8, :]) for a in range(4)],
              psem("w2"), writes=[t_w2])
        gb2, t_gb2 = load_gb(2 + 2 * l, dsem("gb"))
        TG = 256
        NGR = S // TG
        x1T = [A.alloc(8 * TG, BF16).rearrange("p (k n) -> p k n", k=8) for _ in range(2)]
        t_x1T = [Tok() for _ in range(2)]
        s_x1T = [dsem("x1Tl") for _ in range(2)]
        nx = 4
        xr = [A.alloc(D, F32) for _ in range(nx)]
        t_xr = [Tok() for _ in range(nx)]
        s_xr = [dsem("xr") for _ in range(nx)]
        s_xs = [dsem("xs") for _ in range(nx)]
        rl = [A.alloc(512, F32) for _ in range(2)]
        t_rl = [Tok() for _ in range(2)]
        hT = [A.alloc(8 * TG, BF16).rearrange("p (f n) -> p f n", f=8) for _ in range(2)]
        t_hT = [Tok() for _ in range(2)]
        xview = x1T_d.rearrange("(k p) s -> p k s", p=128)
        dst_d = out_d if last else xres_d

        def loads(gi):
            sl = gi % 2
            P.dma("sp", [(x1T[sl], xview[:, :, gi * TG:(gi + 1) * TG])], s_x1T[sl], writes=[t_x1T[sl]])
            for tt in range(2):
                ti = gi * 2 + tt
                P.dma("sp", [(xr[ti % nx], xres_d[ti * 128:(ti + 1) * 128, :])], s_xr[ti % nx], writes=[t_xr[ti % nx]])

        loads(0)
        rcnt = [0]

        def ffn1(b):
            gi, fb = divmod(b, 4)
            sl = gi % 2
            hs = b % 2
            for fp in range(4):
                bk = fp
                for f2 in range(2):
                    f = fb * 8 + fp * 2 + f2
                    for k in range(8):
                        MM(pb[bk][:, f2 * 256:(f2 + 1) * 256], w1[:, k, f * 128:(f + 1) * 128], x1T[sl][:, k, :],
                           k == 0, k == 7, [t_w1, t_x1T[sl]], [ptok[bk]])
                rs = rcnt[0] % 2
                rcnt[0] += 1
                ACTF(rl[rs], pb[bk], AF.Relu, [ptok[bk]], [t_rl[rs]])
                TT("pool" if fp % 2 else "dve", hT[hs][:, fp * 2:fp * 2 + 2, :],
                   rl[rs].rearrange("p (f n) -> p f n", f=2), rl[rs].rearrange("p (f n) -> p f n", f=2),
                   ALU.mult, [t_rl[rs]], [t_hT[hs]])

        def ffn2(b):
            gi, fb = divmod(b, 4)
            hs = b % 2
            for tt in range(2):
                for half in range(2):
                    bk = 4 + tt * 2 + half
                    for ft in range(8):
                        f = fb * 8 + ft
                        MM(pb[bk], hT[hs][:, ft, tt * 128:(tt + 1) * 128], w2[:, f, half * 512:(half + 1) * 512],
                           f == 0, f == 31, [t_hT[hs], t_w2], [ptok[bk]])
            if fb == 3:
                for tt in range(2):
                    ti = gi * 2 + tt
                    xs = ti % nx
                    for half in range(2):
                        bk = 4 + tt * 2 + half
                        STT("dve", xr[xs][:, half * 512:(half + 1) * 512], xr[xs][:, half * 512:(half + 1) * 512], alpha,
                            pb[bk], ALU.mult, ALU.add, [t_xr[xs], ptok[bk]], [t_xr[xs]])
                    LN(xr[xs], t_xr[xs], gb2[:, 0, :], gb2[:, 1, :], t_gb2)
                    P.dma("sp", [(dst_d[ti * 128:(ti + 1) * 128, :], xr[xs])], s_xs[xs], reads=[t_xr[xs]])

        NB = NGR * 4
        if NGR > 1:
            loads(1)
        ffn1(0)
        for b in range(NB):
            if b + 1 < NB:
                ffn1(b + 1)
            ffn2(b)
            if (b + 1) % 4 == 0 and (b + 1) // 4 + 1 < NGR:
                loads((b + 1) // 4 + 1)
        phase_end()

    import os as _os
    _ph = _os.environ.get("KDBG_PHASES")
    phases = [("P1", phase_P1), ("A", lambda l: phase_AB(l, "A")), ("B", lambda l: phase_AB(l, "B")),
              ("C", phase_C), ("P3a", phase_P3a), ("P3b", phase_P3b)]
    for l in range(L):
        for nm, fn in phases:
            if _ph is None or nm in _ph.split(","):
                fn(l)
    P.finalize(st)
    st.close()
    return nc


def _t5_bucket_np(rel):
    nb = 16
    max_exact = 8
    rel = np.asarray(rel, dtype=np.int32)
    n = np.abs(rel)
    nf = np.maximum(n, 1).astype(np.float32)
    large = max_exact + (np.log(nf / np.float32(max_exact)) / np.float32(math.log(128 / max_exact))
                         * np.float32(nb - max_exact)).astype(np.int32)
    large = np.minimum(large, nb - 1)
    return np.where(rel > 0, nb, 0) + np.where(n < max_exact, n, large)


def _host_tables(S, L, t5_table, na_rpb, sw_sink):
    NT = S // 128
    rows = S // 64
    kk = np.arange(128)
    qq = np.arange(128)
    biasA = np.full((L, 5, 128, 5, 4, 128), NEGM, np.float32)
    sets = [NT // 2, 0, 1, NT - 2, NT - 1]
    for si, i in enumerate(sets):
        qtok = i * 128 + qq
        qr, qc = qtok // 64, qtok % 64
        rs = np.clip(qr - 4, 0, rows - 8)
        cs = np.clip(qc - 8, 0, 64 - 16)
        for di, d in enumerate(_a_deltas(i, NT)):
            c = i + d
            if c < 0 or c >= NT:
                continue
            ktok = c * 128 + kk
            kr, kc = ktok // 64, ktok % 64
            valid = ((kr[:, None] >= rs[None, :]) & (kr[:, None] < rs[None, :] + 8)
                     & (kc[:, None] >= cs[None, :]) & (kc[:, None] < cs[None, :] + 16))
            dri = np.clip(kr[:, None] - qr[None, :] + 7, 0, 14)
            dci = np.clip(kc[:, None] - qc[None, :], -15, 15) + 15
            for l in range(L):
                for h in range(4):
                    vals = na_rpb[l, h][dri, dci]
                    biasA[l, si, :, di, (h % 2) * 2 + h // 2, :] = np.where(valid, vals, np.float32(NEGM))
    biasA = biasA.reshape(L, 5, 128, 5 * 512)
    biasB = np.full((128, 2, 3, 4, 128), NEGM, np.float32)
    for di, d in enumerate([-1, 0, 1]):
        rel = d * 128 + kk[:, None] - qq[None, :]
        bkt = _t5_bucket_np(rel)
        valid = np.abs(rel) <= 128
        for h in range(8):
            vals = t5_table[:, h][bkt]
            j = h % 4
            biasB[:, h // 4, di, (j % 2) * 2 + j // 2, :] = np.where(valid, vals, np.float32(NEGM))
    biasB = biasB.reshape(128, 2 * 3 * 512)
    m = np.arange(1152) - 512
    rel = kk[:, None] - m[None, :]
    bkt = _t5_bucket_np(rel)
    bandC = np.stack([t5_table[:, 8 + h][bkt] for h in range(4)], axis=1).astype(np.float32)
    bandC = np.ascontiguousarray(bandC).reshape(128, 4 * 1152)
    bl = int(_t5_bucket_np(np.array([-1000]))[0])
    br = int(_t5_bucket_np(np.array([1000]))[0])
    cvec = np.concatenate([t5_table[bl, 8:12], t5_table[br, 8:12]]).astype(np.float32)
    cvecC = np.ascontiguousarray(np.broadcast_to(cvec[None, :], (128, 8)))
    sinkrow = np.ascontiguousarray(np.repeat(sw_sink.astype(np.float32), 128, axis=1))
    return biasA, biasB, bandC, cvecC, sinkrow


_NC_CACHE = {}


def _make_inputs(x_seq, L, ln_in_g, ln_in_b, t5_table, w_in, w_out, na_rpb, sw_sink, diff_lam_q, diff_lam_k,
                 diff_subln_g, ln_mix_g, ln_mix_b, w_ff1, w_ff2, ln_ff_g, ln_ff_b, tables):
    biasA, biasB, bandC, cvecC, sinkrow = tables
    rowsl = [np.stack([ln_in_g, ln_in_b])]
    for l in range(L):
        rowsl.append(np.stack([ln_mix_g[l], ln_mix_b[l]]))
        rowsl.append(np.stack([ln_ff_g[l], ln_ff_b[l]]))
    lnp = np.ascontiguousarray(np.stack(rowsl).astype(np.float32))
    return {
        "x": np.ascontiguousarray(x_seq, dtype=np.float32),
        "w_in": np.ascontiguousarray(w_in[:L], dtype=np.float32),
        "w_out": np.ascontiguousarray(w_out[:L], dtype=np.float32),
        "w_ff1": np.ascontiguousarray(w_ff1[:L], dtype=np.float32),
        "w_ff2": np.ascontiguousarray(w_ff2[:L], dtype=np.float32),
        "lnp": lnp,
        "ident": np.eye(128, dtype=np.float32),
        "biasA": biasA, "biasB": biasB, "bandC": bandC, "cvecC": cvecC, "sinkrow": sinkrow,
        "lamq": np.ascontiguousarray(diff_lam_q[:L].reshape(L, 64), dtype=np.float32),
        "lamk": np.ascontiguousarray(diff_lam_k[:L].reshape(L, 64), dtype=np.float32),
        "subg": np.ascontiguousarray(diff_subln_g[:L], dtype=np.float32),
    }


def kernel(x, ln_in_g, ln_in_b, t5_table, w_in, w_out, na_rpb, sw_sink, diff_lam_q, diff_lam_k,
           diff_subln_g, ln_mix_g, ln_mix_b, w_ff1, w_ff2, ln_ff_g, ln_ff_b):
    args = [np.asarray(a) for a in (ln_in_g, ln_in_b, t5_table, w_in, w_out, na_rpb, sw_sink, diff_lam_q,
                                    diff_lam_k, diff_subln_g, ln_mix_g, ln_mix_b, w_ff1, w_ff2, ln_ff_g, ln_ff_b)]
    x = np.asarray(x)
    B, S, _ = x.shape
    L = args[3].shape[0]
    tables = _host_tables(S, L, args[2], args[5], args[6])
    key = (S, L)
    if key not in _NC_CACHE:
        _NC_CACHE[key] = build_program(S, L)
    nc = _NC_CACHE[key]
    in_maps = [_make_inputs(x[b], L, *args, tables) for b in range(B)]
    res = run_bass_kernel_spmd(nc, in_maps, core_ids=list(range(B)))
    return np.stack([np.asarray(res.results[b]["out"], dtype=np.float32) for b in range(B)], axis=0)
```

```python
import math
import numpy as np
from contextlib import ExitStack
import concourse.bass as bass
import concourse.mybir as mybir
from concourse.bass_utils import run_bass_kernel_spmd

F32 = mybir.dt.float32
BF16 = mybir.dt.bfloat16
U8 = mybir.dt.uint8
ALU = mybir.AluOpType
AF = mybir.ActivationFunctionType
AX = mybir.AxisListType

ENGS = ("pe", "act", "dve", "pool", "sp")


class Tok:
    __slots__ = ("name", "w", "r")

    def __init__(self, name=""):
        self.name = name
        self.w = None
        self.r = []


class Ev:
    __slots__ = ("eng", "is_dma", "sem", "val", "used", "op")

    def __init__(self, eng, is_dma, sem=None, val=None):
        self.eng = eng
        self.is_dma = is_dma
        self.sem = sem
        self.val = val
        self.used = False
        self.op = None


class DSem:
    def __init__(self, prog, name):
        self.name = name
        self.count = 0
        self.h = None
        prog.dsems.append(self)


class Op:
    __slots__ = ("eng", "fn", "deps", "ev", "is_dma", "epoch")

    def __init__(self, eng, fn, ev, is_dma):
        self.eng = eng
        self.fn = fn
        self.deps = []
        self.ev = ev
        self.is_dma = is_dma
        self.epoch = 0


class Prog:
    def __init__(self, nc):
        self.nc = nc
        self.ops = {e: [] for e in ENGS}
        self.dsems = []
        self.esem = {}
        self.n_wait = 0
        self.epoch = 0

    def _deps(self, op, reads, writes):
        ev = op.ev
        deps = op.deps
        for t in reads:
            if t.w is not None and t.w is not ev:
                deps.append(t.w)
        for t in writes:
            w = t.w
            if w is not None and w is not ev and (w.is_dma or ev.is_dma or w.eng != ev.eng):
                deps.append(w)
            for r in t.r:
                if r is not ev and (r.is_dma or ev.is_dma or r.eng != ev.eng):
                    deps.append(r)
        for t in reads:
            if not ev.is_dma:
                t.r = [r for r in t.r if r.is_dma or r.eng != ev.eng]
            t.r.append(ev)
        for t in writes:
            t.w = ev
            t.r = []

    def op(self, eng, fn, reads=(), writes=()):
        ev = Ev(eng, False)
        o = Op(eng, fn, ev, False)
        o.epoch = self.epoch
        ev.op = o
        self._deps(o, reads, writes)
        self.ops[eng].append(o)
        return ev

    def dma(self, q, pairs, sem, reads=(), writes=()):
        ev = Ev(q, True, sem, None)
        first_deps = None
        for pr in pairs:
            out_ap, in_ap = pr[0], pr[1]
            sem.count += 16

            def fn(e, out_ap=out_ap, in_ap=in_ap):
                return e.dma_start(out=out_ap, in_=in_ap)
            o = Op(q, fn, ev, True)
            if first_deps is None:
                self._deps(o, reads, writes)
                first_deps = o.deps
            else:
                o.deps = list(first_deps)
            self.ops[q].append(o)
        ev.val = sem.count
        return ev

    def wait(self, eng, evs):
        ev = Ev(eng, False)
        o = Op(eng, None, ev, False)
        o.deps = [e for e in evs if e is not None]
        self.ops[eng].append(o)
        return ev

    def barrier(self):
        evs = []
        for e in ENGS:
            for o in reversed(self.ops[e]):
                if o.fn is not None and not o.is_dma:
                    evs.append(o.ev)
                    break
        for s in self.dsems:
            if s.count:
                evs.append(Ev(None, True, s, s.count))
        for e in ENGS:
            self.wait(e, evs)
        self.epoch += 1

    def finalize(self, stack):
        nc = self.nc
        for e in ENGS:
            for ep in range(self.epoch + 1):
                if any((not o.is_dma) and o.fn is not None and o.epoch == ep for o in self.ops[e]):
                    self.esem[(e, ep)] = stack.enter_context(nc.semaphore("es_%s%d" % (e, ep)))
        for s in self.dsems:
            s.h = stack.enter_context(nc.semaphore("ds_" + s.name))
        for e in ENGS:
            for o in self.ops[e]:
                for d in o.deps:
                    d.used = True
        for e in ENGS:
            c = {}
            for o in self.ops[e]:
                if not o.is_dma and o.fn is not None and o.ev.used:
                    c[o.epoch] = c.get(o.epoch, 0) + 1
                    o.ev.sem = (e, o.epoch)
                    o.ev.val = c[o.epoch]
                    assert o.ev.val < 30000
        block = stack.enter_context(nc.Block())
        prog = self

        def run(engkey):
            def body(eng):
                waited = {}
                for o in prog.ops[engkey]:
                    need = {}
                    for d in o.deps:
                        if d.is_dma:
                            key = ("d", id(d.sem))
                            h = d.sem.h
                        else:
                            key = ("e", d.sem)
                            h = prog.esem[d.sem]
                        if d.val > waited.get(key, 0) and d.val > need.get(key, (0, None))[0]:
                            need[key] = (d.val, h)
                    for key, (v, h) in need.items():
                        eng.wait_ge(h, v)
                        waited[key] = v
                        prog.n_wait += 1
                    if o.fn is None:
                        continue
                    ins = o.fn(eng)
                    if o.is_dma:
                        ins.then_inc(o.ev.sem.h, 16)
                    elif o.ev.used:
                        ins.then_inc(prog.esem[(engkey, o.epoch)], 1)
            return body

        block.tensor(run("pe"))
        block.scalar(run("act"))
        block.vector(run("dve"))
        block.gpsimd(run("pool"))
        block.sync(run("sp"))


D = 1024
DFF = 4096
NEGM = -10000.0
LN_EPS = 1e-5
VW = 650
NQK = 14


def _a_deltas(i, NT):
    if i == 0:
        return [0, 1, 2, 3]
    if i == NT - 1:
        return [-3, -2, -1, 0]
    return [-2, -1, 0, 1, 2]


class Arena:
    def __init__(self, nc, st, nbytes):
        self.t = st.enter_context(nc.sbuf_tensor("arena", [128, nbytes], U8))
        self.off = 0
        self.cap = nbytes

    def alloc(self, cols, dtype):
        sz = cols * (4 if dtype == F32 else 2)
        sz_al = (sz + 63) // 64 * 64
        assert self.off + sz_al <= self.cap, ("arena overflow", self.off, sz_al, self.cap)
        v = self.t[:, self.off:self.off + sz].bitcast(dtype)
        self.off += sz_al
        return v


def build_program(S, L):
    NT = S // 128
    NG = S // 512
    assert S % 512 == 0 and NT >= 8
    alpha = float((2 * L) ** 0.25)
    nc = bass.Bass("TRN2", target_bir_lowering=False)
    dr = lambda n, s, d, kind=None: (nc.dram_tensor(n, s, d, kind=kind) if kind else nc.dram_tensor(n, s, d)).ap()
    x_d = dr("x", [S, D], F32, "ExternalInput")
    w_in_d = dr("w_in", [L, D, 2304], F32, "ExternalInput")
    w_out_d = dr("w_out", [L, D, D], F32, "ExternalInput")
    w_ff1_d = dr("w_ff1", [L, D, DFF], F32, "ExternalInput")
    w_ff2_d = dr("w_ff2", [L, DFF, D], F32, "ExternalInput")
    lnp_d = dr("lnp", [1 + 2 * L, 2, D], F32, "ExternalInput")
    ident_d = dr("ident", [128, 128], F32, "ExternalInput")
    biasA_d = dr("biasA", [L, 5, 128, 5 * 512], F32, "ExternalInput")
    biasB_d = dr("biasB", [128, 2 * 3 * 512], F32, "ExternalInput")
    bandC_d = dr("bandC", [128, 4 * 1152], F32, "ExternalInput")
    cvecC_d = dr("cvecC", [128, 8], F32, "ExternalInput")
    sinkrow_d = dr("sinkrow", [L, 1024], F32, "ExternalInput")
    lamq_d = dr("lamq", [L, 64], F32, "ExternalInput")
    lamk_d = dr("lamk", [L, 64], F32, "ExternalInput")
    subg_d = dr("subg", [L, 64], F32, "ExternalInput")
    out_d = dr("out", [S, D], F32, "ExternalOutput")
    xres_d = dr("xres", [S, D], F32)
    qkT_d = dr("qkT", [NQK, 128, S], BF16)
    vtok_d = dr("vtok", [S, VW], BF16)
    attT_d = dr("attT", [D, S], BF16)
    x1T_d = dr("x1T", [D, S], BF16)

    st = ExitStack()
    P = Prog(nc)
    A = Arena(nc, st, 206 * 1024)
    pball = st.enter_context(nc.psum_tensor("pball", [128, 8 * 512], F32))
    pb = [pball[:, i * 512:(i + 1) * 512] for i in range(8)]
    ptok = [Tok("pb%d" % i) for i in range(8)]
    nsem = [0]

    free_sems = []
    used_sems = []

    def dsem(name):
        if free_sems:
            d = free_sems.pop()
        else:
            nsem[0] += 1
            d = DSem(P, "s%d" % nsem[0])
        used_sems.append(d)
        return d

    free_psems = []
    used_psems = []

    def psem(name):
        if free_psems:
            d = free_psems.pop()
        else:
            nsem[0] += 1
            d = DSem(P, "p%d" % nsem[0])
        used_psems.append(d)
        return d

    def phase_end():
        P.barrier()
        free_sems.extend(used_sems)
        del used_sems[:]
        free_psems.extend(used_psems)
        del used_psems[:]

    def MM(out, lhsT, rhs, start, stop, r, w, tp=None, skip=False):
        kw = {}
        if tp is not None:
            kw["tile_position"] = tp
        if skip:
            kw["skip_group_check"] = True
        P.op("pe", lambda e: e.matmul(out, lhsT, rhs, start=start, stop=stop, **kw), r, w)

    def ACTF(out, in_, func, r, w, bias=None, scale=None):
        kw = {}
        if bias is not None:
            kw["bias"] = bias
        if scale is not None:
            kw["scale"] = scale
        return P.op("act", lambda e: e.activation(out, in_, func, **kw), r, w)

    def TT(eng, out, in0, in1, op, r, w):
        return P.op(eng, lambda e: e.tensor_tensor(out, in0, in1, op), r, w)

    def STT(eng, out, in0, scalar, in1, op0, op1, r, w):
        return P.op(eng, lambda e: e.scalar_tensor_tensor(out, in0, scalar, in1, op0=op0, op1=op1), r, w)

    def TS(eng, out, in0, s1, s2, op0, op1, r, w):
        if s2 is None:
            return P.op(eng, lambda e: e.tensor_scalar(out, in0, s1, None, op0=op0), r, w)
        return P.op(eng, lambda e: e.tensor_scalar(out, in0, s1, s2, op0=op0, op1=op1), r, w)

    def CP(eng, out, in_, r, w):
        if eng == "act":
            return ACTF(out, in_, AF.Copy, r, w)
        return P.op(eng, lambda e: e.tensor_copy(out, in_), r, w)

    def MEMSET(eng, ap, val, w):
        return P.op(eng, lambda e: e.memset(ap, val), (), w)

    def SPLIT_MM(out_ps, t_out, p0, p1, src, t_src, hi, lo, t_hl):
        CP("dve", hi[p0:p1, :], src[p0:p1, :], [t_src], [t_hl])
        STT("dve", lo[p0:p1, :], hi[p0:p1, :], -1.0, src[p0:p1, :], ALU.mult, ALU.add, [t_src, t_hl], [t_hl])
        MM(out_ps, ones_b[p0:p1, 0:64], hi[p0:p1, :], True, False, [t_ones, t_hl], [t_out])
        MM(out_ps, ones_b[p0:p1, 0:64], lo[p0:p1, :], False, True, [t_ones, t_hl], [t_out])

    idf = A.alloc(128, F32)
    idb = A.alloc(128, BF16)
    ones_f = A.alloc(64, F32)
    ones_b = A.alloc(64, BF16)
    epsc = A.alloc(1, F32)
    t_idf, t_idb, t_ones, t_eps = Tok(), Tok(), Tok(), Tok()
    s_misc = dsem("misc")
    P.dma("sp", [(idf, ident_d)], s_misc, writes=[t_idf])
    CP("dve", idb, idf, [t_idf], [t_idb])
    MEMSET("pool", ones_f, 1.0, [t_ones])
    MEMSET("pool", ones_b, 1.0, [t_ones])
    MEMSET("pool", epsc, LN_EPS, [t_eps])
    lnsm = []
    for i in range(2):
        lnsm.append(dict(stats=A.alloc(12, F32).rearrange("p (c f) -> p c f", c=2), mv=A.alloc(2, F32),
                         sd=A.alloc(1, F32), rstd=A.alloc(1, F32), nb=A.alloc(1, F32),
                         t_stats=Tok(), t_mv=Tok(), t_sd=Tok(), t_rstd=Tok(), t_nb=Tok()))
    mark = A.off
    lncnt = [0]

    def LN(z, t_z, g, b, t_gb, ybf=None, t_ybf=None):
        sm = lnsm[lncnt[0] % 2]
        lncnt[0] += 1
        zv = z.rearrange("p (c f) -> p c f", c=2)
        for c in range(2):
            P.op("dve", lambda e, c=c: e.bn_stats(sm["stats"][:, c, :], zv[:, c, :]), [t_z], [sm["t_stats"]])
        P.op("dve", lambda e: e.bn_aggr(sm["mv"], sm["stats"]), [sm["t_stats"]], [sm["t_mv"]])
        ACTF(sm["sd"], sm["mv"][:, 1:2], AF.Sqrt, [sm["t_mv"], t_eps], [sm["t_sd"]], bias=epsc, scale=1.0)
        P.op("dve", lambda e: e.reciprocal(sm["rstd"], sm["sd"]), [sm["t_sd"]], [sm["t_rstd"]])
        STT("dve", sm["nb"], sm["mv"][:, 0:1], -1.0, sm["rstd"], ALU.mult, ALU.mult,
            [sm["t_mv"], sm["t_rstd"]], [sm["t_nb"]])
        ACTF(z, z, AF.Identity, [t_z, sm["t_nb"], sm["t_rstd"]], [t_z], bias=sm["nb"], scale=sm["rstd"])
        TT("dve", z, z, g, ALU.mult, [t_z, t_gb], [t_z])
        TT("pool", z, z, b, ALU.add, [t_z, t_gb], [t_z])
        if ybf is not None:
            CP("act", ybf, z, [t_z], [t_ybf])

    def load_gb(idx, sem):
        gb = A.alloc(2 * D, F32).rearrange("p (c f) -> p c f", c=2)
        t = Tok()
        P.dma("sp", [(gb[:, 0, :], lnp_d[idx, 0:1, :].to_broadcast([128, D])),
                     (gb[:, 1, :], lnp_d[idx, 1:2, :].to_broadcast([128, D]))], sem, writes=[t])
        return gb, t

    def transposes(src_bf, t_src, dstT, t_dst, col0, bank):
        pT = pb[bank].bitcast(BF16).rearrange("p (k f) -> p k f", k=8)
        for k in range(8):
            P.op("pe", lambda e, k=k: e.transpose(pT[:, k, :], src_bf[:, k * 128:(k + 1) * 128], idb),
                 [t_src, t_idb], [ptok[bank]])
        CP("dve", dstT[:, :, col0:col0 + 128], pT, [ptok[bank]], [t_dst])

    def phase_P1(l):
        A.off = mark
        s_w = psem("w")
        w_sb = A.alloc(8 * 2432, BF16).rearrange("p (k n) -> p k n", k=8)
        t_w = Tok()
        src = w_in_d[l].rearrange("(k p) n -> p k n", p=128)
        cols = [(0, 0, 256), (256, 256, 256), (512, 768, 512), (1024, 1280, 64), (1088, 1280, 64),
                (1152, 1344, 64), (1216, 1344, 64), (1280, 1536, 256), (1536, 1792, 256),
                (1792, 512, 256), (2048, 1408, 128), (2176, 2048, 256)]
        P.dma("pool", [(w_sb[:, :, d0:d0 + w], src[:, :, s0:s0 + w]) for d0, s0, w in cols], s_w, writes=[t_w])
        if l == 0:
            gb, t_gb = load_gb(0, dsem("gb"))
        nx = 3
        xin = [A.alloc(D, F32) for _ in range(nx)]
        t_xin = [Tok() for _ in range(nx)]
        s_xin = [dsem("xin") for _ in range(nx)]
        s_yb = [dsem("yst") for _ in range(nx)]
        ybf = [A.alloc(D, BF16) for _ in range(2)]
        t_ybf = [Tok() for _ in range(2)]
        xT = [A.alloc(8 * 512, BF16).rearrange("p (k n) -> p k n", k=8) for _ in range(2)]
        t_xT = [Tok() for _ in range(2)]
        stg = [A.alloc(NQK * 512, BF16).rearrange("p (t n) -> p t n", t=NQK) for _ in range(2)]
        t_stg = [Tok() for _ in range(2)]
        s_stg = [dsem("stg") for _ in range(2)]
        vstg = [A.alloc(4 * VW, BF16).rearrange("p (t h c) -> p t h c", t=4, h=10) for _ in range(2)]
        t_vstg = [Tok() for _ in range(2)]
        s_vstg = [dsem("vstg") for _ in range(2)]
        for i in range(2):
            MEMSET("pool", vstg[i][:, :, :, 64:65], 1.0, [t_vstg[i]])
        src_d = x_d if l == 0 else xres_d
        qk_view = qkT_d.rearrange("t p s -> p t s")
        v_view = vtok_d.rearrange("(n p) c -> p n c", p=128)

        def load_x(ti):
            sl = ti % nx
            P.dma("sp", [(xin[sl], src_d[ti * 128:(ti + 1) * 128, :])], s_xin[sl], writes=[t_xin[sl]])

        load_x(0)
        load_x(1)
        pcnt = 0
        for g in range(NG):
            gs = g % 2
            for tt in range(4):
                ti = g * 4 + tt
                if ti + 2 < NT:
                    load_x(ti + 2)
                sl = ti % nx
                ys = ti % 2
                if l == 0:
                    LN(xin[sl], t_xin[sl], gb[:, 0, :], gb[:, 1, :], t_gb, ybf[ys], t_ybf[ys])
                    P.dma("sp", [(xres_d[ti * 128:(ti + 1) * 128, :], xin[sl])], s_yb[sl], reads=[t_xin[sl]])
                else:
                    CP("act", ybf[ys], xin[sl], [t_xin[sl]], [t_ybf[ys]])
                transposes(ybf[ys], t_ybf[ys], xT[gs], t_xT[gs], tt * 128, 7)
            for t in range(NQK):
                bk = pcnt % 3
                pcnt += 1
                for k in range(8):
                    MM(pb[bk], w_sb[:, k, t * 128:(t + 1) * 128], xT[gs][:, k, :], k == 0, k == 7,
                       [t_w, t_xT[gs]], [ptok[bk]])
                CP("act" if t % 2 == 0 else "dve", stg[gs][:, t, :], pb[bk], [ptok[bk]], [t_stg[gs]])
            P.dma("sp", [(qk_view[:, :, g * 512:(g + 1) * 512], stg[gs])], s_stg[gs], reads=[t_stg[gs]])
            for tt in range(4):
                ba, bb = (3, 4) if tt % 2 == 0 else (5, 6)
                for k in range(8):
                    MM(pb[ba], xT[gs][:, k, tt * 128:(tt + 1) * 128], w_sb[:, k, 1792:2304], k == 0, k == 7,
                       [t_w, t_xT[gs]], [ptok[ba]])
                for k in range(8):
                    MM(pb[bb][:, 0:128], xT[gs][:, k, tt * 128:(tt + 1) * 128], w_sb[:, k, 2304:2432], k == 0, k == 7,
                       [t_w, t_xT[gs]], [ptok[bb]])
                CP("dve", vstg[gs][:, tt, 0:8, 0:64], pb[ba].rearrange("p (h c) -> p h c", h=8),
                   [ptok[ba]], [t_vstg[gs]])
                CP("act", vstg[gs][:, tt, 8:10, 0:64], pb[bb][:, 0:128].rearrange("p (h c) -> p h c", h=2),
                   [ptok[bb]], [t_vstg[gs]])
            P.dma("sp", [(v_view[:, g * 4:(g + 1) * 4, :], vstg[gs].rearrange("p t h c -> p t (h c)"))],
                  s_vstg[gs], reads=[t_vstg[gs]])
        phase_end()

    def load_big(dst, src, nsplit, sem, tok):
        n = dst.shape[1]
        step = (n + nsplit - 1) // nsplit
        pairs = []
        for a in range(0, n, step):
            b = min(n, a + step)
            pairs.append((dst[:, a:b], src[:, a:b]))
        P.dma("sp", pairs, sem, writes=[tok])

    def phase_AB(l, mixer):
        A.off = mark
        qk_view = qkT_d.rearrange("t p s -> p t s")
        v_view = vtok_d.rearrange("(n p) c -> p n c", p=128)
        if mixer == "A":
            q0, nqt, k0, ngrp, deltas, vcol0, nvh, arow0 = 0, 2, 2, 1, [-2, -1, 0, 1, 2], 0, 4, 0
        else:
            q0, nqt, k0, ngrp, deltas, vcol0, nvh, arow0 = 4, 4, 8, 2, [-1, 0, 1], 260, 2, 256
        nd = len(deltas)
        QT = A.alloc(nqt * S, BF16).rearrange("p (t s) -> p t s", t=nqt)
        KT = A.alloc(2 * S, BF16).rearrange("p (t s) -> p t s", t=2)
        V = A.alloc(NT * nvh * 65, BF16).rearrange("p (n c) -> p n c", n=NT)
        t_Q, t_K, t_V = Tok(), Tok(), Tok()
        P.dma("sp", [(QT[:, t, :], qk_view[:, q0 + t, :]) for t in range(nqt)], dsem("ldq"), writes=[t_Q])
        P.dma("sp", [(KT[:, t, :], qk_view[:, k0 + t, :]) for t in range(2)], dsem("ldk"), writes=[t_K])
        load_big(V, v_view[:, :, vcol0:vcol0 + nvh * 65], max(1, NT // 8), dsem("ldv"), t_V)
        if mixer == "A":
            bias_g = A.alloc(5 * 512, F32).rearrange("p (d n) -> p d n", d=5)
            bias_s2 = [A.alloc(5 * 512, F32).rearrange("p (d n) -> p d n", d=5) for _ in range(2)]
            t_bg = Tok()
            t_bs2 = [Tok(), Tok()]
            s_bg = dsem("bg")
            s_bs2 = [dsem("bs"), dsem("bs")]
            P.dma("sp", [(bias_g, biasA_d[l, 0].rearrange("p (d n) -> p d n", d=5))], s_bg, writes=[t_bg])
            special = {0: 1, 1: 2, NT - 2: 3, NT - 1: 4}
        else:
            bias_b = A.alloc(6 * 512, F32).rearrange("p (g d n) -> p g d n", g=2, d=3)
            t_bb = Tok()
            s_bb = dsem("bb")
            P.dma("sp", [(bias_b, biasB_d.rearrange("p (g d n) -> p g d n", g=2, d=3))], s_bb, writes=[t_bb])
            esrow = A.alloc(1024, F32)
            t_es = Tok()
            P.dma("sp", [(esrow[64:65, :], sinkrow_d[l:l + 1, :])], dsem("es"), writes=[t_es])
            ACTF(esrow[64:65, :], esrow[64:65, :], AF.Exp, [t_es], [t_es])
        tmpb = [A.alloc(512, F32) for _ in range(2)]
        t_tmpb = [Tok() for _ in range(2)]
        PT = [A.alloc(512, BF16) for _ in range(3)]
        t_PT = [Tok() for _ in range(3)]
        osb = [A.alloc(512, F32) for _ in range(2)]
        t_osb = [Tok() for _ in range(2)]
        rr = [A.alloc(512, F32) for _ in range(2)]
        t_rr = [Tok() for _ in range(2)]
        hi = [A.alloc(512, BF16) for _ in range(2)]
        lo = [A.alloc(512, BF16) for _ in range(2)]
        t_hl = [Tok() for _ in range(2)]
        stage = [A.alloc(ngrp * 4 * 512, BF16).rearrange("p (g j n) -> p g j n", g=ngrp, j=4) for _ in range(2)]
        t_stage = [Tok() for _ in range(2)]
        s_stage = [dsem("stage") for _ in range(2)]
        steps = []
        for i in range(NT):
            for grp in range(ngrp):
                dl = _a_deltas(i, NT) if mixer == "A" else deltas
                cands = [d for d in dl if 0 <= i + d < NT]
                for ci, d in enumerate(cands):
                    steps.append(dict(i=i, grp=grp, d=d, di=dl.index(d), c=i + d, first=(ci == 0),
                                      last=(ci == len(cands) - 1), unit=i * ngrp + grp))
        deferred = []

        def emit_qk(idx):
            sp_ = steps[idx]
            i, grp, c = sp_["i"], sp_["grp"], sp_["c"]
            pp = idx % 2
            if mixer == "A" and i in special and sp_["first"] and grp == 0:
                sl_ = special[i] % 2
                P.dma("sp", [(bias_s2[sl_], biasA_d[l, special[i]].rearrange("p (d n) -> p d n", d=5))], s_bs2[sl_],
                      writes=[t_bs2[sl_]])
            for j in range(4):
                h = grp * 4 + j
                pr = (h % 2) * 64
                qt = h // 2
                kt = (j // 2) if mixer == "A" else grp
                a_, b_ = j // 2, j % 2
                MM(pb[2 * pp + b_][:, a_ * 128:(a_ + 1) * 128], KT[pr:pr + 64, kt, c * 128:(c + 1) * 128],
                   QT[pr:pr + 64, qt, i * 128:(i + 1) * 128], True, True, [t_K, t_Q], [ptok[2 * pp + b_]])

        def emit_rest(idx):
            sp_ = steps[idx]
            i, grp, c, di = sp_["i"], sp_["grp"], sp_["c"], sp_["di"]
            pp = idx % 2
            tb = idx % 2
            pt = idx % 3
            ob = 4 + (sp_["unit"] % 2)
            if mixer == "A":
                if i in special:
                    btile, t_b = bias_s2[special[i] % 2][:, di, :], t_bs2[special[i] % 2]
                else:
                    btile, t_b = bias_g[:, di, :], t_bg
            else:
                btile, t_b = bias_b[:, grp, di, :], t_bb
            ps2 = pball[:, 2 * pp * 512:(2 * pp + 2) * 512].rearrange("p (b x) -> p b x", b=2)[:, :, 0:256]
            STT("dve", tmpb[tb].rearrange("p (b x) -> p b x", b=2), ps2, 0.125,
                btile.rearrange("p (b x) -> p b x", b=2),
                ALU.mult, ALU.add, [ptok[2 * pp], ptok[2 * pp + 1], t_b], [t_tmpb[tb]])
            ACTF(PT[pt], tmpb[tb], AF.Exp, [t_tmpb[tb]], [t_PT[pt]])
            for j in range(4):
                vc = (j if mixer == "A" else grp) * 65
                pcol = ((j % 2) * 2 + j // 2) * 128
                MM(pb[ob][0:65, j * 128:(j + 1) * 128], V[:, c, vc:vc + 65], PT[pt][:, pcol:pcol + 128],
                   sp_["first"] and j == 0, sp_["last"], [t_V, t_PT[pt]], [ptok[ob]], skip=True)
            if sp_["last"]:
                post_a(idx, sp_)

        def post_a(idx, sp_):
            i, grp = sp_["i"], sp_["grp"]
            ob = 4 + (sp_["unit"] % 2)
            os_ = sp_["unit"] % 2
            g4 = i // 4
            ss = g4 % 2
            CP("dve", osb[os_][0:65, :], pb[ob][0:65, :], [ptok[ob]], [t_osb[os_]])
            if mixer == "B":
                TT("dve", osb[os_][64:65, :], osb[os_][64:65, :], esrow[64:65, grp * 512:(grp + 1) * 512], ALU.add,
                   [t_osb[os_], t_es], [t_osb[os_]])
            P.op("dve", lambda e, a=rr[os_], b=osb[os_]: e.reciprocal(a[64:65, :], b[64:65, :]),
                 [t_osb[os_]], [t_rr[os_]])
            CP("dve", hi[os_][64:65, :], rr[os_][64:65, :], [t_rr[os_]], [t_hl[os_]])
            STT("dve", lo[os_][64:65, :], hi[os_][64:65, :], -1.0, rr[os_][64:65, :], ALU.mult, ALU.add,
                [t_rr[os_], t_hl[os_]], [t_hl[os_]])

            def part2():
                MM(pb[6][0:64, :], ones_b[64:65, 0:64], hi[os_][64:65, :], True, False, [t_ones, t_hl[os_]], [ptok[6]])
                MM(pb[6][0:64, :], ones_b[64:65, 0:64], lo[os_][64:65, :], False, True, [t_ones, t_hl[os_]], [ptok[6]])
                TT("dve", stage[ss][0:64, grp, :, (i % 4) * 128:(i % 4 + 1) * 128],
                   osb[os_][0:64, :].rearrange("p (j n) -> p j n", j=4),
                   pb[6][0:64, :].rearrange("p (j n) -> p j n", j=4), ALU.mult,
                   [t_osb[os_], ptok[6]], [t_stage[ss]])
                if i % 4 == 3 and grp == ngrp - 1:
                    pairs = []
                    for g_ in range(ngrp):
                        r0 = arow0 + g_ * 256
                        pairs.append((attT_d[r0:r0 + 256, g4 * 512:(g4 + 1) * 512].rearrange("(j p) s -> p j s", p=64),
                                      stage[ss][0:64, g_, :, :]))
                    P.dma("sp", pairs, s_stage[ss], reads=[t_stage[ss]])
            deferred.append((idx + 2, part2))

        LA = 1
        nst = len(steps)
        for idx in range(nst + LA + 3):
            if idx < nst:
                emit_qk(idx)
            j_ = idx - LA
            if 0 <= j_ < nst:
                emit_rest(j_)
            while deferred and deferred[0][0] <= j_:
                deferred.pop(0)[1]()
        assert not deferred
        phase_end()

    def phase_C(l):
        A.off = mark
        lam_init = 0.8 - 0.6 * math.exp(-0.3 * l)
        scale = float(32 ** -0.5)
        qk_view = qkT_d.rearrange("t p s -> p t s")
        v_view = vtok_d.rearrange("(n p) c -> p n c", p=128)
        QT = A.alloc(2 * S, BF16).rearrange("p (t s) -> p t s", t=2)
        KT = A.alloc(2 * S, BF16).rearrange("p (t s) -> p t s", t=2)
        V = A.alloc(NT * 260, BF16).rearrange("p (n c) -> p n c", n=NT)
        t_Q, t_K, t_V = Tok(), Tok(), Tok()
        P.dma("sp", [(KT[:, t, :], qk_view[:, 12 + t, :]) for t in range(2)], dsem("ldk"), writes=[t_K])
        P.dma("sp", [(QT[:, t, :], qk_view[:, 10 + t, :]) for t in range(2)], dsem("ldq"), writes=[t_Q])
        load_big(V, v_view[:, :, 390:650], max(1, NT // 8), dsem("ldv"), t_V)
        band = A.alloc(4 * 1152, F32).rearrange("p (h n) -> p h n", h=4)
        cvec = A.alloc(8, F32)
        t_band, t_cvec = Tok(), Tok()
        P.dma("sp", [(band, bandC_d.rearrange("p (h n) -> p h n", h=4))], dsem("band"), writes=[t_band])
        P.dma("sp", [(cvec, cvecC_d)], dsem("cvec"), writes=[t_cvec])
        lq = A.alloc(64, F32)
        lk = A.alloc(64, F32)
        e2 = A.alloc(2, F32)
        nlam = A.alloc(1, F32)
        gcol = A.alloc(1, F32)
        t_lq, t_lk, t_e2, t_nlam, t_g = Tok(), Tok(), Tok(), Tok(), Tok()
        P.dma("sp", [(lq, lamq_d[l:l + 1, :].to_broadcast([128, 64]))], dsem("lq"), writes=[t_lq])
        P.dma("sp", [(lk, lamk_d[l:l + 1, :].to_broadcast([128, 64]))], dsem("lk"), writes=[t_lk])
        P.dma("sp", [(gcol[0:64, :], subg_d[l].rearrange("(p o) -> p o", o=1))], dsem("g"), writes=[t_g])
        TT("dve", lq, lq, lk, ALU.mult, [t_lq, t_lk], [t_lq])
        P.op("dve", lambda e: e.reduce_sum(e2, lq.rearrange("p (m d) -> p m d", m=2), AX.X), [t_lq], [t_e2])
        ACTF(e2, e2, AF.Exp, [t_e2], [t_e2])
        TT("dve", nlam, e2[:, 1:2], e2[:, 0:1], ALU.subtract, [t_e2], [t_nlam])
        TS("dve", nlam, nlam, -lam_init, None, ALU.add, None, [t_nlam], [t_nlam])
        TS("dve", gcol[0:64, :], gcol[0:64, :], 1.0 - lam_init, None, ALU.mult, None, [t_g], [t_g])

        tmpb = [A.alloc(512, F32) for _ in range(2)]
        t_tmpb = [Tok() for _ in range(2)]
        PT = [A.alloc(512, BF16) for _ in range(4)]
        t_PT = [Tok() for _ in range(4)]
        osb = [A.alloc(512, F32) for _ in range(2)]
        t_osb = [Tok() for _ in range(2)]
        rr = [A.alloc(512, F32) for _ in range(2)]
        t_rr = [Tok() for _ in range(2)]
        av = [A.alloc(512, F32) for _ in range(2)]
        t_av = [Tok() for _ in range(2)]
        ov = A.alloc(512, F32)
        sq = A.alloc(512, F32)
        sdv = A.alloc(512, F32)
        t_ov, t_sq, t_sdv = Tok(), Tok(), Tok()
        hi = [A.alloc(512, BF16) for _ in range(2)]
        lo = [A.alloc(512, BF16) for _ in range(2)]
        t_hl = [Tok() for _ in range(2)]
        stage = [A.alloc(4 * 512, BF16).rearrange("p (j n) -> p j n", j=4) for _ in range(2)]
        t_stage = [Tok() for _ in range(2)]
        s_stage = [dsem("stc") for _ in range(2)]
        its = [(G, h, c) for G in range(NG) for h in range(4) for c in range(NT)]
        nit = len(its)
        deferred = []
        zi = [0]
        PT2 = [A.alloc(1024, BF16) for _ in range(3)]
        t_PT2 = [Tok() for _ in range(3)]
        tmp2 = [A.alloc(1024, F32) for _ in range(2)]
        t_tmp2 = [Tok() for _ in range(2)]

        def emit_qk(idx):
            G, h, c = its[idx]
            tile = h // 2
            for m in range(2):
                pr = ((h % 2) * 2 + m) * 32
                tp = (96, 0) if pr == 96 else None
                bk = 2 * (idx % 2) + m
                MM(pb[bk], KT[pr:pr + 32, tile, c * 128:(c + 1) * 128], QT[pr:pr + 32, tile, G * 512:(G + 1) * 512],
                   True, True, [t_K, t_Q], [ptok[bk]], tp=tp)

        def emit_rest(idx):
            G, h, c = its[idx]
            b0 = 2 * (idx % 2)
            ps2 = pball[:, b0 * 512:(b0 + 2) * 512]
            pt = idx % 3
            o = c - 4 * G
            if -1 <= o <= 4:
                tb = zi[0] % 2
                zi[0] += 1
                for m in range(2):
                    STT("dve", tmp2[tb][:, m * 512:(m + 1) * 512], pb[b0 + m], scale,
                        band[:, h, 512 - 128 * o:1024 - 128 * o], ALU.mult, ALU.add,
                        [ptok[b0 + m], t_band], [t_tmp2[tb]])
                ACTF(PT2[pt], tmp2[tb], AF.Exp, [t_tmp2[tb]], [t_PT2[pt]])
            else:
                side = 0 if o < 0 else 1
                ACTF(PT2[pt], ps2, AF.Exp, [ptok[b0], ptok[b0 + 1], t_cvec], [t_PT2[pt]],
                     bias=cvec[:, side * 4 + h:side * 4 + h + 1], scale=scale)
            for m in range(2):
                MM(pb[4 + m][0:65, :], V[:, c, h * 65:(h + 1) * 65], PT2[pt][:, m * 512:(m + 1) * 512],
                   c == 0, c == NT - 1, [t_V, t_PT2[pt]], [ptok[4 + m]])
            if c == NT - 1:
                post_c(idx, G, h)

        def post_c(idx, G, h):
            ss = G % 2
            for m in range(2):
                ob = 4 + m
                CP("dve", osb[m][0:65, :], pb[ob][0:65, :], [ptok[ob]], [t_osb[m]])
                P.op("dve", lambda e, a=rr[m], b=osb[m]: e.reciprocal(a[64:65, :], b[64:65, :]), [t_osb[m]], [t_rr[m]])
                CP("dve", hi[m][64:65, :], rr[m][64:65, :], [t_rr[m]], [t_hl[m]])
                STT("dve", lo[m][64:65, :], hi[m][64:65, :], -1.0, rr[m][64:65, :], ALU.mult, ALU.add,
                    [t_rr[m], t_hl[m]], [t_hl[m]])

            def part2():
                for m in range(2):
                    MM(pb[6][0:64, :], ones_b[64:65, 0:64], hi[m][64:65, :], True, False, [t_ones, t_hl[m]], [ptok[6]])
                    MM(pb[6][0:64, :], ones_b[64:65, 0:64], lo[m][64:65, :], False, True, [t_ones, t_hl[m]], [ptok[6]])
                    TT("dve", av[m][0:64, :], osb[m][0:64, :], pb[6][0:64, :], ALU.mult, [t_osb[m], ptok[6]], [t_av[m]])
                STT("dve", ov[0:64, :], av[1][0:64, :], nlam[0:64, :], av[0][0:64, :], ALU.mult, ALU.add,
                    [t_av[0], t_av[1], t_nlam], [t_ov])
                TT("pool", sq[0:64, :], ov[0:64, :], ov[0:64, :], ALU.mult, [t_ov], [t_sq])
                CP("dve", hi[0][0:64, :], sq[0:64, :], [t_sq], [t_hl[0]])
                STT("dve", lo[0][0:64, :], hi[0][0:64, :], -1.0, sq[0:64, :], ALU.mult, ALU.add, [t_sq, t_hl[0]], [t_hl[0]])

            def part3():
                MM(pb[6][0:64, :], ones_b[0:64, 0:64], hi[0][0:64, :], True, False, [t_ones, t_hl[0]], [ptok[6]])
                MM(pb[6][0:64, :], ones_b[0:64, 0:64], lo[0][0:64, :], False, True, [t_ones, t_hl[0]], [ptok[6]])
                ACTF(sdv[0:64, :], pb[6][0:64, :], AF.Sqrt, [ptok[6], t_eps], [t_sdv], bias=epsc[0:64, :], scale=1.0 / 64.0)
                P.op("dve", lambda e: e.reciprocal(sdv[0:64, :], sdv[0:64, :]), [t_sdv], [t_sdv])
                STT("dve", stage[ss][0:64, h, :], ov[0:64, :], gcol[0:64, :], sdv[0:64, :], ALU.mult, ALU.mult,
                    [t_ov, t_g, t_sdv], [t_stage[ss]])
                if h == 3:
                    P.dma("sp", [(attT_d[768:1024, G * 512:(G + 1) * 512].rearrange("(j p) s -> p j s", p=64),
                                  stage[ss][0:64, :, :])], s_stage[ss], reads=[t_stage[ss]])
            deferred.append((idx + 4, part2))
            deferred.append((idx + 9, part3))

        LA = 1
        for idx in range(nit + LA + 12):
            if idx < nit:
                emit_qk(idx)
            j_ = idx - LA
            if 0 <= j_ < nit:
                emit_rest(j_)
            while deferred and deferred[0][0] <= j_:
                deferred.pop(0)[1]()
        assert not deferred
        phase_end()

    def phase_P3a(l):
        A.off = mark
        s_w = psem("w3")
        wo = A.alloc(8 * D, BF16).rearrange("p (k n) -> p k n", k=8)
        t_wo = Tok()
        P.dma("pool", [(wo, w_out_d[l].rearrange("(k p) n -> p k n", p=128))], s_w, writes=[t_wo])
        gb1, t_gb1 = load_gb(1 + 2 * l, dsem("gb"))
        aT = [A.alloc(8 * 512, BF16).rearrange("p (k n) -> p k n", k=8) for _ in range(2)]
        t_aT = [Tok() for _ in range(2)]
        s_aT = [dsem("aT") for _ in range(2)]
        nx = 3
        xr = [A.alloc(D, F32) for _ in range(nx)]
        t_xr = [Tok() for _ in range(nx)]
        s_xr = [dsem("xr") for _ in range(nx)]
        s_xs = [dsem("xs") for _ in range(nx)]
        x1bf = [A.alloc(D, BF16) for _ in range(2)]
        t_x1bf = [Tok() for _ in range(2)]
        x1T = [A.alloc(8 * 512, BF16).rearrange("p (k n) -> p k n", k=8) for _ in range(2)]
        t_x1T = [Tok() for _ in range(2)]
        s_x1T = [dsem("x1T") for _ in range(2)]
        aview = attT_d.rearrange("(k p) s -> p k s", p=128)
        xview = x1T_d.rearrange("(k p) s -> p k s", p=128)

        def loads(g):
            P.dma("sp", [(aT[g % 2], aview[:, :, g * 512:(g + 1) * 512])], s_aT[g % 2], writes=[t_aT[g % 2]])

        def load_xr(ti):
            sl = ti % nx
            P.dma("sp", [(xr[sl], xres_d[ti * 128:(ti + 1) * 128, :])], s_xr[sl], writes=[t_xr[sl]])

        loads(0)
        load_xr(0)
        load_xr(1)
        for g in range(NG):
            gs = g % 2
            if g + 1 < NG:
                loads(g + 1)
            for tt in range(4):
                ti = g * 4 + tt
                if ti + 2 < NT:
                    load_xr(ti + 2)
                xs = ti % nx
                bs = ti % 2
                for half in range(2):
                    bk = (tt % 2) * 2 + half
                    for k in range(8):
                        MM(pb[bk], aT[gs][:, k, tt * 128:(tt + 1) * 128], wo[:, k, half * 512:(half + 1) * 512],
                           k == 0, k == 7, [t_aT[gs], t_wo], [ptok[bk]])
                    STT("dve", xr[xs][:, half * 512:(half + 1) * 512], xr[xs][:, half * 512:(half + 1) * 512], alpha,
                        pb[bk], ALU.mult, ALU.add, [t_xr[xs], ptok[bk]], [t_xr[xs]])
                LN(xr[xs], t_xr[xs], gb1[:, 0, :], gb1[:, 1, :], t_gb1, x1bf[bs], t_x1bf[bs])
                P.dma("sp", [(xres_d[ti * 128:(ti + 1) * 128, :], xr[xs])], s_xs[xs], reads=[t_xr[xs]])
                transposes(x1bf[bs], t_x1bf[bs], x1T[gs], t_x1T[gs], tt * 128, 7)
            P.dma("sp", [(xview[:, :, g * 512:(g + 1) * 512], x1T[gs])], s_x1T[gs], reads=[t_x1T[gs]])
        phase_end()

    def phase_P3b(l):
        A.off = mark
        last = (l == L - 1)
        s_w = psem("w4")
        w1 = A.alloc(8 * DFF, BF16).rearrange("p (k n) -> p k n", k=8)
        w2 = A.alloc(32 * D, BF16).rearrange("p (k n) -> p k n", k=32)
        t_w1, t_w2 = Tok(), Tok()
        w1src = w_ff1_d[l].rearrange("(k p) n -> p k n", p=128)
        P.dma("pool", [(w1[:, :, a * 1024:(a + 1) * 1024], w1src[:, :, a * 1024:(a + 1) * 1024]) for a in range(4)],
              s_w, writes=[t_w1])
        w2src = w_ff2_d[l].rearrange("(k p) n -> p k n", p=128)
        P.dma("pool", [(w2[:, a * 8:(a + 1) * 8, :], w2src[:, a * 8:(a + 1) * 8, :]) for a in range(4)],
              psem("w2"), writes=[t_w2])
        gb2, t_gb2 = load_gb(2 + 2 * l, dsem("gb"))
        TG = 256
        NGR = S // TG
        x1T = [A.alloc(8 * TG, BF16).rearrange("p (k n) -> p k n", k=8) for _ in range(2)]
        t_x1T = [Tok() for _ in range(2)]
        s_x1T = [dsem("x1Tl") for _ in range(2)]
        nx = 4
        xr = [A.alloc(D, F32) for _ in range(nx)]
        t_xr = [Tok() for _ in range(nx)]
        s_xr = [dsem("xr") for _ in range(nx)]
        s_xs = [dsem("xs") for _ in range(nx)]
        rl = [A.alloc(512, F32) for _ in range(2)]
        t_rl = [Tok() for _ in range(2)]
        hT = [A.alloc(8 * TG, BF16).rearrange("p (f n) -> p f n", f=8) for _ in range(2)]
        t_hT = [Tok() for _ in range(2)]
        xview = x1T_d.rearrange("(k p) s -> p k s", p=128)
        dst_d = out_d if last else xres_d

        def loads(gi):
            sl = gi % 2
            P.dma("sp", [(x1T[sl], xview[:, :, gi * TG:(gi + 1) * TG])], s_x1T[sl], writes=[t_x1T[sl]])
            for tt in range(2):
                ti = gi * 2 + tt
                P.dma("sp", [(xr[ti % nx], xres_d[ti * 128:(ti + 1) * 128, :])], s_xr[ti % nx], writes=[t_xr[ti % nx]])

        loads(0)
        rcnt = [0]

        def ffn1(b):
            gi, fb = divmod(b, 4)
            sl = gi % 2
            hs = b % 2
            for fp in range(4):
                bk = fp
                for f2 in range(2):
                    f = fb * 8 + fp * 2 + f2
                    for k in range(8):
                        MM(pb[bk][:, f2 * 256:(f2 + 1) * 256], w1[:, k, f * 128:(f + 1) * 128], x1T[sl][:, k, :],
                           k == 0, k == 7, [t_w1, t_x1T[sl]], [ptok[bk]])
                rs = rcnt[0] % 2
                rcnt[0] += 1
                ACTF(rl[rs], pb[bk], AF.Relu, [ptok[bk]], [t_rl[rs]])
                TT("pool" if fp % 2 else "dve", hT[hs][:, fp * 2:fp * 2 + 2, :],
                   rl[rs].rearrange("p (f n) -> p f n", f=2), rl[rs].rearrange("p (f n) -> p f n", f=2),
                   ALU.mult, [t_rl[rs]], [t_hT[hs]])

        def ffn2(b):
            gi, fb = divmod(b, 4)
            hs = b % 2
            for tt in range(2):
                for half in range(2):
                    bk = 4 + tt * 2 + half
                    for ft in range(8):
                        f = fb * 8 + ft
                        MM(pb[bk], hT[hs][:, ft, tt * 128:(tt + 1) * 128], w2[:, f, half * 512:(half + 1) * 512],
                           f == 0, f == 31, [t_hT[hs], t_w2], [ptok[bk]])
            if fb == 3:
                for tt in range(2):
                    ti = gi * 2 + tt
                    xs = ti % nx
                    for half in range(2):
                        bk = 4 + tt * 2 + half
                        STT("dve", xr[xs][:, half * 512:(half + 1) * 512], xr[xs][:, half * 512:(half + 1) * 512], alpha,
                            pb[bk], ALU.mult, ALU.add, [t_xr[xs], ptok[bk]], [t_xr[xs]])
                    LN(xr[xs], t_xr[xs], gb2[:, 0, :], gb2[:, 1, :], t_gb2)
                    P.dma("sp", [(dst_d[ti * 128:(ti + 1) * 128, :], xr[xs])], s_xs[xs], reads=[t_xr[xs]])

        NB = NGR * 4
        if NGR > 1:
            loads(1)
        ffn1(0)
        for b in range(NB):
            if b + 1 < NB:
                ffn1(b + 1)
            ffn2(b)
            if (b + 1) % 4 == 0 and (b + 1) // 4 + 1 < NGR:
                loads((b + 1) // 4 + 1)
        phase_end()

    import os as _os
    _ph = _os.environ.get("KDBG_PHASES")
    phases = [("P1", phase_P1), ("A", lambda l: phase_AB(l, "A")), ("B", lambda l: phase_AB(l, "B")),
              ("C", phase_C), ("P3a", phase_P3a), ("P3b", phase_P3b)]
    for l in range(L):
        for nm, fn in phases:
            if _ph is None or nm in _ph.split(","):
                fn(l)
    P.finalize(st)
    st.close()
    return nc


def _t5_bucket_np(rel):
    nb = 16
    max_exact = 8
    rel = np.asarray(rel, dtype=np.int32)
    n = np.abs(rel)
    nf = np.maximum(n, 1).astype(np.float32)
    large = max_exact + (np.log(nf / np.float32(max_exact)) / np.float32(math.log(128 / max_exact))
                         * np.float32(nb - max_exact)).astype(np.int32)
    large = np.minimum(large, nb - 1)
    return np.where(rel > 0, nb, 0) + np.where(n < max_exact, n, large)


def _host_tables(S, L, t5_table, na_rpb, sw_sink):
    NT = S // 128
    rows = S // 64
    kk = np.arange(128)
    qq = np.arange(128)
    biasA = np.full((L, 5, 128, 5, 4, 128), NEGM, np.float32)
    sets = [NT // 2, 0, 1, NT - 2, NT - 1]
    for si, i in enumerate(sets):
        qtok = i * 128 + qq
        qr, qc = qtok // 64, qtok % 64
        rs = np.clip(qr - 4, 0, rows - 8)
        cs = np.clip(qc - 8, 0, 64 - 16)
        for di, d in enumerate(_a_deltas(i, NT)):
            c = i + d
            if c < 0 or c >= NT:
                continue
            ktok = c * 128 + kk
            kr, kc = ktok // 64, ktok % 64
            valid = ((kr[:, None] >= rs[None, :]) & (kr[:, None] < rs[None, :] + 8)
                     & (kc[:, None] >= cs[None, :]) & (kc[:, None] < cs[None, :] + 16))
            dri = np.clip(kr[:, None] - qr[None, :] + 7, 0, 14)
            dci = np.clip(kc[:, None] - qc[None, :], -15, 15) + 15
            for l in range(L):
                for h in range(4):
                    vals = na_rpb[l, h][dri, dci]
                    biasA[l, si, :, di, (h % 2) * 2 + h // 2, :] = np.where(valid, vals, np.float32(NEGM))
    biasA = biasA.reshape(L, 5, 128, 5 * 512)
    biasB = np.full((128, 2, 3, 4, 128), NEGM, np.float32)
    for di, d in enumerate([-1, 0, 1]):
        rel = d * 128 + kk[:, None] - qq[None, :]
        bkt = _t5_bucket_np(rel)
        valid = np.abs(rel) <= 128
        for h in range(8):
            vals = t5_table[:, h][bkt]
            j = h % 4
            biasB[:, h // 4, di, (j % 2) * 2 + j // 2, :] = np.where(valid, vals, np.float32(NEGM))
    biasB = biasB.reshape(128, 2 * 3 * 512)
    m = np.arange(1152) - 512
    rel = kk[:, None] - m[None, :]
    bkt = _t5_bucket_np(rel)
    bandC = np.stack([t5_table[:, 8 + h][bkt] for h in range(4)], axis=1).astype(np.float32)
    bandC = np.ascontiguousarray(bandC).reshape(128, 4 * 1152)
    bl = int(_t5_bucket_np(np.array([-1000]))[0])
    br = int(_t5_bucket_np(np.array([1000]))[0])
    cvec = np.concatenate([t5_table[bl, 8:12], t5_table[br, 8:12]]).astype(np.float32)
    cvecC = np.ascontiguousarray(np.broadcast_to(cvec[None, :], (128, 8)))
    sinkrow = np.ascontiguousarray(np.repeat(sw_sink.astype(np.float32), 128, axis=1))
    return biasA, biasB, bandC, cvecC, sinkrow


_NC_CACHE = {}


def _make_inputs(x_seq, L, ln_in_g, ln_in_b, t5_table, w_in, w_out, na_rpb, sw_sink, diff_lam_q, diff_lam_k,
                 diff_subln_g, ln_mix_g, ln_mix_b, w_ff1, w_ff2, ln_ff_g, ln_ff_b, tables):
    biasA, biasB, bandC, cvecC, sinkrow = tables
    rowsl = [np.stack([ln_in_g, ln_in_b])]
    for l in range(L):
        rowsl.append(np.stack([ln_mix_g[l], ln_mix_b[l]]))
        rowsl.append(np.stack([ln_ff_g[l], ln_ff_b[l]]))
    lnp = np.ascontiguousarray(np.stack(rowsl).astype(np.float32))
    return {
        "x": np.ascontiguousarray(x_seq, dtype=np.float32),
        "w_in": np.ascontiguousarray(w_in[:L], dtype=np.float32),
        "w_out": np.ascontiguousarray(w_out[:L], dtype=np.float32),
        "w_ff1": np.ascontiguousarray(w_ff1[:L], dtype=np.float32),
        "w_ff2": np.ascontiguousarray(w_ff2[:L], dtype=np.float32),
        "lnp": lnp,
        "ident": np.eye(128, dtype=np.float32),
        "biasA": biasA, "biasB": biasB, "bandC": bandC, "cvecC": cvecC, "sinkrow": sinkrow,
        "lamq": np.ascontiguousarray(diff_lam_q[:L].reshape(L, 64), dtype=np.float32),
        "lamk": np.ascontiguousarray(diff_lam_k[:L].reshape(L, 64), dtype=np.float32),
        "subg": np.ascontiguousarray(diff_subln_g[:L], dtype=np.float32),
    }


def kernel(x, ln_in_g, ln_in_b, t5_table, w_in, w_out, na_rpb, sw_sink, diff_lam_q, diff_lam_k,
           diff_subln_g, ln_mix_g, ln_mix_b, w_ff1, w_ff2, ln_ff_g, ln_ff_b):
    args = [np.asarray(a) for a in (ln_in_g, ln_in_b, t5_table, w_in, w_out, na_rpb, sw_sink, diff_lam_q,
                                    diff_lam_k, diff_subln_g, ln_mix_g, ln_mix_b, w_ff1, w_ff2, ln_ff_g, ln_ff_b)]
    x = np.asarray(x)
    B, S, _ = x.shape
    L = args[3].shape[0]
    tables = _host_tables(S, L, args[2], args[5], args[6])
    key = (S, L)
    if key not in _NC_CACHE:
        _NC_CACHE[key] = build_program(S, L)
    nc = _NC_CACHE[key]
    in_maps = [_make_inputs(x[b], L, *args, tables) for b in range(B)]
    res = run_bass_kernel_spmd(nc, in_maps, core_ids=list(range(B)))
    return np.stack([np.asarray(res.results[b]["out"], dtype=np.float32) for b in range(B)], axis=0)
```

```python
import math
import numpy as np
from contextlib import ExitStack
import concourse.bass as bass
import concourse.mybir as mybir
from concourse.bass_utils import run_bass_kernel_spmd

F32 = mybir.dt.float32
BF16 = mybir.dt.bfloat16
U8 = mybir.dt.uint8
ALU = mybir.AluOpType
AF = mybir.ActivationFunctionType
AX = mybir.AxisListType

ENGS = ("pe", "act", "dve", "pool", "sp")


class Tok:
    __slots__ = ("name", "w", "r")

    def __init__(self, name=""):
        self.name = name
        self.w = None
        self.r = []


class Ev:
    __slots__ = ("eng", "is_dma", "sem", "val", "used", "op")

    def __init__(self, eng, is_dma, sem=None, val=None):
        self.eng = eng
        self.is_dma = is_dma
        self.sem = sem
        self.val = val
        self.used = False
        self.op = None


class DSem:
    def __init__(self, prog, name):
        self.name = name
        self.count = 0
        self.h = None
        prog.dsems.append(self)


class Op:
    __slots__ = ("eng", "fn", "deps", "ev", "is_dma", "epoch", "inc")

    def __init__(self, eng, fn, ev, is_dma):
        self.eng = eng
        self.fn = fn
        self.deps = []
        self.ev = ev
        self.is_dma = is_dma
        self.epoch = 0
        self.inc = 16


class Prog:
    def __init__(self, nc):
        self.nc = nc
        self.ops = {e: [] for e in ENGS}
        self.dsems = []
        self.esem = {}
        self.n_wait = 0
        self.epoch = 0

    def _deps(self, op, reads, writes):
        ev = op.ev
        deps = op.deps
        for t in reads:
            if t.w is not None and t.w is not ev:
                deps.append(t.w)
        for t in writes:
            w = t.w
            if w is not None and w is not ev and (w.is_dma or ev.is_dma or w.eng != ev.eng):
                deps.append(w)
            for r in t.r:
                if r is not ev and (r.is_dma or ev.is_dma or r.eng != ev.eng):
                    deps.append(r)
        for t in reads:
            if not ev.is_dma:
                t.r = [r for r in t.r if r.is_dma or r.eng != ev.eng]
            t.r.append(ev)
        for t in writes:
            t.w = ev
            t.r = []

    def op(self, eng, fn, reads=(), writes=()):
        ev = Ev(eng, False)
        o = Op(eng, fn, ev, False)
        o.epoch = self.epoch
        ev.op = o
        self._deps(o, reads, writes)
        self.ops[eng].append(o)
        return ev

    def dma(self, q, pairs, sem, reads=(), writes=()):
        ev = Ev(q, True, sem, None)
        first_deps = None
        for pr in pairs:
            out_ap, in_ap = pr[0], pr[1]
            sem.count += 16

            def fn(e, out_ap=out_ap, in_ap=in_ap):
                return e.dma_start(out=out_ap, in_=in_ap)
            o = Op(q, fn, ev, True)
            if first_deps is None:
                self._deps(o, reads, writes)
                first_deps = o.deps
            else:
                o.deps = list(first_deps)
            self.ops[q].append(o)
        ev.val = sem.count
        return ev

    def collective(self, fn, sem):
        ev = Ev("pool", True, sem, 1)
        o = Op("pool", fn, ev, True)
        o.inc = 1
        sem.count = 1
        self.ops["pool"].append(o)
        return ev

    def wait(self, eng, evs):
        ev = Ev(eng, False)
        o = Op(eng, None, ev, False)
        o.deps = [e for e in evs if e is not None]
        self.ops[eng].append(o)
        return ev

    def barrier(self):
        evs = []
        for e in ENGS:
            for o in reversed(self.ops[e]):
                if o.fn is not None and not o.is_dma:
                    evs.append(o.ev)
                    break
        for s in self.dsems:
            if s.count:
                evs.append(Ev(None, True, s, s.count))
        for e in ENGS:
            self.wait(e, evs)
        self.epoch += 1

    def finalize(self, stack):
        nc = self.nc
        for e in ENGS:
            for ep in range(self.epoch + 1):
                if any((not o.is_dma) and o.fn is not None and o.epoch == ep for o in self.ops[e]):
                    self.esem[(e, ep)] = stack.enter_context(nc.semaphore("es_%s%d" % (e, ep)))
        for s in self.dsems:
            s.h = stack.enter_context(nc.semaphore("ds_" + s.name))
        for e in ENGS:
            for o in self.ops[e]:
                for d in o.deps:
                    d.used = True
        for e in ENGS:
            c = {}
            for o in self.ops[e]:
                if not o.is_dma and o.fn is not None and o.ev.used:
                    c[o.epoch] = c.get(o.epoch, 0) + 1
                    o.ev.sem = (e, o.epoch)
                    o.ev.val = c[o.epoch]
                    assert o.ev.val < 30000
        block = stack.enter_context(nc.Block())
        prog = self

        def run(engkey):
            def body(eng):
                waited = {}
                for o in prog.ops[engkey]:
                    need = {}
                    for d in o.deps:
                        if d.is_dma:
                            key = ("d", id(d.sem))
                            h = d.sem.h
                        else:
                            key = ("e", d.sem)
                            h = prog.esem[d.sem]
                        if d.val > waited.get(key, 0) and d.val > need.get(key, (0, None))[0]:
                            need[key] = (d.val, h)
                    for key, (v, h) in need.items():
                        eng.wait_ge(h, v)
                        waited[key] = v
                        prog.n_wait += 1
                    if o.fn is None:
                        continue
                    ins = o.fn(eng)
                    if o.is_dma:
                        ins.then_inc(o.ev.sem.h, o.inc)
                    elif o.ev.used:
                        ins.then_inc(prog.esem[(engkey, o.epoch)], 1)
            return body

        block.tensor(run("pe"))
        block.scalar(run("act"))
        block.vector(run("dve"))
        block.gpsimd(run("pool"))
        block.sync(run("sp"))


D = 1024
DFF = 4096
NEGM = -10000.0
LN_EPS = 1e-5
VW = 650
NQK = 14


def _a_deltas(i, NT):
    if i == 0:
        return [0, 1, 2, 3]
    if i == NT - 1:
        return [-3, -2, -1, 0]
    return [-2, -1, 0, 1, 2]


class Arena:
    def __init__(self, nc, st, nbytes):
        self.t = st.enter_context(nc.sbuf_tensor("arena", [128, nbytes], U8))
        self.off = 0
        self.cap = nbytes

    def alloc(self, cols, dtype):
        sz = cols * (4 if dtype == F32 else 2)
        sz_al = (sz + 63) // 64 * 64
        assert self.off + sz_al <= self.cap, ("arena overflow", self.off, sz_al, self.cap)
        v = self.t[:, self.off:self.off + sz].bitcast(dtype)
        self.off += sz_al
        return v


def build_program(S, L, B=1):
    NT = S // 128
    NG = S // 512
    assert S % 512 == 0 and NT >= 8
    alpha = float((2 * L) ** 0.25)
    nc = bass.Bass("TRN2", target_bir_lowering=False)
    dr = lambda n, s, d, kind=None: (nc.dram_tensor(n, s, d, kind=kind) if kind else nc.dram_tensor(n, s, d)).ap()
    x_d = dr("x", [S, D], F32, "ExternalInput")
    w_in_d = dr("w_in", [L, D, 2304], F32, "ExternalInput")
    w_out_d = dr("w_out", [L, D, D], F32, "ExternalInput")
    w_ff1_d = dr("w_ff1", [L, D, DFF], F32, "ExternalInput")
    w_ff2_d = dr("w_ff2", [L, DFF, D], F32, "ExternalInput")
    lnp_d = dr("lnp", [1 + 2 * L, 2, D], F32, "ExternalInput")
    ident_d = dr("ident", [128, 128], F32, "ExternalInput")
    biasA_d = dr("biasA", [L, 5, 128, 5 * 512], F32, "ExternalInput")
    biasB_d = dr("biasB", [128, 2 * 3 * 512], F32, "ExternalInput")
    bandC_d = dr("bandC", [128, 4 * 1152], F32, "ExternalInput")
    cvecC_d = dr("cvecC", [128, 8], F32, "ExternalInput")
    sinkrow_d = dr("sinkrow", [L, 1024], F32, "ExternalInput")
    lamq_d = dr("lamq", [L, 64], F32, "ExternalInput")
    lamk_d = dr("lamk", [L, 64], F32, "ExternalInput")
    subg_d = dr("subg", [L, 64], F32, "ExternalInput")
    out_d = dr("out", [S, D], F32, "ExternalOutput")
    xres_d = dr("xres", [S, D], F32)
    qkT_d = dr("qkT", [NQK, 128, S], BF16)
    vtok_d = dr("vtok", [S, VW], BF16)
    attT_d = dr("attT", [D, S], BF16)
    x1T_d = dr("x1T", [D, S], BF16)
    cin_d = [nc.dram_tensor("cin%d" % l_, [128, S], BF16) for l_ in range(L)]
    cout_d = [nc.dram_tensor("cout%d" % l_, [256, S], BF16) for l_ in range(L)]

    st = ExitStack()
    P = Prog(nc)
    A = Arena(nc, st, 206 * 1024)
    pball = st.enter_context(nc.psum_tensor("pball", [128, 8 * 512], F32))
    pb = [pball[:, i * 512:(i + 1) * 512] for i in range(8)]
    ptok = [Tok("pb%d" % i) for i in range(8)]
    nsem = [0]

    free_sems = []
    used_sems = []

    def dsem(name):
        if free_sems:
            d = free_sems.pop()
        else:
            nsem[0] += 1
            d = DSem(P, "s%d" % nsem[0])
        used_sems.append(d)
        return d

    free_psems = []
    used_psems = []

    def psem(name):
        if free_psems:
            d = free_psems.pop()
        else:
            nsem[0] += 1
            d = DSem(P, "p%d" % nsem[0])
        used_psems.append(d)
        return d

    def phase_end():
        P.barrier()
        free_sems.extend(used_sems)
        del used_sems[:]
        free_psems.extend(used_psems)
        del used_psems[:]

    def MM(out, lhsT, rhs, start, stop, r, w, tp=None, skip=False):
        kw = {}
        if tp is not None:
            kw["tile_position"] = tp
        if skip:
            kw["skip_group_check"] = True
        P.op("pe", lambda e: e.matmul(out, lhsT, rhs, start=start, stop=stop, **kw), r, w)

    def ACTF(out, in_, func, r, w, bias=None, scale=None):
        kw = {}
        if bias is not None:
            kw["bias"] = bias
        if scale is not None:
            kw["scale"] = scale
        return P.op("act", lambda e: e.activation(out, in_, func, **kw), r, w)

    def TT(eng, out, in0, in1, op, r, w):
        return P.op(eng, lambda e: e.tensor_tensor(out, in0, in1, op), r, w)

    def STT(eng, out, in0, scalar, in1, op0, op1, r, w):
        return P.op(eng, lambda e: e.scalar_tensor_tensor(out, in0, scalar, in1, op0=op0, op1=op1), r, w)

    def TS(eng, out, in0, s1, s2, op0, op1, r, w):
        if s2 is None:
            return P.op(eng, lambda e: e.tensor_scalar(out, in0, s1, None, op0=op0), r, w)
        return P.op(eng, lambda e: e.tensor_scalar(out, in0, s1, s2, op0=op0, op1=op1), r, w)

    def CP(eng, out, in_, r, w):
        if eng == "act":
            return ACTF(out, in_, AF.Copy, r, w)
        return P.op(eng, lambda e: e.tensor_copy(out, in_), r, w)

    def MEMSET(eng, ap, val, w):
        return P.op(eng, lambda e: e.memset(ap, val), (), w)

    def SPLIT_MM(out_ps, t_out, p0, p1, src, t_src, hi, lo, t_hl):
        CP("dve", hi[p0:p1, :], src[p0:p1, :], [t_src], [t_hl])
        STT("dve", lo[p0:p1, :], hi[p0:p1, :], -1.0, src[p0:p1, :], ALU.mult, ALU.add, [t_src, t_hl], [t_hl])
        MM(out_ps, ones_b[p0:p1, 0:64], hi[p0:p1, :], True, False, [t_ones, t_hl], [t_out])
        MM(out_ps, ones_b[p0:p1, 0:64], lo[p0:p1, :], False, True, [t_ones, t_hl], [t_out])

    idf = A.alloc(128, F32)
    idb = A.alloc(128, BF16)
    ones_f = A.alloc(64, F32)
    ones_b = A.alloc(64, BF16)
    epsc = A.alloc(1, F32)
    t_idf, t_idb, t_ones, t_eps = Tok(), Tok(), Tok(), Tok()
    s_misc = dsem("misc")
    P.dma("sp", [(idf, ident_d)], s_misc, writes=[t_idf])
    CP("dve", idb, idf, [t_idf], [t_idb])
    MEMSET("pool", ones_f, 1.0, [t_ones])
    MEMSET("pool", ones_b, 1.0, [t_ones])
    MEMSET("pool", epsc, LN_EPS, [t_eps])
    lnsm = []
    for i in range(2):
        lnsm.append(dict(stats=A.alloc(12, F32).rearrange("p (c f) -> p c f", c=2), mv=A.alloc(2, F32),
                         sd=A.alloc(1, F32), rstd=A.alloc(1, F32), nb=A.alloc(1, F32),
                         t_stats=Tok(), t_mv=Tok(), t_sd=Tok(), t_rstd=Tok(), t_nb=Tok()))
    mark = A.off
    lncnt = [0]

    def LN(z, t_z, g, b, t_gb, ybf=None, t_ybf=None):
        sm = lnsm[lncnt[0] % 2]
        lncnt[0] += 1
        zv = z.rearrange("p (c f) -> p c f", c=2)
        for c in range(2):
            P.op("dve", lambda e, c=c: e.bn_stats(sm["stats"][:, c, :], zv[:, c, :]), [t_z], [sm["t_stats"]])
        P.op("dve", lambda e: e.bn_aggr(sm["mv"], sm["stats"]), [sm["t_stats"]], [sm["t_mv"]])
        ACTF(sm["sd"], sm["mv"][:, 1:2], AF.Sqrt, [sm["t_mv"], t_eps], [sm["t_sd"]], bias=epsc, scale=1.0)
        P.op("dve", lambda e: e.reciprocal(sm["rstd"], sm["sd"]), [sm["t_sd"]], [sm["t_rstd"]])
        STT("dve", sm["nb"], sm["mv"][:, 0:1], -1.0, sm["rstd"], ALU.mult, ALU.mult,
            [sm["t_mv"], sm["t_rstd"]], [sm["t_nb"]])
        ACTF(z, z, AF.Identity, [t_z, sm["t_nb"], sm["t_rstd"]], [t_z], bias=sm["nb"], scale=sm["rstd"])
        TT("dve", z, z, g, ALU.mult, [t_z, t_gb], [t_z])
        TT("pool", z, z, b, ALU.add, [t_z, t_gb], [t_z])
        if ybf is not None:
            CP("act", ybf, z, [t_z], [t_ybf])

    def load_gb(idx, sem):
        gb = A.alloc(2 * D, F32).rearrange("p (c f) -> p c f", c=2)
        t = Tok()
        P.dma("sp", [(gb[:, 0, :], lnp_d[idx, 0:1, :].to_broadcast([128, D])),
                     (gb[:, 1, :], lnp_d[idx, 1:2, :].to_broadcast([128, D]))], sem, writes=[t])
        return gb, t

    def transposes(src_bf, t_src, dstT, t_dst, col0, bank):
        pT = pb[bank].bitcast(BF16).rearrange("p (k f) -> p k f", k=8)
        for k in range(8):
            P.op("pe", lambda e, k=k: e.transpose(pT[:, k, :], src_bf[:, k * 128:(k + 1) * 128], idb),
                 [t_src, t_idb], [ptok[bank]])
        CP("dve", dstT[:, :, col0:col0 + 128], pT, [ptok[bank]], [t_dst])

    def phase_P1(l):
        A.off = mark
        s_w = psem("w")
        w_sb = A.alloc(8 * 2432, BF16).rearrange("p (k n) -> p k n", k=8)
        t_w = Tok()
        src = w_in_d[l].rearrange("(k p) n -> p k n", p=128)
        cols = [(0, 0, 256), (256, 256, 256), (512, 768, 512), (1024, 1280, 64), (1088, 1280, 64),
                (1152, 1344, 64), (1216, 1344, 64), (1280, 1536, 256), (1536, 1792, 256),
                (1792, 512, 256), (2048, 1408, 128), (2176, 2048, 256)]
        P.dma("pool", [(w_sb[:, :, d0:d0 + w], src[:, :, s0:s0 + w]) for d0, s0, w in cols], s_w, writes=[t_w])
        if l == 0:
            gb, t_gb = load_gb(0, dsem("gb"))
        nx = 3
        xin = [A.alloc(D, F32) for _ in range(nx)]
        t_xin = [Tok() for _ in range(nx)]
        s_xin = [dsem("xin") for _ in range(nx)]
        s_yb = [dsem("yst") for _ in range(nx)]
        ybf = [A.alloc(D, BF16) for _ in range(2)]
        t_ybf = [Tok() for _ in range(2)]
        xT = [A.alloc(8 * 512, BF16).rearrange("p (k n) -> p k n", k=8) for _ in range(2)]
        t_xT = [Tok() for _ in range(2)]
        stg = [A.alloc(NQK * 512, BF16).rearrange("p (t n) -> p t n", t=NQK) for _ in range(2)]
        t_stg = [Tok() for _ in range(2)]
        s_stg = [dsem("stg") for _ in range(2)]
        vstg = [A.alloc(4 * VW, BF16).rearrange("p (t h c) -> p t h c", t=4, h=10) for _ in range(2)]
        t_vstg = [Tok() for _ in range(2)]
        s_vstg = [dsem("vstg") for _ in range(2)]
        for i in range(2):
            MEMSET("pool", vstg[i][:, :, :, 64:65], 1.0, [t_vstg[i]])
        src_d = x_d if l == 0 else xres_d
        qk_view = qkT_d.rearrange("t p s -> p t s")
        v_view = vtok_d.rearrange("(n p) c -> p n c", p=128)

        def load_x(ti):
            sl = ti % nx
            P.dma("sp", [(xin[sl], src_d[ti * 128:(ti + 1) * 128, :])], s_xin[sl], writes=[t_xin[sl]])

        load_x(0)
        load_x(1)
        pcnt = 0
        for g in range(NG):
            gs = g % 2
            for tt in range(4):
                ti = g * 4 + tt
                if ti + 2 < NT:
                    load_x(ti + 2)
                sl = ti % nx
                ys = ti % 2
                if l == 0:
                    LN(xin[sl], t_xin[sl], gb[:, 0, :], gb[:, 1, :], t_gb, ybf[ys], t_ybf[ys])
                    P.dma("sp", [(xres_d[ti * 128:(ti + 1) * 128, :], xin[sl])], s_yb[sl], reads=[t_xin[sl]])
                else:
                    CP("act", ybf[ys], xin[sl], [t_xin[sl]], [t_ybf[ys]])
                transposes(ybf[ys], t_ybf[ys], xT[gs], t_xT[gs], tt * 128, 7)
            for t in range(NQK):
                bk = pcnt % 3
                pcnt += 1
                for k in range(8):
                    MM(pb[bk], w_sb[:, k, t * 128:(t + 1) * 128], xT[gs][:, k, :], k == 0, k == 7,
                       [t_w, t_xT[gs]], [ptok[bk]])
                CP("act" if t % 2 == 0 else "dve", stg[gs][:, t, :], pb[bk], [ptok[bk]], [t_stg[gs]])
            P.dma("sp", [(qk_view[:, :, g * 512:(g + 1) * 512], stg[gs])], s_stg[gs], reads=[t_stg[gs]])
            for tt in range(4):
                ba, bb = (3, 4) if tt % 2 == 0 else (5, 6)
                for k in range(8):
                    MM(pb[ba], xT[gs][:, k, tt * 128:(tt + 1) * 128], w_sb[:, k, 1792:2304], k == 0, k == 7,
                       [t_w, t_xT[gs]], [ptok[ba]])
                for k in range(8):
                    MM(pb[bb][:, 0:128], xT[gs][:, k, tt * 128:(tt + 1) * 128], w_sb[:, k, 2304:2432], k == 0, k == 7,
                       [t_w, t_xT[gs]], [ptok[bb]])
                CP("dve", vstg[gs][:, tt, 0:8, 0:64], pb[ba].rearrange("p (h c) -> p h c", h=8),
                   [ptok[ba]], [t_vstg[gs]])
                CP("act", vstg[gs][:, tt, 8:10, 0:64], pb[bb][:, 0:128].rearrange("p (h c) -> p h c", h=2),
                   [ptok[bb]], [t_vstg[gs]])
            P.dma("sp", [(v_view[:, g * 4:(g + 1) * 4, :], vstg[gs].rearrange("p t h c -> p t (h c)"))],
                  s_vstg[gs], reads=[t_vstg[gs]])
        phase_end()

    def load_big(dst, src, nsplit, sem, tok):
        n = dst.shape[1]
        step = (n + nsplit - 1) // nsplit
        pairs = []
        for a in range(0, n, step):
            b = min(n, a + step)
            pairs.append((dst[:, a:b], src[:, a:b]))
        P.dma("sp", pairs, sem, writes=[tok])

    def phase_AB(l, mixer):
        A.off = mark
        qk_view = qkT_d.rearrange("t p s -> p t s")
        v_view = vtok_d.rearrange("(n p) c -> p n c", p=128)
        if mixer == "A":
            q0, nqt, k0, ngrp, deltas, vcol0, nvh, arow0 = 0, 2, 2, 1, [-2, -1, 0, 1, 2], 0, 4, 0
        else:
            q0, nqt, k0, ngrp, deltas, vcol0, nvh, arow0 = 4, 4, 8, 2, [-1, 0, 1], 260, 2, 256
        nd = len(deltas)
        QT = A.alloc(nqt * S, BF16).rearrange("p (t s) -> p t s", t=nqt)
        KT = A.alloc(2 * S, BF16).rearrange("p (t s) -> p t s", t=2)
        V = A.alloc(NT * nvh * 65, BF16).rearrange("p (n c) -> p n c", n=NT)
        t_Q, t_K, t_V = Tok(), Tok(), Tok()
        P.dma("sp", [(QT[:, t, :], qk_view[:, q0 + t, :]) for t in range(nqt)], dsem("ldq"), writes=[t_Q])
        P.dma("sp", [(KT[:, t, :], qk_view[:, k0 + t, :]) for t in range(2)], dsem("ldk"), writes=[t_K])
        load_big(V, v_view[:, :, vcol0:vcol0 + nvh * 65], max(1, NT // 8), dsem("ldv"), t_V)
        if mixer == "A":
            bias_g = A.alloc(5 * 512, F32).rearrange("p (d n) -> p d n", d=5)
            bias_s2 = [A.alloc(5 * 512, F32).rearrange("p (d n) -> p d n", d=5) for _ in range(2)]
            t_bg = Tok()
            t_bs2 = [Tok(), Tok()]
            s_bg = dsem("bg")
            s_bs2 = [dsem("bs"), dsem("bs")]
            P.dma("sp", [(bias_g, biasA_d[l, 0].rearrange("p (d n) -> p d n", d=5))], s_bg, writes=[t_bg])
            special = {0: 1, 1: 2, NT - 2: 3, NT - 1: 4}
        else:
            bias_b = A.alloc(6 * 512, F32).rearrange("p (g d n) -> p g d n", g=2, d=3)
            t_bb = Tok()
            s_bb = dsem("bb")
            P.dma("sp", [(bias_b, biasB_d.rearrange("p (g d n) -> p g d n", g=2, d=3))], s_bb, writes=[t_bb])
            esrow = A.alloc(1024, F32)
            t_es = Tok()
            P.dma("sp", [(esrow[64:65, :], sinkrow_d[l:l + 1, :])], dsem("es"), writes=[t_es])
            ACTF(esrow[64:65, :], esrow[64:65, :], AF.Exp, [t_es], [t_es])
        tmpb = [A.alloc(512, F32) for _ in range(2)]
        t_tmpb = [Tok() for _ in range(2)]
        PT = [A.alloc(512, BF16) for _ in range(3)]
        t_PT = [Tok() for _ in range(3)]
        osb = [A.alloc(512, F32) for _ in range(2)]
        t_osb = [Tok() for _ in range(2)]
        rr = [A.alloc(512, F32) for _ in range(2)]
        t_rr = [Tok() for _ in range(2)]
        hi = [A.alloc(512, BF16) for _ in range(2)]
        lo = [A.alloc(512, BF16) for _ in range(2)]
        t_hl = [Tok() for _ in range(2)]
        stage = [A.alloc(ngrp * 4 * 512, BF16).rearrange("p (g j n) -> p g j n", g=ngrp, j=4) for _ in range(2)]
        t_stage = [Tok() for _ in range(2)]
        s_stage = [dsem("stage") for _ in range(2)]
        steps = []
        for i in range(NT):
            for grp in range(ngrp):
                dl = _a_deltas(i, NT) if mixer == "A" else deltas
                cands = [d for d in dl if 0 <= i + d < NT]
                for ci, d in enumerate(cands):
                    steps.append(dict(i=i, grp=grp, d=d, di=dl.index(d), c=i + d, first=(ci == 0),
                                      last=(ci == len(cands) - 1), unit=i * ngrp + grp))
        deferred = []

        def emit_qk(idx):
            sp_ = steps[idx]
            i, grp, c = sp_["i"], sp_["grp"], sp_["c"]
            pp = idx % 2
            if mixer == "A" and i in special and sp_["first"] and grp == 0:
                sl_ = special[i] % 2
                P.dma("sp", [(bias_s2[sl_], biasA_d[l, special[i]].rearrange("p (d n) -> p d n", d=5))], s_bs2[sl_],
                      writes=[t_bs2[sl_]])
            for j in range(4):
                h = grp * 4 + j
                pr = (h % 2) * 64
                qt = h // 2
                kt = (j // 2) if mixer == "A" else grp
                a_, b_ = j // 2, j % 2
                MM(pb[2 * pp + b_][:, a_ * 128:(a_ + 1) * 128], KT[pr:pr + 64, kt, c * 128:(c + 1) * 128],
                   QT[pr:pr + 64, qt, i * 128:(i + 1) * 128], True, True, [t_K, t_Q], [ptok[2 * pp + b_]])

        def emit_rest(idx):
            sp_ = steps[idx]
            i, grp, c, di = sp_["i"], sp_["grp"], sp_["c"], sp_["di"]
            pp = idx % 2
            tb = idx % 2
            pt = idx % 3
            ob = 4 + (sp_["unit"] % 2)
            if mixer == "A":
                if i in special:
                    btile, t_b = bias_s2[special[i] % 2][:, di, :], t_bs2[special[i] % 2]
                else:
                    btile, t_b = bias_g[:, di, :], t_bg
            else:
                btile, t_b = bias_b[:, grp, di, :], t_bb
            ps2 = pball[:, 2 * pp * 512:(2 * pp + 2) * 512].rearrange("p (b x) -> p b x", b=2)[:, :, 0:256]
            STT("dve", tmpb[tb].rearrange("p (b x) -> p b x", b=2), ps2, 0.125,
                btile.rearrange("p (b x) -> p b x", b=2),
                ALU.mult, ALU.add, [ptok[2 * pp], ptok[2 * pp + 1], t_b], [t_tmpb[tb]])
            ACTF(PT[pt], tmpb[tb], AF.Exp, [t_tmpb[tb]], [t_PT[pt]])
            for j in range(4):
                vc = (j if mixer == "A" else grp) * 65
                pcol = ((j % 2) * 2 + j // 2) * 128
                MM(pb[ob][0:65, j * 128:(j + 1) * 128], V[:, c, vc:vc + 65], PT[pt][:, pcol:pcol + 128],
                   sp_["first"] and j == 0, sp_["last"], [t_V, t_PT[pt]], [ptok[ob]], skip=True)
            if sp_["last"]:
                post_a(idx, sp_)

        def post_a(idx, sp_):
            i, grp = sp_["i"], sp_["grp"]
            ob = 4 + (sp_["unit"] % 2)
            os_ = sp_["unit"] % 2
            g4 = i // 4
            ss = g4 % 2
            CP("dve", osb[os_][0:65, :], pb[ob][0:65, :], [ptok[ob]], [t_osb[os_]])
            if mixer == "B":
                TT("dve", osb[os_][64:65, :], osb[os_][64:65, :], esrow[64:65, grp * 512:(grp + 1) * 512], ALU.add,
                   [t_osb[os_], t_es], [t_osb[os_]])
            P.op("dve", lambda e, a=rr[os_], b=osb[os_]: e.reciprocal(a[64:65, :], b[64:65, :]),
                 [t_osb[os_]], [t_rr[os_]])
            CP("dve", hi[os_][64:65, :], rr[os_][64:65, :], [t_rr[os_]], [t_hl[os_]])
            STT("dve", lo[os_][64:65, :], hi[os_][64:65, :], -1.0, rr[os_][64:65, :], ALU.mult, ALU.add,
                [t_rr[os_], t_hl[os_]], [t_hl[os_]])

            def part2():
                MM(pb[6][0:64, :], ones_b[64:65, 0:64], hi[os_][64:65, :], True, False, [t_ones, t_hl[os_]], [ptok[6]])
                MM(pb[6][0:64, :], ones_b[64:65, 0:64], lo[os_][64:65, :], False, True, [t_ones, t_hl[os_]], [ptok[6]])
                TT("dve", stage[ss][0:64, grp, :, (i % 4) * 128:(i % 4 + 1) * 128],
                   osb[os_][0:64, :].rearrange("p (j n) -> p j n", j=4),
                   pb[6][0:64, :].rearrange("p (j n) -> p j n", j=4), ALU.mult,
                   [t_osb[os_], ptok[6]], [t_stage[ss]])
                if i % 4 == 3 and grp == ngrp - 1:
                    pairs = []
                    for g_ in range(ngrp):
                        r0 = arow0 + g_ * 256
                        pairs.append((attT_d[r0:r0 + 256, g4 * 512:(g4 + 1) * 512].rearrange("(j p) s -> p j s", p=64),
                                      stage[ss][0:64, g_, :, :]))
                    P.dma("sp", pairs, s_stage[ss], reads=[t_stage[ss]])
            deferred.append((idx + 2, part2))

        LA = 1
        nst = len(steps)
        for idx in range(nst + LA + 3):
            if idx < nst:
                emit_qk(idx)
            j_ = idx - LA
            if 0 <= j_ < nst:
                emit_rest(j_)
            while deferred and deferred[0][0] <= j_:
                deferred.pop(0)[1]()
        assert not deferred
        phase_end()

    def phase_C(l):
        A.off = mark
        lam_init = 0.8 - 0.6 * math.exp(-0.3 * l)
        scale = float(32 ** -0.5)
        qk_view = qkT_d.rearrange("t p s -> p t s")
        v_view = vtok_d.rearrange("(n p) c -> p n c", p=128)
        QT = A.alloc(2 * S, BF16).rearrange("p (t s) -> p t s", t=2)
        KT = A.alloc(2 * S, BF16).rearrange("p (t s) -> p t s", t=2)
        V = A.alloc(NT * 260, BF16).rearrange("p (n c) -> p n c", n=NT)
        t_Q, t_K, t_V = Tok(), Tok(), Tok()
        P.dma("sp", [(KT[:, t, :], qk_view[:, 12 + t, :]) for t in range(2)], dsem("ldk"), writes=[t_K])
        P.dma("sp", [(QT[:, t, :], qk_view[:, 10 + t, :]) for t in range(2)], dsem("ldq"), writes=[t_Q])
        load_big(V, v_view[:, :, 390:650], max(1, NT // 8), dsem("ldv"), t_V)
        band = A.alloc(4 * 1152, F32).rearrange("p (h n) -> p h n", h=4)
        cvec = A.alloc(8, F32)
        t_band, t_cvec = Tok(), Tok()
        P.dma("sp", [(band, bandC_d.rearrange("p (h n) -> p h n", h=4))], dsem("band"), writes=[t_band])
        P.dma("sp", [(cvec, cvecC_d)], dsem("cvec"), writes=[t_cvec])
        lq = A.alloc(64, F32)
        lk = A.alloc(64, F32)
        e2 = A.alloc(2, F32)
        nlam = A.alloc(1, F32)
        gcol = A.alloc(1, F32)
        t_lq, t_lk, t_e2, t_nlam, t_g = Tok(), Tok(), Tok(), Tok(), Tok()
        P.dma("sp", [(lq, lamq_d[l:l + 1, :].to_broadcast([128, 64]))], dsem("lq"), writes=[t_lq])
        P.dma("sp", [(lk, lamk_d[l:l + 1, :].to_broadcast([128, 64]))], dsem("lk"), writes=[t_lk])
        P.dma("sp", [(gcol[0:64, :], subg_d[l].rearrange("(p o) -> p o", o=1))], dsem("g"), writes=[t_g])
        TT("dve", lq, lq, lk, ALU.mult, [t_lq, t_lk], [t_lq])
        P.op("dve", lambda e: e.reduce_sum(e2, lq.rearrange("p (m d) -> p m d", m=2), AX.X), [t_lq], [t_e2])
        ACTF(e2, e2, AF.Exp, [t_e2], [t_e2])
        TT("dve", nlam, e2[:, 1:2], e2[:, 0:1], ALU.subtract, [t_e2], [t_nlam])
        TS("dve", nlam, nlam, -lam_init, None, ALU.add, None, [t_nlam], [t_nlam])
        TS("dve", gcol[0:64, :], gcol[0:64, :], 1.0 - lam_init, None, ALU.mult, None, [t_g], [t_g])

        tmpb = [A.alloc(512, F32) for _ in range(2)]
        t_tmpb = [Tok() for _ in range(2)]
        PT = [A.alloc(512, BF16) for _ in range(4)]
        t_PT = [Tok() for _ in range(4)]
        osb = [A.alloc(512, F32) for _ in range(2)]
        t_osb = [Tok() for _ in range(2)]
        rr = [A.alloc(512, F32) for _ in range(2)]
        t_rr = [Tok() for _ in range(2)]
        av = [A.alloc(512, F32) for _ in range(2)]
        t_av = [Tok() for _ in range(2)]
        ov = A.alloc(512, F32)
        sq = A.alloc(512, F32)
        sdv = A.alloc(512, F32)
        t_ov, t_sq, t_sdv = Tok(), Tok(), Tok()
        hi = [A.alloc(512, BF16) for _ in range(2)]
        lo = [A.alloc(512, BF16) for _ in range(2)]
        t_hl = [Tok() for _ in range(2)]
        stage = [A.alloc(4 * 512, BF16).rearrange("p (j n) -> p j n", j=4) for _ in range(2)]
        t_stage = [Tok() for _ in range(2)]
        s_stage = [dsem("stc") for _ in range(2)]
        NHL = 2
        its = [(G, h, c) for G in range(NG) for h in range(NHL) for c in range(NT)]
        nit = len(its)
        deferred = []
        zi = [0]
        PT2 = [A.alloc(1024, BF16) for _ in range(3)]
        t_PT2 = [Tok() for _ in range(3)]
        tmp2 = [A.alloc(1024, F32) for _ in range(2)]
        t_tmp2 = [Tok() for _ in range(2)]

        def emit_qk(idx):
            G, h, c = its[idx]
            tile = h // 2
            for m in range(2):
                pr = ((h % 2) * 2 + m) * 32
                tp = (96, 0) if pr == 96 else None
                bk = 2 * (idx % 2) + m
                MM(pb[bk], KT[pr:pr + 32, tile, c * 128:(c + 1) * 128], QT[pr:pr + 32, tile, G * 512:(G + 1) * 512],
                   True, True, [t_K, t_Q], [ptok[bk]], tp=tp)

        def emit_rest(idx):
            G, h, c = its[idx]
            b0 = 2 * (idx % 2)
            ps2 = pball[:, b0 * 512:(b0 + 2) * 512]
            pt = idx % 3
            o = c - 4 * G
            if -1 <= o <= 4:
                tb = zi[0] % 2
                zi[0] += 1
                for m in range(2):
                    STT("dve", tmp2[tb][:, m * 512:(m + 1) * 512], pb[b0 + m], scale,
                        band[:, h, 512 - 128 * o:1024 - 128 * o], ALU.mult, ALU.add,
                        [ptok[b0 + m], t_band], [t_tmp2[tb]])
                ACTF(PT2[pt], tmp2[tb], AF.Exp, [t_tmp2[tb]], [t_PT2[pt]])
            else:
                side = 0 if o < 0 else 1
                ACTF(PT2[pt], ps2, AF.Exp, [ptok[b0], ptok[b0 + 1], t_cvec], [t_PT2[pt]],
                     bias=cvec[:, side * 4 + h:side * 4 + h + 1], scale=scale)
            for m in range(2):
                MM(pb[4 + m][0:65, :], V[:, c, h * 65:(h + 1) * 65], PT2[pt][:, m * 512:(m + 1) * 512],
                   c == 0, c == NT - 1, [t_V, t_PT2[pt]], [ptok[4 + m]])
            if c == NT - 1:
                post_c(idx, G, h)

        def post_c(idx, G, h):
            ss = G % 2
            for m in range(2):
                ob = 4 + m
                CP("dve", osb[m][0:65, :], pb[ob][0:65, :], [ptok[ob]], [t_osb[m]])
                P.op("dve", lambda e, a=rr[m], b=osb[m]: e.reciprocal(a[64:65, :], b[64:65, :]), [t_osb[m]], [t_rr[m]])
                CP("dve", hi[m][64:65, :], rr[m][64:65, :], [t_rr[m]], [t_hl[m]])
                STT("dve", lo[m][64:65, :], hi[m][64:65, :], -1.0, rr[m][64:65, :], ALU.mult, ALU.add,
                    [t_rr[m], t_hl[m]], [t_hl[m]])

            def part2():
                for m in range(2):
                    MM(pb[6][0:64, :], ones_b[64:65, 0:64], hi[m][64:65, :], True, False, [t_ones, t_hl[m]], [ptok[6]])
                    MM(pb[6][0:64, :], ones_b[64:65, 0:64], lo[m][64:65, :], False, True, [t_ones, t_hl[m]], [ptok[6]])
                    TT("dve", av[m][0:64, :], osb[m][0:64, :], pb[6][0:64, :], ALU.mult, [t_osb[m], ptok[6]], [t_av[m]])
                STT("dve", ov[0:64, :], av[1][0:64, :], nlam[0:64, :], av[0][0:64, :], ALU.mult, ALU.add,
                    [t_av[0], t_av[1], t_nlam], [t_ov])
                TT("pool", sq[0:64, :], ov[0:64, :], ov[0:64, :], ALU.mult, [t_ov], [t_sq])
                CP("dve", hi[0][0:64, :], sq[0:64, :], [t_sq], [t_hl[0]])
                STT("dve", lo[0][0:64, :], hi[0][0:64, :], -1.0, sq[0:64, :], ALU.mult, ALU.add, [t_sq, t_hl[0]], [t_hl[0]])

            def part3():
                MM(pb[6][0:64, :], ones_b[0:64, 0:64], hi[0][0:64, :], True, False, [t_ones, t_hl[0]], [ptok[6]])
                MM(pb[6][0:64, :], ones_b[0:64, 0:64], lo[0][0:64, :], False, True, [t_ones, t_hl[0]], [ptok[6]])
                ACTF(sdv[0:64, :], pb[6][0:64, :], AF.Sqrt, [ptok[6], t_eps], [t_sdv], bias=epsc[0:64, :], scale=1.0 / 64.0)
                P.op("dve", lambda e: e.reciprocal(sdv[0:64, :], sdv[0:64, :]), [t_sdv], [t_sdv])
                STT("dve", stage[ss][0:64, h, :], ov[0:64, :], gcol[0:64, :], sdv[0:64, :], ALU.mult, ALU.mult,
                    [t_ov, t_g, t_sdv], [t_stage[ss]])
                if h == NHL - 1:
                    P.dma("sp", [(cin_d[l].ap()[0:128, G * 512:(G + 1) * 512].rearrange("(j p) s -> p j s", p=64),
                                  stage[ss][0:64, 0:NHL, :])], s_stage[ss], reads=[t_stage[ss]])
            deferred.append((idx + 4, part2))
            deferred.append((idx + 9, part3))

        LA = 1
        for idx in range(nit + LA + 12):
            if idx < nit:
                emit_qk(idx)
            j_ = idx - LA
            if 0 <= j_ < nit:
                emit_rest(j_)
            while deferred and deferred[0][0] <= j_:
                deferred.pop(0)[1]()
        assert not deferred
        P.barrier()
        nsem[0] += 1
        s_cc = DSem(P, "cc%d" % nsem[0])
        groups = [[b_, b_ + B] for b_ in range(B)]
        cin_ap, cout_ap = cin_d[l].ap().opt(), cout_d[l].ap().opt()
        P.collective(lambda e: e.collective_compute("AllGather", ALU.bypass, replica_groups=groups,
                                                    ins=[cin_ap], outs=[cout_ap]), s_cc)
        phase_end()

    def phase_P3a(l):
        A.off = mark
        s_w = psem("w3")
        wo = A.alloc(8 * D, BF16).rearrange("p (k n) -> p k n", k=8)
        t_wo = Tok()
        P.dma("pool", [(wo, w_out_d[l].rearrange("(k p) n -> p k n", p=128))], s_w, writes=[t_wo])
        gb1, t_gb1 = load_gb(1 + 2 * l, dsem("gb"))
        aT = [A.alloc(8 * 512, BF16).rearrange("p (k n) -> p k n", k=8) for _ in range(2)]
        t_aT = [Tok() for _ in range(2)]
        s_aT = [dsem("aT") for _ in range(2)]
        nx = 3
        xr = [A.alloc(D, F32) for _ in range(nx)]
        t_xr = [Tok() for _ in range(nx)]
        s_xr = [dsem("xr") for _ in range(nx)]
        s_xs = [dsem("xs") for _ in range(nx)]
        x1bf = [A.alloc(D, BF16) for _ in range(2)]
        t_x1bf = [Tok() for _ in range(2)]
        x1T = [A.alloc(8 * 512, BF16).rearrange("p (k n) -> p k n", k=8) for _ in range(2)]
        t_x1T = [Tok() for _ in range(2)]
        s_x1T = [dsem("x1T") for _ in range(2)]
        aview = attT_d.rearrange("(k p) s -> p k s", p=128)
        xview = x1T_d.rearrange("(k p) s -> p k s", p=128)

        cview = cout_d[l].ap().rearrange("(k p) s -> p k s", p=128)

        def loads(g):
            P.dma("sp", [(aT[g % 2][:, 0:6, :], aview[:, 0:6, g * 512:(g + 1) * 512]),
                         (aT[g % 2][:, 6:8, :], cview[:, :, g * 512:(g + 1) * 512])], s_aT[g % 2], writes=[t_aT[g % 2]])

        def load_xr(ti):
            sl = ti % nx
            P.dma("sp", [(xr[sl], xres_d[ti * 128:(ti + 1) * 128, :])], s_xr[sl], writes=[t_xr[sl]])

        loads(0)
        load_xr(0)
        load_xr(1)
        for g in range(NG):
            gs = g % 2
            if g + 1 < NG:
                loads(g + 1)
            for tt in range(4):
                ti = g * 4 + tt
                if ti + 2 < NT:
                    load_xr(ti + 2)
                xs = ti % nx
                bs = ti % 2
                for half in range(2):
                    bk = (tt % 2) * 2 + half
                    for k in range(8):
                        MM(pb[bk], aT[gs][:, k, tt * 128:(tt + 1) * 128], wo[:, k, half * 512:(half + 1) * 512],
                           k == 0, k == 7, [t_aT[gs], t_wo], [ptok[bk]])
                    STT("dve", xr[xs][:, half * 512:(half + 1) * 512], xr[xs][:, half * 512:(half + 1) * 512], alpha,
                        pb[bk], ALU.mult, ALU.add, [t_xr[xs], ptok[bk]], [t_xr[xs]])
                LN(xr[xs], t_xr[xs], gb1[:, 0, :], gb1[:, 1, :], t_gb1, x1bf[bs], t_x1bf[bs])
                P.dma("sp", [(xres_d[ti * 128:(ti + 1) * 128, :], xr[xs])], s_xs[xs], reads=[t_xr[xs]])
                transposes(x1bf[bs], t_x1bf[bs], x1T[gs], t_x1T[gs], tt * 128, 7)
            P.dma("sp", [(xview[:, :, g * 512:(g + 1) * 512], x1T[gs])], s_x1T[gs], reads=[t_x1T[gs]])
        phase_end()

    def phase_P3b(l):
        A.off = mark
        last = (l == L - 1)
        s_w = psem("w4")
        w1 = A.alloc(8 * DFF, BF16).rearrange("p (k n) -> p k n", k=8)
        w2 = A.alloc(32 * D, BF16).rearrange("p (k n) -> p k n", k=32)
        t_w1, t_w2 = Tok(), Tok()
        w1src = w_ff1_d[l].rearrange("(k p) n -> p k n", p=128)
        P.dma("pool", [(w1[:, :, a * 1024:(a + 1) * 1024], w1src[:, :, a * 1024:(a + 1) * 1024]) for a in range(4)],
              s_w, writes=[t_w1])
        w2src = w_ff2_d[l].rearrange("(k p) n -> p k n", p=128)
        P.dma("pool", [(w2[:, a * 8:(a + 1) * 8, :], w2src[:, a * 8:(a + 1) * 8, :]) for a in range(4)],
              psem("w2"), writes=[t_w2])
        gb2, t_gb2 = load_gb(2 + 2 * l, dsem("gb"))
        TG = 256
        NGR = S // TG
        x1T = [A.alloc(8 * TG, BF16).rearrange("p (k n) -> p k n", k=8) for _ in range(2)]
        t_x1T = [Tok() for _ in range(2)]
        s_x1T = [dsem("x1Tl") for _ in range(2)]
        nx = 4
        xr = [A.alloc(D, F32) for _ in range(nx)]
        t_xr = [Tok() for _ in range(nx)]
        s_xr = [dsem("xr") for _ in range(nx)]
        s_xs = [dsem("xs") for _ in range(nx)]
        rl = [A.alloc(512, F32) for _ in range(2)]
        t_rl = [Tok() for _ in range(2)]
        hT = [A.alloc(8 * TG, BF16).rearrange("p (f n) -> p f n", f=8) for _ in range(2)]
        t_hT = [Tok() for _ in range(2)]
        xview = x1T_d.rearrange("(k p) s -> p k s", p=128)
        dst_d = out_d if last else xres_d

        def loads(gi):
            sl = gi % 2
            P.dma("sp", [(x1T[sl], xview[:, :, gi * TG:(gi + 1) * TG])], s_x1T[sl], writes=[t_x1T[sl]])
            for tt in range(2):
                ti = gi * 2 + tt
                P.dma("sp", [(xr[ti % nx], xres_d[ti * 128:(ti + 1) * 128, :])], s_xr[ti % nx], writes=[t_xr[ti % nx]])

        loads(0)
        rcnt = [0]

        def ffn1(b):
            gi, fb = divmod(b, 4)
            sl = gi % 2
            hs = b % 2
            for fp in range(4):
                bk = fp
                for f2 in range(2):
                    f = fb * 8 + fp * 2 + f2
                    for k in range(8):
                        MM(pb[bk][:, f2 * 256:(f2 + 1) * 256], w1[:, k, f * 128:(f + 1) * 128], x1T[sl][:, k, :],
                           k == 0, k == 7, [t_w1, t_x1T[sl]], [ptok[bk]])
                rs = rcnt[0] % 2
                rcnt[0] += 1
                ACTF(rl[rs], pb[bk], AF.Relu, [ptok[bk]], [t_rl[rs]])
                TT("pool" if fp % 2 else "dve", hT[hs][:, fp * 2:fp * 2 + 2, :],
                   rl[rs].rearrange("p (f n) -> p f n", f=2), rl[rs].rearrange("p (f n) -> p f n", f=2),
                   ALU.mult, [t_rl[rs]], [t_hT[hs]])

        def ffn2(b):
            gi, fb = divmod(b, 4)
            hs = b % 2
            for tt in range(2):
                for half in range(2):
                    bk = 4 + tt * 2 + half
                    for ft in range(8):
                        f = fb * 8 + ft
                        MM(pb[bk], hT[hs][:, ft, tt * 128:(tt + 1) * 128], w2[:, f, half * 512:(half + 1) * 512],
                           f == 0, f == 31, [t_hT[hs], t_w2], [ptok[bk]])
            if fb == 3:
                for tt in range(2):
                    ti = gi * 2 + tt
                    xs = ti % nx
                    for half in range(2):
                        bk = 4 + tt * 2 + half
                        STT("dve", xr[xs][:, half * 512:(half + 1) * 512], xr[xs][:, half * 512:(half + 1) * 512], alpha,
                            pb[bk], ALU.mult, ALU.add, [t_xr[xs], ptok[bk]], [t_xr[xs]])
                    LN(xr[xs], t_xr[xs], gb2[:, 0, :], gb2[:, 1, :], t_gb2)
                    P.dma("sp", [(dst_d[ti * 128:(ti + 1) * 128, :], xr[xs])], s_xs[xs], reads=[t_xr[xs]])

        NB = NGR * 4
        if NGR > 1:
            loads(1)
        ffn1(0)
        for b in range(NB):
            if b + 1 < NB:
                ffn1(b + 1)
            ffn2(b)
            if (b + 1) % 4 == 0 and (b + 1) // 4 + 1 < NGR:
                loads((b + 1) // 4 + 1)
        phase_end()

    import os as _os
    _ph = _os.environ.get("KDBG_PHASES")
    phases = [("P1", phase_P1), ("A", lambda l: phase_AB(l, "A")), ("B", lambda l: phase_AB(l, "B")),
              ("C", phase_C), ("P3a", phase_P3a), ("P3b", phase_P3b)]
    for l in range(L):
        for nm, fn in phases:
            if _ph is None or nm in _ph.split(","):
                fn(l)
    P.finalize(st)
    st.close()
    return nc


def _t5_bucket_np(rel):
    nb = 16
    max_exact = 8
    rel = np.asarray(rel, dtype=np.int32)
    n = np.abs(rel)
    nf = np.maximum(n, 1).astype(np.float32)
    large = max_exact + (np.log(nf / np.float32(max_exact)) / np.float32(math.log(128 / max_exact))
                         * np.float32(nb - max_exact)).astype(np.int32)
    large = np.minimum(large, nb - 1)
    return np.where(rel > 0, nb, 0) + np.where(n < max_exact, n, large)


def _host_tables(S, L, t5_table, na_rpb, sw_sink):
    NT = S // 128
    rows = S // 64
    kk = np.arange(128)
    qq = np.arange(128)
    biasA = np.full((L, 5, 128, 5, 4, 128), NEGM, np.float32)
    sets = [NT // 2, 0, 1, NT - 2, NT - 1]
    for si, i in enumerate(sets):
        qtok = i * 128 + qq
        qr, qc = qtok // 64, qtok % 64
        rs = np.clip(qr - 4, 0, rows - 8)
        cs = np.clip(qc - 8, 0, 64 - 16)
        for di, d in enumerate(_a_deltas(i, NT)):
            c = i + d
            if c < 0 or c >= NT:
                continue
            ktok = c * 128 + kk
            kr, kc = ktok // 64, ktok % 64
            valid = ((kr[:, None] >= rs[None, :]) & (kr[:, None] < rs[None, :] + 8)
                     & (kc[:, None] >= cs[None, :]) & (kc[:, None] < cs[None, :] + 16))
            dri = np.clip(kr[:, None] - qr[None, :] + 7, 0, 14)
            dci = np.clip(kc[:, None] - qc[None, :], -15, 15) + 15
            for l in range(L):
                for h in range(4):
                    vals = na_rpb[l, h][dri, dci]
                    biasA[l, si, :, di, (h % 2) * 2 + h // 2, :] = np.where(valid, vals, np.float32(NEGM))
    biasA = biasA.reshape(L, 5, 128, 5 * 512)
    biasB = np.full((128, 2, 3, 4, 128), NEGM, np.float32)
    for di, d in enumerate([-1, 0, 1]):
        rel = d * 128 + kk[:, None] - qq[None, :]
        bkt = _t5_bucket_np(rel)
        valid = np.abs(rel) <= 128
        for h in range(8):
            vals = t5_table[:, h][bkt]
            j = h % 4
            biasB[:, h // 4, di, (j % 2) * 2 + j // 2, :] = np.where(valid, vals, np.float32(NEGM))
    biasB = biasB.reshape(128, 2 * 3 * 512)
    m = np.arange(1152) - 512
    rel = kk[:, None] - m[None, :]
    bkt = _t5_bucket_np(rel)
    bandC = np.stack([t5_table[:, 8 + h][bkt] for h in range(4)], axis=1).astype(np.float32)
    bandC = np.ascontiguousarray(bandC).reshape(128, 4 * 1152)
    bl = int(_t5_bucket_np(np.array([-1000]))[0])
    br = int(_t5_bucket_np(np.array([1000]))[0])
    cvec = np.concatenate([t5_table[bl, 8:12], t5_table[br, 8:12]]).astype(np.float32)
    cvecC = np.ascontiguousarray(np.broadcast_to(cvec[None, :], (128, 8)))
    sinkrow = np.ascontiguousarray(np.repeat(sw_sink.astype(np.float32), 128, axis=1))
    return biasA, biasB, bandC, cvecC, sinkrow


_NC_CACHE = {}


def _perm_c_heads(p, w_in, bandC, cvecC):
    order = [2 * p, 2 * p + 1] + [h for h in range(4) if h // 2 != p]
    w = np.array(w_in, dtype=np.float32, copy=True)
    for base in (1536, 1792, 2048):
        blk = w[:, :, base:base + 256].reshape(w.shape[0], w.shape[1], 4, 64)
        w[:, :, base:base + 256] = blk[:, :, order, :].reshape(w.shape[0], w.shape[1], 256)
    b = bandC.reshape(128, 4, 1152)[:, order, :].reshape(128, 4 * 1152)
    c = cvecC.reshape(128, 2, 4)[:, :, order].reshape(128, 8)
    return np.ascontiguousarray(w), np.ascontiguousarray(b), np.ascontiguousarray(c)


def _make_inputs(x_seq, L, ln_in_g, ln_in_b, t5_table, w_in, w_out, na_rpb, sw_sink, diff_lam_q, diff_lam_k,
                 diff_subln_g, ln_mix_g, ln_mix_b, w_ff1, w_ff2, ln_ff_g, ln_ff_b, tables):
    biasA, biasB, bandC, cvecC, sinkrow = tables
    rowsl = [np.stack([ln_in_g, ln_in_b])]
    for l in range(L):
        rowsl.append(np.stack([ln_mix_g[l], ln_mix_b[l]]))
        rowsl.append(np.stack([ln_ff_g[l], ln_ff_b[l]]))
    lnp = np.ascontiguousarray(np.stack(rowsl).astype(np.float32))
    return {
        "x": np.ascontiguousarray(x_seq, dtype=np.float32),
        "w_in": np.ascontiguousarray(w_in[:L], dtype=np.float32),
        "w_out": np.ascontiguousarray(w_out[:L], dtype=np.float32),
        "w_ff1": np.ascontiguousarray(w_ff1[:L], dtype=np.float32),
        "w_ff2": np.ascontiguousarray(w_ff2[:L], dtype=np.float32),
        "lnp": lnp,
        "ident": np.eye(128, dtype=np.float32),
        "biasA": biasA, "biasB": biasB, "bandC": bandC, "cvecC": cvecC, "sinkrow": sinkrow,
        "lamq": np.ascontiguousarray(diff_lam_q[:L].reshape(L, 64), dtype=np.float32),
        "lamk": np.ascontiguousarray(diff_lam_k[:L].reshape(L, 64), dtype=np.float32),
        "subg": np.ascontiguousarray(diff_subln_g[:L], dtype=np.float32),
    }


def _all_inputs(x, L, args, tables):
    B = x.shape[0]
    base = [_make_inputs(x[b], L, *args, tables) for b in range(B)]
    in_maps = []
    for p in range(2):
        w_p, band_p, cvec_p = _perm_c_heads(p, base[0]["w_in"], tables[2], tables[3])
        for b in range(B):
            m = dict(base[b])
            m["w_in"], m["bandC"], m["cvecC"] = w_p, band_p, cvec_p
            in_maps.append(m)
    return in_maps


def kernel(x, ln_in_g, ln_in_b, t5_table, w_in, w_out, na_rpb, sw_sink, diff_lam_q, diff_lam_k,
           diff_subln_g, ln_mix_g, ln_mix_b, w_ff1, w_ff2, ln_ff_g, ln_ff_b):
    args = [np.asarray(a) for a in (ln_in_g, ln_in_b, t5_table, w_in, w_out, na_rpb, sw_sink, diff_lam_q,
                                    diff_lam_k, diff_subln_g, ln_mix_g, ln_mix_b, w_ff1, w_ff2, ln_ff_g, ln_ff_b)]
    x = np.asarray(x)
    B, S, _ = x.shape
    L = args[3].shape[0]
    tables = _host_tables(S, L, args[2], args[5], args[6])
    key = (S, L, B)
    if key not in _NC_CACHE:
        _NC_CACHE[key] = build_program(S, L, B)
    nc = _NC_CACHE[key]
    in_maps = _all_inputs(x, L, args, tables)
    res = run_bass_kernel_spmd(nc, in_maps, core_ids=list(range(2 * B)))
    return np.stack([np.asarray(res.results[b]["out"], dtype=np.float32) for b in range(B)], axis=0)
```
